# Optimizing a Trainium2 kernel written in Bass

```python
import jax, jax.numpy as jnp
from jax import lax
import numpy as np

D_MODEL = 2048
BATCH = 4
SEQ = 2048
DEPTH = 2
DEC_BATCH = 8
DEC_SEQ = 8
PAST_LEN = 16384
PAGE_SIZE = 128

N_META = 16
EPS = 1e-6
H_A = 4
DK_A = 128
DV_A = 256
LOWRANK_A = 16
GATE_TAU = 16.0
GLA_CHUNK = 64
C_B = 1024
CONV_W = 31
H_C = 16
DH_C = D_MODEL // H_C
SB_BLOCK = 128
SB_BIAS_INIT = -6.5
PEER_HEADS = 8
PEER_NKEYS = 128
PEER_N = PEER_NKEYS * PEER_NKEYS
PEER_DQ = 256
PEER_TOPK = 16
PEER_BLOCK = 64
PEER_V_SCALE = 0.05

N_EVEN = (DEPTH + 1) // 2
N_ODD = DEPTH // 2
QA = H_A * DK_A
VA = H_A * DV_A
PROJ_EVEN = 2 * QA + 2 * VA + LOWRANK_A + 2 * C_B
MIX_EVEN = VA + C_B

kernel_name = "hybrid_gla_conformer_stickbreak_peer_step"


def rmsnorm(x, g):
    xf = x.astype(jnp.float32)
    y = xf * lax.rsqrt(jnp.mean(xf * xf, axis=-1, keepdims=True) + EPS)
    return (y * g.astype(jnp.float32)).astype(x.dtype)


def layernorm(x, g, b):
    xf = x.astype(jnp.float32)
    mu = jnp.mean(xf, axis=-1, keepdims=True)
    var = jnp.mean(jnp.square(xf - mu), axis=-1, keepdims=True)
    return ((xf - mu) * lax.rsqrt(var + EPS) * g.astype(jnp.float32) + b.astype(jnp.float32)).astype(x.dtype)


def gla_chunked(q, k, v, loga, s0, chunk):
    B, L, H, _ = q.shape
    nc = L // chunk
    to_chunks = lambda t: jnp.moveaxis(t.reshape(B, nc, chunk, H, t.shape[-1]), 1, 0)
    causal = jnp.tril(jnp.ones((chunk, chunk), bool))[None, :, :, None, None]

    def step(S, inp):
        qi, ki, vi, ai = inp
        b = jnp.cumsum(ai, axis=1)
        diff = b[:, :, None] - b[:, None, :]
        decay = jnp.exp(jnp.where(causal, diff, -jnp.inf))
        att = jnp.einsum('bthd,bshd,btshd->bhts', qi, ki, decay)
        o = (jnp.einsum('bhts,bshv->bthv', att, vi)
             + jnp.einsum('bthd,bhdv->bthv', qi * jnp.exp(b), S))
        b_last = b[:, -1]
        k_dec = ki * jnp.exp(b_last[:, None] - b)
        S = jnp.exp(b_last)[..., None] * S + jnp.einsum('bshd,bshv->bhdv', k_dec, vi)
        return S, o

    S, o = lax.scan(step, s0, (to_chunks(q), to_chunks(k), to_chunks(v), to_chunks(loga)))
    return jnp.moveaxis(o, 0, 1).reshape(B, L, H, v.shape[-1]), S


def causal_depthwise_conv(u, buf, w, b):
    ucat = jnp.concatenate([buf.astype(u.dtype), u], axis=1)
    y = lax.conv_general_dilated(ucat, w[:, None, :].astype(u.dtype), (1,), 'VALID',
                                 dimension_numbers=('NWC', 'WIO', 'NWC'),
                                 feature_group_count=u.shape[-1])
    return y + b.astype(u.dtype), ucat[:, -(CONV_W - 1):]


def even_mixer(h, s0, buf, segments, w_in, w_lr, b_lr, gla_g, cw, cb, cng, cnb, w_out):
    f32 = jnp.float32
    B, L, _ = h.shape
    p = h @ w_in
    cuts = [int(c) for c in np.cumsum([QA, QA, VA, VA, LOWRANK_A, C_B])]
    q, k, v, g, lr, ga, gb = jnp.split(p, cuts, axis=-1)
    loga = jax.nn.log_sigmoid((lr @ w_lr + b_lr).astype(f32)) / GATE_TAU
    heads = lambda t, d: t.astype(f32).reshape(B, L, H_A, d)
    q = heads(q, DK_A) * (DK_A ** -0.5)
    k = heads(k, DK_A)
    v = heads(v, DV_A)
    loga = loga.reshape(B, L, H_A, DK_A)
    S = s0.astype(f32)
    outs = []
    off = 0
    for seg_len, chunk in segments:
        sl = slice(off, off + seg_len)
        o, S = gla_chunked(q[:, sl], k[:, sl], v[:, sl], loga[:, sl], S, chunk)
        outs.append(o)
        off += seg_len
    o = jnp.concatenate(outs, axis=1)
    o_a = (rmsnorm(o, gla_g) * jax.nn.silu(heads(g, DV_A))).reshape(B, L, VA)
    u = ga * jax.nn.sigmoid(gb)
    c, new_buf = causal_depthwise_conv(u, buf, cw, cb)
    o_b = jax.nn.silu(layernorm(c, cng, cnb).astype(f32))
    y = jnp.concatenate([o_a.astype(h.dtype), o_b.astype(h.dtype)], axis=-1) @ w_out
    return y, S, new_buf


def stick_breaking_weights(z, mask):
    sp = jnp.where(mask, jax.nn.softplus(z), 0.0)
    rc = lax.cumsum(sp, axis=z.ndim - 1, reverse=True)
    after = jnp.concatenate([rc[..., 1:], jnp.zeros_like(rc[..., :1])], axis=-1)
    return jnp.where(mask, jnp.exp(jax.nn.log_sigmoid(z) - after), 0.0)


def qkv_heads(h, w_qkv):
    B, L, _ = h.shape
    q, k, v = jnp.split(h @ w_qkv, 3, axis=-1)
    r = lambda t: t.reshape(B, L, H_C, DH_C)
    return r(q) * (DH_C ** -0.5), r(k), r(v)


def sb_block(qb, qpos, kb, vb, kpos, bias):
    z = jnp.einsum('bqhd,bkhd->bhqk', qb, kb).astype(jnp.float32) + bias.astype(jnp.float32)[None, :, None, None]
    A = stick_breaking_weights(z, kpos[None, :] < qpos[:, None])
    return jnp.einsum('bhqk,bkhd->bqhd', A, vb.astype(jnp.float32))


def sb_prompt(h, w_qkv, w_o, bias):
    B, L, _ = h.shape
    q, k, v = qkv_heads(h, w_qkv)
    pos = jnp.arange(L)
    o_meta = sb_block(q[:, :N_META], pos[:N_META], k[:, :N_META], v[:, :N_META], pos[:N_META], bias)
    nb = (L - N_META) // SB_BLOCK
    qr = jnp.swapaxes(q[:, N_META:].reshape(B, nb, SB_BLOCK, H_C, DH_C), 0, 1)
    pr = pos[N_META:].reshape(nb, SB_BLOCK)
    o_real = lax.map(lambda a: sb_block(a[0], a[1], k, v, pos, bias), (qr, pr))
    o_real = jnp.swapaxes(o_real, 0, 1).reshape(B, L - N_META, H_C, DH_C)
    o = jnp.concatenate([o_meta, o_real], axis=1).reshape(B, L, H_C * DH_C)
    return o.astype(h.dtype) @ w_o, k, v


def sb_sample(h, cache_k, cache_v, page_table, w_qkv, w_o, bias):
    B, T, _ = h.shape
    q, k, v = qkv_heads(h, w_qkv)
    n_pages = page_table.shape[1]
    past = n_pages * PAGE_SIZE
    k_past = cache_k[page_table].reshape(B, past, H_C, DH_C)
    v_past = cache_v[page_table].reshape(B, past, H_C, DH_C)
    z = jnp.concatenate([jnp.einsum('bqhd,bkhd->bhqk', q, k_past.astype(q.dtype)),
                         jnp.einsum('bqhd,bkhd->bhqk', q, k)], axis=-1).astype(jnp.float32)
    z = z + bias.astype(jnp.float32)[None, :, None, None]
    qpos = past + jnp.arange(T)
    kpos = jnp.arange(past + T)
    A = stick_breaking_weights(z, kpos[None, :] < qpos[:, None])
    o = (jnp.einsum('bhqk,bkhd->bqhd', A[..., :past], v_past.astype(jnp.float32))
         + jnp.einsum('bhqk,bkhd->bqhd', A[..., past:], v.astype(jnp.float32)))
    return o.reshape(B, T, H_C * DH_C).astype(h.dtype) @ w_o, k, v


def peer_ffn(h, w_q, sub_keys, u_tab, v_tab):
    f32 = jnp.float32
    B, L, D = h.shape
    T = B * L
    x = h.reshape(T, D)
    q = (x @ w_q).astype(f32).reshape(T, PEER_HEADS, 2, PEER_DQ // 2)
    s = jnp.einsum('thcd,hcnd->thcn', q, sub_keys.astype(f32))
    sv, si = lax.top_k(s, PEER_TOPK)
    cand = (sv[:, :, 0, :, None] + sv[:, :, 1, None, :]).reshape(T, PEER_HEADS, -1)
    cidx = (si[:, :, 0, :, None] * PEER_NKEYS + si[:, :, 1, None, :]).reshape(T, PEER_HEADS, -1)
    best, pos = lax.top_k(cand, PEER_TOPK)
    eidx = jnp.take_along_axis(cidx, pos, axis=-1)
    gate = jax.nn.softmax(best, axis=-1)
    nblk = -(-T // PEER_BLOCK)
    pad = nblk * PEER_BLOCK - T
    xb = jnp.pad(x, ((0, pad), (0, 0))).reshape(nblk, PEER_BLOCK, D)
    eb = jnp.pad(eidx, ((0, pad), (0, 0), (0, 0))).reshape(nblk, PEER_BLOCK, PEER_HEADS, PEER_TOPK)
    gb = jnp.pad(gate, ((0, pad), (0, 0), (0, 0))).reshape(nblk, PEER_BLOCK, PEER_HEADS, PEER_TOPK)

    def expert_block(args):
        xi, ei, gi = args
        act = jax.nn.gelu(jnp.einsum('td,thkd->thk', xi, u_tab[ei]).astype(f32))
        return jnp.einsum('thk,thkd->td', (gi * act).astype(x.dtype), v_tab[ei])

    y = lax.map(expert_block, (xb, eb, gb)).reshape(nblk * PEER_BLOCK, D)[:T]
    return y.reshape(B, L, D)


def setup_inputs(seed: int = 0) -> dict:
    key = jax.random.key(seed)
    ks = jax.random.split(key, 32)
    f32 = jnp.float32
    n_pages = PAST_LEN // PAGE_SIZE
    n_pool = (5 * DEC_BATCH * n_pages) // 4
    nrm = lambda k, shape, s: jax.random.normal(k, shape, f32) * s
    gain = lambda k, shape: 1.0 + 0.01 * jax.random.normal(k, shape, f32)
    page_table = jax.random.permutation(ks[6], n_pool)[:DEC_BATCH * n_pages]
    page_table = page_table.reshape(DEC_BATCH, n_pages).astype(jnp.int32)
    return {
        "x_prompt": nrm(ks[0], (BATCH, SEQ, D_MODEL), 1.0),
        "x_sample": nrm(ks[1], (DEC_BATCH, DEC_SEQ, D_MODEL), 1.0),
        "state_gla": nrm(ks[2], (N_EVEN, DEC_BATCH, H_A, DK_A, DV_A), 1.0),
        "state_conv": nrm(ks[3], (N_EVEN, DEC_BATCH, CONV_W - 1, C_B), 0.5),
        "cache_k": nrm(ks[4], (N_ODD, n_pool, PAGE_SIZE, H_C, DH_C), 1.0),
        "cache_v": nrm(ks[5], (N_ODD, n_pool, PAGE_SIZE, H_C, DH_C), 1.0),
        "page_table": page_table,
        "meta_tokens": nrm(ks[7], (N_META, D_MODEL), 1.0),
        "norm_mix": gain(ks[8], (DEPTH, D_MODEL)),
        "norm_ffn": gain(ks[9], (DEPTH, D_MODEL)),
        "norm_final": gain(ks[10], (D_MODEL,)),
        "w_in_even": nrm(ks[11], (N_EVEN, D_MODEL, PROJ_EVEN), D_MODEL ** -0.5),
        "w_gate_lr": nrm(ks[12], (N_EVEN, LOWRANK_A, QA), LOWRANK_A ** -0.5),
        "b_gate_lr": nrm(ks[13], (N_EVEN, QA), 0.01),
        "gla_norm": gain(ks[14], (N_EVEN, DV_A)),
        "conv_w": nrm(ks[15], (N_EVEN, CONV_W, C_B), CONV_W ** -0.5),
        "conv_b": nrm(ks[16], (N_EVEN, C_B), 0.01),
        "conv_norm_g": gain(ks[17], (N_EVEN, C_B)),
        "conv_norm_b": nrm(ks[18], (N_EVEN, C_B), 0.01),
        "w_out_even": nrm(ks[19], (N_EVEN, MIX_EVEN, D_MODEL), MIX_EVEN ** -0.5),
        "w_qkv_odd": nrm(ks[20], (N_ODD, D_MODEL, 3 * H_C * DH_C), D_MODEL ** -0.5),
        "w_out_odd": nrm(ks[21], (N_ODD, H_C * DH_C, D_MODEL), (H_C * DH_C) ** -0.5),
        "sb_bias": SB_BIAS_INIT + nrm(ks[26], (N_ODD, H_C), 0.3),
        "peer_wq": nrm(ks[22], (DEPTH, D_MODEL, PEER_HEADS * PEER_DQ), D_MODEL ** -0.5),
        "peer_keys": nrm(ks[23], (DEPTH, PEER_HEADS, 2, PEER_NKEYS, PEER_DQ // 2), (PEER_DQ // 2) ** -0.5),
        "peer_u": nrm(ks[24], (DEPTH, PEER_N, D_MODEL), D_MODEL ** -0.5),
        "peer_v": nrm(ks[25], (DEPTH, PEER_N, D_MODEL), PEER_V_SCALE),
    }


def reference(x_prompt, x_sample, state_gla, state_conv, cache_k, cache_v, page_table,
              meta_tokens, norm_mix, norm_ffn, norm_final, w_in_even, w_gate_lr, b_gate_lr,
              gla_norm, conv_w, conv_b, conv_norm_g, conv_norm_b, w_out_even, w_qkv_odd,
              w_out_odd, sb_bias, peer_wq, peer_keys, peer_u, peer_v):
    B = x_prompt.shape[0]
    xp = jnp.concatenate([jnp.broadcast_to(meta_tokens.astype(x_prompt.dtype)[None], (B, N_META, D_MODEL)),
                          x_prompt], axis=1)
    xs = x_sample
    Lp = xp.shape[1]
    Ts = xs.shape[1]
    gla_p, gla_s, conv_p, conv_s, k_p, v_p, k_s, v_s = [], [], [], [], [], [], [], []
    for layer in range(DEPTH):
        i = layer // 2
        hp = rmsnorm(xp, norm_mix[layer])
        hs = rmsnorm(xs, norm_mix[layer])
        if layer % 2 == 0:
            ew = (w_in_even[i], w_gate_lr[i], b_gate_lr[i], gla_norm[i], conv_w[i], conv_b[i],
                  conv_norm_g[i], conv_norm_b[i], w_out_even[i])
            s0 = jnp.zeros((B, H_A, DK_A, DV_A), jnp.float32)
            buf0 = jnp.zeros((B, CONV_W - 1, C_B), xp.dtype)
            yp, sp, cp = even_mixer(hp, s0, buf0, ((N_META, N_META), (Lp - N_META, GLA_CHUNK)), *ew)
            ys, ss, cs = even_mixer(hs, state_gla[i], state_conv[i], ((Ts, Ts),), *ew)
            gla_p.append(sp)
            gla_s.append(ss)
            conv_p.append(cp)
            conv_s.append(cs)
        else:
            yp, kp_, vp_ = sb_prompt(hp, w_qkv_odd[i], w_out_odd[i], sb_bias[i])
            ys, ks_, vs_ = sb_sample(hs, cache_k[i], cache_v[i], page_table, w_qkv_odd[i], w_out_odd[i], sb_bias[i])
            k_p.append(kp_)
            v_p.append(vp_)
            k_s.append(ks_)
            v_s.append(vs_)
        xp = xp + yp
        xs = xs + ys
        pw = (peer_wq[layer], peer_keys[layer], peer_u[layer], peer_v[layer])
        xp = xp + peer_ffn(rmsnorm(xp, norm_ffn[layer]), *pw)
        xs = xs + peer_ffn(rmsnorm(xs, norm_ffn[layer]), *pw)
    y_prompt = rmsnorm(xp, norm_final)[:, N_META:]
    y_sample = rmsnorm(xs, norm_final)
    return (y_prompt, y_sample, jnp.stack(gla_p), jnp.stack(gla_s), jnp.stack(conv_p), jnp.stack(conv_s),
            jnp.stack(k_p), jnp.stack(v_p), jnp.stack(k_s), jnp.stack(v_s))
```

```python
import contextlib
import numpy as np
import concourse.bass as bass
import concourse.mybir as mybir
from concourse.bass_utils import run_bass_kernel_spmd

F32 = mybir.dt.float32
BF16 = mybir.dt.bfloat16
I32 = mybir.dt.int32
U32 = mybir.dt.uint32
ALU = mybir.AluOpType
AF = mybir.ActivationFunctionType
AX = mybir.AxisListType

D = 2048
KC = 16
N_META = 16
EPS = 1e-6
NDMASEM = 8
import os as _os
GLA_STAGE = int(_os.environ.get('GLA_STAGE', 9))
GLA_SUB = int(_os.environ.get('GLA_SUB', 9))


class Buf:
    __slots__ = ("name", "w", "r", "x", "multi", "mw")

    def __init__(self, name, init=None, x=False, multi=False):
        self.name = name
        self.w = None
        self.r = dict(init) if init else {}
        self.multi = multi
        self.mw = {}
        self.x = x


class Prog:
    ENGS = ("pe", "act", "dve", "pool", "sp")

    def __init__(self, nc):
        self.nc = nc
        self.ops = {e: [] for e in self.ENGS}
        self.cnt = {e: 0 for e in self.ENGS}
        self.dma_n = {e: 0 for e in self.ENGS}
        self.dma_last = {}
        self.waited = {e: {} for e in self.ENGS}
        self.barrier = {}

    def _deps(self, eng, reads, writes):
        deps = {}

        def add(k, v):
            if deps.get(k, 0) < v:
                deps[k] = v
        for b in reads:
            if b.multi:
                for k, v in b.mw.items():
                    add(k, v)
            elif b.w is not None:
                add(*b.w)
        for b in writes:
            if not b.multi and b.w is not None:
                add(*b.w)
            for k, v in b.r.items():
                add(k, v)
        out = []
        wd = self.waited[eng]
        for k, v in deps.items():
            if eng == "pe" and k == ("e", "pe"):
                continue
            if wd.get(k, 0) >= v:
                continue
            wd[k] = v
            out.append((k, v))
        return out

    def _commit(self, ev, reads, writes):
        k, v = ev
        for b in reads:
            if b.r.get(k, 0) < v:
                b.r[k] = v
        for b in writes:
            if b.multi:
                if b.mw.get(k, 0) < v:
                    b.mw[k] = v
            else:
                b.w = ev
                b.r = {}

    def op(self, eng, fn, reads=(), writes=()):
        xr = [b for b in reads if b.x]
        if xr:
            reads = [b for b in reads if not b.x]
            writes = list(writes) + [b for b in xr if b not in writes]
        waits = self._deps(eng, reads, writes)
        self.cnt[eng] += 1
        ev = (("e", eng), self.cnt[eng])
        self.ops[eng].append((waits, fn, ev, 1))
        self._commit(ev, reads, writes)

    def dma(self, eng, fn, reads=(), writes=()):
        n = self.dma_n[eng]
        k = ("d", eng, n % NDMASEM)
        waits = self._deps(eng, reads, writes)
        prev = self.dma_last.get(k)
        if prev is not None and self.waited[eng].get(k, 0) < prev:
            self.waited[eng][k] = prev
            waits.append((k, prev))
        val = (prev or 0) + 16
        self.dma_last[k] = val
        self.dma_n[eng] = n + 1
        ev = (k, val)
        self.ops[eng].append((waits, fn, ev, 16))
        self._commit(ev, reads, writes)

    def release(self, bufs):
        for b in bufs:
            evs = list(b.r.items())
            if b.w is not None:
                evs.append(b.w)
            for k, v in evs:
                if self.barrier.get(k, 0) < v:
                    self.barrier[k] = v

    def emit(self):
        nc = self.nc
        final_events = list(self.dma_last.items())
        with contextlib.ExitStack() as st:
            sems = {}
            for e in self.ENGS:
                sems[("e", e)] = st.enter_context(nc.semaphore("s_" + e))
                for s in range(NDMASEM):
                    sems[("d", e, s)] = st.enter_context(nc.semaphore("d_%s_%d" % (e, s)))
            block = st.enter_context(nc.Block())
            engmap = {"pe": "tensor", "act": "scalar", "dve": "vector", "pool": "gpsimd", "sp": "sync"}

            def make(ename):
                oplist = self.ops[ename]

                def body(eng):
                    for waits, fn, ev, inc in oplist:
                        for k, v in waits:
                            eng.wait_ge(sems[k], v)
                        fn(eng).then_inc(sems[ev[0]], inc)
                    if ename == "sp":
                        for k, v in final_events:
                            eng.wait_ge(sems[k], v)
                return body
            for ename in self.ENGS:
                getattr(block, engmap[ename])(make(ename))


class Phase:
    def __init__(self, P, name):
        self.P = P
        self.nc = P.nc
        self.name = name
        self.st = contextlib.ExitStack()
        self.bufs = []
        self.k = 0

    def __enter__(self):
        self.st.__enter__()
        return self

    def __exit__(self, *a):
        self.P.release(self.bufs)
        return self.st.__exit__(*a)

    def buf(self, name="b", x=False):
        b = Buf(name, self.P.barrier, x)
        self.bufs.append(b)
        return b

    def sb(self, name, shape, dt=F32, nb=None):
        t = self.st.enter_context(self.nc.sbuf_tensor("%s_%s" % (self.name, name), list(shape), dt))
        if nb is None:
            return t, self.buf(name)
        return t, [self.buf(name + str(i)) for i in range(nb)]

    def ps(self, name, shape, dt=F32):
        t = self.st.enter_context(self.nc.psum_tensor("%s_%s" % (self.name, name), list(shape), dt))
        return t, self.buf(name, x=True)


def tiles_of(NP, NS):
    tl = []
    r = 0
    while r + 128 <= NP:
        tl.append((r, 128))
        r += 128
    tl.append((r, NP - r + NS))
    return tl


def blocks_of(NP, NS, bs=512):
    bl = []
    r = 0
    while r < NP:
        n = min(bs, NP - r)
        bl.append((r, n))
        r += n
    bl.append((NP, NS))
    return bl


def build(SEQ=2048, NPAGES=128, NPOOL=1280, debug=False, stop_after=None, NEXP=16384):
    nc = bass.Bass("TRN2", target_bir_lowering=False)
    NP = N_META + SEQ
    NS = 8
    NT = NP + NS
    TILES = tiles_of(NP, NS)
    BLOCKS = blocks_of(NP, NS)
    NTILE = len(TILES)

    def din(name, shape, dt=F32):
        return nc.dram_tensor(name, list(shape), dt, kind="ExternalInput").ap()

    def dout(name, shape, dt=F32):
        return nc.dram_tensor(name, list(shape), dt, kind="ExternalOutput").ap()

    def dscr(name, shape, dt=F32):
        if debug:
            return nc.dram_tensor(name, list(shape), dt, kind="ExternalOutput").ap()
        return nc.dram_tensor(name, list(shape), dt).ap()

    xin = din("xin", [NT, D])
    sgla = din("sgla", [4, 128, 256])
    sconv = din("sconv", [30, 1024])
    cache_k = din("cache_k", [NPOOL * 128, D])
    cache_v = din("cache_v", [NPOOL * 128, D])
    ptab = din("ptab", [1, NPAGES], I32)
    norm_mix = din("norm_mix", [2, D])
    norm_ffn = din("norm_ffn", [2, D])
    norm_final = din("norm_final", [1, D])
    w_in = din("w_in", [D, 5136])
    w_lr = din("w_lr", [16, 512])
    b_lr = din("b_lr", [1, 512])
    gla_norm = din("gla_norm", [1, 256])
    conv_w = din("conv_w", [31, 1024])
    conv_vec = din("conv_vec", [24, 128])
    w_out_e = din("w_out_e", [D, D])
    w_qkv = din("w_qkv", [D, 3 * D])
    w_out_o = din("w_out_o", [D, D])
    sb_bias = din("sb_bias", [1, 16])
    peer_wq = din("peer_wq", [2, D, D])
    peer_keys = din("peer_keys", [2, 16, 128, 128])
    peer_u = din("peer_u", [2, NEXP, D])
    peer_v = din("peer_v", [2, NEXP, D])
    y_out = dout("y_out", [NT, D])
    gla_p = dout("gla_p", [4, 128, 256])
    gla_s = dout("gla_s", [4, 128, 256])
    conv_p = dout("conv_p", [30, 1024])
    conv_s = dout("conv_s", [30, 1024])
    k_all = dout("k_all", [NT, D])
    v_all = dout("v_all", [NT, D])
    PT = dscr("PT", [NT, 3088])
    UT = dscr("UT", [1024, 30 + NP])
    UTS = dscr("UTS", [1024, 30 + NS])
    OA = dscr("OA", [NT, 1024])
    X1 = dscr("X1", [NT, D])
    X2 = dscr("X2", [NT, D])
    X3 = dscr("X3", [NT, D])
    X4 = dscr("X4", [NT, D])
    HF = dscr("HF", [NT, D])
    QP = dscr("QP", [NT, D])
    QT = dscr("QT", [D, NT])
    KT = dscr("KT", [D, NT])

    global LASTP
    UVB = [nc.dram_tensor("UB16", [2 * NEXP, D], BF16).ap(), nc.dram_tensor("VB16", [2 * NEXP, D], BF16).ap()]

    P = Prog(nc)
    LASTP = P
    DB = {}

    def db(name):
        if name not in DB:
            DB[name] = Buf(name, multi=True)
        return DB[name]

    G = Phase(P, "g")
    G.__enter__()
    ident_f, b_ident_f = G.sb("ident_f", [128, 128])
    ident_b, b_ident_b = G.sb("ident_b", [128, 128], BF16)
    ones_f, b_ones_f = G.sb("ones_f", [128, 128])
    b_const = [b_ident_f, b_ident_b, b_ones_f]
    P.op("pool", lambda e: e.memset(ident_f[:], 0.0), writes=[b_ident_f])
    P.op("pool", lambda e: e.affine_select(out=ident_f[:], in_=ident_f[:], pattern=[[-1, 128]], compare_op=ALU.not_equal,
                                           fill=1.0, base=0, channel_multiplier=1), reads=[b_ident_f], writes=[b_ident_f])
    P.op("pool", lambda e: e.tensor_copy(out=ident_b[:], in_=ident_f[:]), reads=[b_ident_f], writes=[b_ident_b])
    P.op("pool", lambda e: e.memset(ones_f[:], 1.0), writes=[b_ones_f])
    eps_t, b_eps = G.sb("eps_t", [128, 1])
    P.op("pool", lambda e: e.memset(eps_t[:], EPS), writes=[b_eps])

    bgf = [G.sb("bgf%d" % i, [128, D]) for i in range(2)]
    bgb = [G.sb("bgb%d" % i, [128, D], BF16) for i in range(2)]
    NBG = NEXP // 128

    def bg_gen():
        i = 0
        for L in range(2):
            for tab, src in ((0, peer_u), (1, peer_v)):
                for t in range(NBG):
                    f_t, bf = bgf[i % 2]
                    b_t, bb = bgb[i % 2]
                    i += 1
                    P.dma("act", lambda e, L=L, t=t, src=src, f_t=f_t: e.dma_start(out=f_t[:], in_=src[L, t * 128:(t + 1) * 128, :]), writes=[bf])
                    if i % 2:
                        P.op("dve", lambda e, f_t=f_t, b_t=b_t: e.tensor_copy(out=b_t[:], in_=f_t[:]), reads=[bf], writes=[bb])
                    else:
                        P.op("act", lambda e, f_t=f_t, b_t=b_t: e.activation(out=b_t[:], in_=f_t[:], func=AF.Copy), reads=[bf], writes=[bb])
                    P.dma("act", lambda e, L=L, t=t, tab=tab, b_t=b_t: e.dma_start(out=UVB[tab][L * NEXP + t * 128:L * NEXP + (t + 1) * 128, :], in_=b_t[:]),
                          reads=[bb], writes=[db("UVB%d" % L)])
                    yield L
    bg_state = {"it": bg_gen(), "done": [0, 0]}

    def bg_step(k=1):
        for _ in range(k):
            try:
                L = next(bg_state["it"])
                bg_state["done"][L] += 1
            except StopIteration:
                return

    def bg_flush(L):
        while bg_state["done"][L] < 2 * NBG:
            bg_step()

    def gain_tile(ph, name, src_row):
        t, b = ph.sb(name, [128, D])
        P.dma("sp", lambda e: e.dma_start(out=t[:], in_=src_row.to_broadcast([128, D])), writes=[b])
        return t, b

    def rmsnorm_tile(ph, xt, bx, n, gain, bgain, out_t, bout, scr):
        junk, bjunk, ss, bss = scr
        P.op("dve", lambda e: e.memset(ss[:n], 0.0), writes=[bss])
        P.op("act", lambda e: e.activation(out=junk[:n], in_=xt[:n], func=AF.Square, accum_out=ss[:n, 0:1]),
             reads=[bx, bss], writes=[bjunk, bss])
        P.op("act", lambda e: e.activation(out=ss[:n, 1:2], in_=ss[:n, 0:1], func=AF.Sqrt, bias=eps_t[:n, 0:1], scale=1.0 / D),
             reads=[bss, b_eps], writes=[bss])
        P.op("dve", lambda e: e.reciprocal(out=ss[:n, 2:3], in_=ss[:n, 1:2]), reads=[bss], writes=[bss])
        P.op("dve", lambda e: e.scalar_tensor_tensor(out=out_t[:n], in0=xt[:n], scalar=ss[:n, 2:3], in1=gain[:n],
                                                     op0=ALU.mult, op1=ALU.mult), reads=[bx, bss, bgain], writes=[bout])

    def transpose_to_fm(src, bsrc, n, nk, dst, bdst, k0, c0, pst, bpst, dt_is_bf=True, evac="act"):
        idt = ident_b if dt_is_bf else ident_f
        bid = b_ident_b if dt_is_bf else b_ident_f
        for g0 in range(0, nk, 4):
            gn = min(4, nk - g0)
            for j in range(gn):
                k = g0 + j
                P.op("pe", lambda e, k=k, j=j: e.transpose(out=pst[:, j, :n], in_=src[:n, k * 128:(k + 1) * 128], identity=idt[:n, :n]),
                     reads=[bsrc, bid], writes=[bpst])
            if evac == "act":
                P.op("act", lambda e, g0=g0, gn=gn: e.activation(out=dst[:, k0 + g0:k0 + g0 + gn, c0:c0 + n], in_=pst[:, 0:gn, :n], func=AF.Copy),
                     reads=[bpst], writes=[bdst])
            else:
                P.op("dve", lambda e, g0=g0, gn=gn: e.tensor_copy(out=dst[:, k0 + g0:k0 + g0 + gn, c0:c0 + n], in_=pst[:, 0:gn, :n]),
                     reads=[bpst], writes=[bdst])

    def norm_phase(name, xsrc, bxsrc, gain_row, AT, bAT, hf_dst=None, bhf=None):
        with Phase(P, name) as ph:
            gain, bgain = gain_tile(ph, "gain", gain_row)
            xt = [ph.sb("x%d" % i, [128, D]) for i in range(2)]
            hb = [ph.sb("hb%d" % i, [128, D], BF16) for i in range(2)]
            hf = [ph.sb("hf%d" % i, [128, D]) for i in range(2)] if hf_dst is not None else None
            junk, bjunk = ph.sb("junk", [128, D])
            ss, bss = ph.sb("ss", [128, 4])
            pst = [ph.ps("pst%d" % i, [128, 4, 128], BF16) for i in range(2)]
            for ti, (r0, n) in enumerate(TILES):
                x_t, bx = xt[ti % 2]
                h_t, bh = hb[ti % 2]
                P.dma("sp", lambda e, r0=r0, n=n, x_t=x_t: e.dma_start(out=x_t[:n], in_=xsrc[r0:r0 + n, :]), reads=[bxsrc], writes=[bx])
                if hf_dst is not None:
                    f_t, bf = hf[ti % 2]
                    rmsnorm_tile(ph, x_t, bx, n, gain, bgain, f_t, bf, (junk, bjunk, ss, bss))
                    P.dma("sp", lambda e, r0=r0, n=n, f_t=f_t: e.dma_start(out=hf_dst[r0:r0 + n, :], in_=f_t[:n]), reads=[bf], writes=[bhf])
                    P.op("pool", lambda e, n=n, f_t=f_t, h_t=h_t: e.tensor_copy(out=h_t[:n], in_=f_t[:n]), reads=[bf], writes=[bh])
                else:
                    rmsnorm_tile(ph, x_t, bx, n, gain, bgain, h_t, bh, (junk, bjunk, ss, bss))
                p_t, bp = pst[ti % 2]
                transpose_to_fm(h_t, bh, n, KC, AT, bAT, 0, r0, p_t, bp, evac=("act" if ti % 2 else "dve"))

    def proj_tok(ph, AT, bAT, nk, W, col0, ncols, sink, CB=256, wscale=None):
        wf = [ph.sb("wf%d" % i, [128, nk, CB]) for i in range(2)]
        wb = [ph.sb("wb%d" % i, [128, nk, CB], BF16) for i in range(2)]
        pp = [ph.ps("pp%d" % i, [128, 512]) for i in range(4)]
        cnt = 0
        for bi, c0 in enumerate(range(col0, col0 + ncols, CB)):
            cn = min(CB, col0 + ncols - c0)
            wf_t, bwf = wf[bi % 2]
            wb_t, bwb = wb[bi % 2]
            P.dma("sp", lambda e, c0=c0, cn=cn, wf_t=wf_t: e.dma_start(out=wf_t[:, :, :cn], in_=W[:, c0:c0 + cn].rearrange("(k p) c -> p k c", p=128)),
                  writes=[bwf])
            hk = nk // 2
            P.op("dve", lambda e, cn=cn, wf_t=wf_t, wb_t=wb_t: e.tensor_copy(out=wb_t[:, :hk, :cn], in_=wf_t[:, :hk, :cn]), reads=[bwf], writes=[bwb])
            P.op("act", lambda e, cn=cn, wf_t=wf_t, wb_t=wb_t: e.activation(out=wb_t[:, hk:, :cn], in_=wf_t[:, hk:, :cn], func=AF.Copy), reads=[bwf], writes=[bwb])
            for ti, (r0, n) in enumerate(TILES):
                p_t, bp = pp[cnt % 4]
                cnt += 1
                for k in range(nk):
                    P.op("pe", lambda e, k=k, r0=r0, n=n, cn=cn, p_t=p_t, wb_t=wb_t: e.matmul(p_t[:n, :cn], lhsT=AT[:, k, r0:r0 + n], rhs=wb_t[:, k, :cn],
                                                                                         start=(k == 0), stop=(k == nk - 1)),
                         reads=[bAT, bwb], writes=[bp])
                sink(ti, r0, n, c0 - col0, cn, p_t, bp)
                bg_step()

    def proj_ch(ph, AT, bAT, nk, W, cols, sink, tag=""):
        ng = len(cols[0])
        wf = [ph.sb("cwf%s%d" % (tag, i), [128, nk, ng * 128]) for i in range(2)]
        wb = [ph.sb("cwb%s%d" % (tag, i), [128, nk, ng * 128], BF16) for i in range(2)]
        pp = [ph.ps("cpp%s%d" % (tag, i), [128, 512]) for i in range(2 * ng)]
        cnt = 0
        for gi, grp in enumerate(cols):
            wf_t, bwf = wf[gi % 2]
            wb_t, bwb = wb[gi % 2]
            for j, c0 in enumerate(grp):
                P.dma("sp", lambda e, c0=c0, j=j, wf_t=wf_t: e.dma_start(out=wf_t[:, :, j * 128:(j + 1) * 128],
                                                                       in_=W[:, c0:c0 + 128].rearrange("(k p) c -> p k c", p=128)), writes=[bwf])
            hk = nk // 2
            P.op("dve", lambda e, wf_t=wf_t, wb_t=wb_t: e.tensor_copy(out=wb_t[:, :hk, :], in_=wf_t[:, :hk, :]), reads=[bwf], writes=[bwb])
            P.op("act", lambda e, wf_t=wf_t, wb_t=wb_t: e.activation(out=wb_t[:, hk:, :], in_=wf_t[:, hk:, :], func=AF.Copy), reads=[bwf], writes=[bwb])
            for bi, (t0, nt) in enumerate(BLOCKS):
                pts = []
                for j in range(ng):
                    p_t, bp = pp[(cnt % 2) * ng + j]
                    for k in range(nk):
                        P.op("pe", lambda e, k=k, j=j, t0=t0, nt=nt, p_t=p_t, wb_t=wb_t: e.matmul(p_t[:, :nt], lhsT=wb_t[:, k, j * 128:(j + 1) * 128],
                                                                                             rhs=AT[:, k, t0:t0 + nt], start=(k == 0), stop=(k == nk - 1)),
                             reads=[bAT, bwb], writes=[bp])
                    pts.append((p_t, bp))
                cnt += 1
                sink(gi, bi, t0, nt, pts)
                bg_step()

    def store_sink(ph, dst, bdst, colbase, scale=None):
        stg = [ph.sb("stg%d" % i, [128, 256]) for i in range(4)]
        state = {"i": 0}

        def sink(ti, r0, n, c0, cn, p_t, bp):
            s_t, bs = stg[state["i"] % 4]
            use_act = state["i"] % 2 == 0
            state["i"] += 1
            if use_act:
                if scale is None:
                    P.op("act", lambda e: e.activation(out=s_t[:n, :cn], in_=p_t[:n, :cn], func=AF.Copy), reads=[bp], writes=[bs])
                else:
                    P.op("act", lambda e: e.activation(out=s_t[:n, :cn], in_=p_t[:n, :cn], func=AF.Copy, scale=scale), reads=[bp], writes=[bs])
            else:
                if scale is None:
                    P.op("dve", lambda e: e.tensor_copy(out=s_t[:n, :cn], in_=p_t[:n, :cn]), reads=[bp], writes=[bs])
                else:
                    P.op("dve", lambda e: e.tensor_scalar(out=s_t[:n, :cn], in0=p_t[:n, :cn], scalar1=scale, scalar2=None, op0=ALU.mult), reads=[bp], writes=[bs])
            P.dma("sp", lambda e: e.dma_start(out=dst[r0:r0 + n, colbase + c0:colbase + c0 + cn], in_=s_t[:n, :cn]), reads=[bs], writes=[bdst])
        return sink

    def resid_sink(ph, xsrc, bxsrc, dst, bdst):
        stg = [ph.sb("rs%d" % i, [128, 256]) for i in range(4)]
        xin_t = [ph.sb("rx%d" % i, [128, 256]) for i in range(4)]
        state = {"i": 0}

        def sink(ti, r0, n, c0, cn, p_t, bp):
            s_t, bs = stg[state["i"] % 4]
            x_t, bx = xin_t[state["i"] % 4]
            state["i"] += 1
            P.dma("sp", lambda e: e.dma_start(out=x_t[:n, :cn], in_=xsrc[r0:r0 + n, c0:c0 + cn]), reads=[bxsrc], writes=[bx])
            P.op("dve", lambda e: e.tensor_tensor(out=s_t[:n, :cn], in0=p_t[:n, :cn], in1=x_t[:n, :cn], op=ALU.add), reads=[bp, bx], writes=[bs])
            P.dma("sp", lambda e: e.dma_start(out=dst[r0:r0 + n, c0:c0 + cn], in_=s_t[:n, :cn]), reads=[bs], writes=[bdst])
        return sink

    def layer0_inproj():
        with Phase(P, "A") as ph:
            AT, bAT = ph.sb("AT", [128, KC, NT], BF16)
            norm_phase("A0", xin, db("xin"), norm_mix[0:1, :], AT, bAT)
            with Phase(P, "A1") as p1:
                proj_tok(p1, AT, bAT, KC, w_in, 0, 3088, store_sink(p1, PT, db("PT"), 0))
            with Phase(P, "A2") as p2:
                zt, bz = p2.sb("zt", [128, 30])
                P.op("dve", lambda e: e.memset(zt[:], 0.0), writes=[bz])
                for j in range(8):
                    P.dma("sp", lambda e, j=j: e.dma_start(out=UT[j * 128:(j + 1) * 128, 0:30], in_=zt[:]), reads=[bz], writes=[db("UT")])
                sc, bsc = p2.sb("sc", [30, 1024])
                P.dma("sp", lambda e: e.dma_start(out=sc[:], in_=sconv[:, :]), writes=[bsc])
                pst, bpst = p2.ps("pst", [128, 8, 32])
                sct, bsct = p2.sb("sct", [128, 8, 30])
                for j in range(8):
                    P.op("pe", lambda e, j=j: e.transpose(out=pst[:, j, :30], in_=sc[:30, j * 128:(j + 1) * 128], identity=ident_f[:30, :30]),
                         reads=[bsc, b_ident_f], writes=[bpst])
                P.op("dve", lambda e: e.tensor_copy(out=sct[:], in_=pst[:, :, :30]), reads=[bpst], writes=[bsct])
                P.dma("sp", lambda e: e.dma_start(out=UTS.rearrange("(j p) t -> p j t", p=128)[:, :, 0:30], in_=sct[:]), reads=[bsct], writes=[db("UTS")])
                sg = [p2.sb("sg%d" % i, [128, 512]) for i in range(2)]
                us = [p2.sb("us%d" % i, [128, 512]) for i in range(2)]
                state = {"i": 0}

                def sink(gi, bi, t0, nt, pts):
                    (pa, bpa), (pb, bpb) = pts
                    s_t, bs = sg[state["i"] % 2]
                    u_t, bu = us[state["i"] % 2]
                    state["i"] += 1
                    P.op("act", lambda e: e.activation(out=s_t[:, :nt], in_=pb[:, :nt], func=AF.Sigmoid), reads=[bpb], writes=[bs])
                    P.op("dve", lambda e: e.tensor_tensor(out=u_t[:, :nt], in0=pa[:, :nt], in1=s_t[:, :nt], op=ALU.mult), reads=[bpa, bs], writes=[bu])
                    if t0 < NP:
                        P.dma("sp", lambda e: e.dma_start(out=UT[gi * 128:(gi + 1) * 128, 30 + t0:30 + t0 + nt], in_=u_t[:, :nt]), reads=[bu], writes=[db("UT")])
                    else:
                        P.dma("sp", lambda e: e.dma_start(out=UTS[gi * 128:(gi + 1) * 128, 30:30 + nt], in_=u_t[:, :nt]), reads=[bu], writes=[db("UTS")])
                proj_ch(p2, AT, bAT, KC, w_in, [[3088 + 128 * j, 4112 + 128 * j] for j in range(8)], sink)

    def layer0_gla():
        with Phase(P, "B") as ph:
            C = 64
            SCALE = -1.0 / 16.0
            tri_s, btri = ph.sb("tri_s", [C, C])
            gtr_s, bgtr = ph.sb("gtr_s", [C, C])
            ones_s, bones = ph.sb("ones_s", [C, 8])
            tri01, btri01 = ph.sb("tri01", [C, C])
            P.op("pool", lambda e: e.memset(tri_s[:], SCALE), writes=[btri])
            P.op("pool", lambda e: e.affine_select(out=tri_s[:], in_=tri_s[:], pattern=[[1, C]], compare_op=ALU.is_ge, fill=0.0, base=0, channel_multiplier=-1),
                 reads=[btri], writes=[btri])
            P.op("pool", lambda e: e.memset(gtr_s[:], SCALE), writes=[bgtr])
            P.op("pool", lambda e: e.affine_select(out=gtr_s[:], in_=gtr_s[:], pattern=[[-1, C]], compare_op=ALU.is_ge, fill=0.0, base=-1, channel_multiplier=1),
                 reads=[bgtr], writes=[bgtr])
            P.op("pool", lambda e: e.memset(ones_s[:], SCALE), writes=[bones])
            P.op("pool", lambda e: e.memset(tri01[:], 1.0), writes=[btri01])
            P.op("pool", lambda e: e.affine_select(out=tri01[:], in_=tri01[:], pattern=[[1, C]], compare_op=ALU.is_ge, fill=0.0, base=0, channel_multiplier=-1),
                 reads=[btri01], writes=[btri01])
            wlr, bwlr = ph.sb("wlr", [32, 512])
            glag, bglag = ph.sb("glag", [C, 256])
            P.op("dve", lambda e: e.memset(wlr[:], 0.0), writes=[bwlr])
            P.dma("sp", lambda e: e.dma_start(out=wlr[0:16, :], in_=w_lr[:, :]), writes=[bwlr])
            P.dma("sp", lambda e: e.dma_start(out=wlr[16:17, :], in_=b_lr[:, :]), writes=[bwlr])
            P.dma("sp", lambda e: e.dma_start(out=glag[:], in_=gla_norm[0:1, :].to_broadcast([C, 256])), writes=[bglag])
            S, bS = ph.sb("S", [128, 4, 256])
            P.op("dve", lambda e: e.memset(S[:], 0.0), writes=[bS])
            qkv = [ph.sb("qkv%d" % i, [C, 2048]) for i in range(2)]
            gg = [ph.sb("gg%d" % i, [C, 1024]) for i in range(2)]
            lr = [ph.sb("lr%d" % i, [C, 32]) for i in range(2)]
            for l_t, bl in lr:
                P.op("dve", lambda e, l_t=l_t: e.memset(l_t[:], 0.0), writes=[bl])
                P.op("dve", lambda e, l_t=l_t: e.memset(l_t[:, 16:17], 1.0), writes=[bl])
            lrT, blrT = ph.sb("lrT", [32, C])
            lsp, blsp = ph.sb("lsp", [C, 512])
            EB, bEB = ph.sb("EB", [C, 512])
            ENB, bENB = ph.sb("ENB", [C, 512])
            ED, bED = ph.sb("ED", [C, 512])
            ebl, bebl = ph.sb("ebl", [128, 32])
            qtl, bqtl = ph.sb("qtl", [C, 512])
            ktl, bktl = ph.sb("ktl", [C, 512])
            kdc, bkdc = ph.sb("kdc", [C, 512])
            qT, bqT = ph.sb("qT", [128, 4, C])
            kT, bkT = ph.sb("kT", [128, 4, C])
            att, batt = ph.sb("att", [C, 4, C])
            osb, bosb = ph.sb("osb", [C, 4, 256])
            osq, bosq = ph.sb("osq", [C, 4, 256])
            rs, brs = ph.sb("rs", [C, 12])
            sgl, bsgl = ph.sb("sgl", [C, 1024])
            pA, bpA = ph.ps("pA", [128, 512])
            pB, bpB = ph.ps("pB", [128, 512])
            pC, bpC = ph.ps("pC", [128, 512])
            pD, bpD = ph.ps("pD", [128, 512])
            pO, bpO = ph.ps("pO", [128, 1024])
            pK, bpK = ph.ps("pK", [128, 1024])
            chunks = [(r, min(C, NP - r)) for r in range(0, NP, C)] + [(NP, NS)]
            bPT = db("PT")
            for (t_, b_) in qkv + [(lsp, blsp), (att, batt), (kdc, bkdc)]:
                P.op("pool", lambda e, t_=t_: e.memset(t_[:], 0.0), writes=[b_])
            for ci, (r0, n) in enumerate(chunks):
                q_t, bq = qkv[ci % 2]
                g_t, bg = gg[ci % 2]
                l_t, bl = lr[ci % 2]
                if n < C:
                    for (t_, b_) in [(lsp, blsp), (att, batt), (kdc, bkdc)]:
                        P.op("pool", lambda e, t_=t_: e.memset(t_[:], 0.0), writes=[b_])
                if r0 == NP:
                    P.dma("sp", lambda e: e.dma_start(out=gla_p.rearrange("h d v -> d h v"), in_=S[:]), reads=[bS], writes=[db("gla_p")])
                    P.dma("sp", lambda e: e.dma_start(out=S[:], in_=sgla.rearrange("h d v -> d h v")), writes=[bS])
                P.dma("sp", lambda e, r0=r0, n=n, q_t=q_t: e.dma_start(out=q_t[:n], in_=PT[r0:r0 + n, 0:2048]), reads=[bPT], writes=[bq])
                P.dma("sp", lambda e, r0=r0, n=n, g_t=g_t: e.dma_start(out=g_t[:n], in_=PT[r0:r0 + n, 2048:3072]), reads=[bPT], writes=[bg])
                P.dma("sp", lambda e, r0=r0, n=n, l_t=l_t: e.dma_start(out=l_t[:n, 0:16], in_=PT[r0:r0 + n, 3072:3088]), reads=[bPT], writes=[bl])
                P.op("pe", lambda e, n=n, l_t=l_t: e.transpose(out=pD[:32, :n], in_=l_t[:n, :32], identity=ident_f[:n, :n]), reads=[bl, b_ident_f], writes=[bpD])
                P.op("dve", lambda e, n=n: e.tensor_copy(out=lrT[:, :n], in_=pD[:32, :n]), reads=[bpD], writes=[blrT])
                P.op("pe", lambda e, n=n: e.matmul(pA[:n, :512], lhsT=lrT[:32, :n], rhs=wlr[:32, :], start=True, stop=True), reads=[blrT, bwlr], writes=[bpA])
                if GLA_STAGE < 2:
                    continue
                P.op("act", lambda e, n=n: e.activation(out=lsp[:n], in_=pA[:n, :512], func=AF.Exp, scale=-1.0), reads=[bpA], writes=[blsp])
                P.op("act", lambda e, n=n: e.activation(out=lsp[:n], in_=lsp[:n], func=AF.Ln, bias=1.0, scale=1.0), reads=[blsp], writes=[blsp])
                P.op("pe", lambda e, n=n: e.matmul(pB[:n, :512], lhsT=tri_s[:, :n], rhs=lsp[:, :], start=True, stop=True), reads=[btri, blsp], writes=[bpB])
                P.op("pe", lambda e, n=n: e.matmul(pC[:n, :512], lhsT=gtr_s[:, :n], rhs=lsp[:, :], start=True, stop=True), reads=[bgtr, blsp], writes=[bpC])
                for h in range(4):
                    P.op("pe", lambda e, n=n, h=h: e.matmul(pD[:, 64 + 8 * h:72 + 8 * h], lhsT=lsp[:, h * 128:(h + 1) * 128], rhs=ones_s[:, 0:8], start=True, stop=True),
                         reads=[blsp, bones], writes=[bpD])
                P.op("act", lambda e, n=n: e.activation(out=EB[:n], in_=pB[:n, :512], func=AF.Exp), reads=[bpB], writes=[bEB])
                P.op("act", lambda e, n=n: e.activation(out=ENB[:n], in_=pB[:n, :512], func=AF.Exp, scale=-1.0), reads=[bpB], writes=[bENB])
                P.op("act", lambda e, n=n: e.activation(out=ED[:n], in_=pC[:n, :512], func=AF.Exp), reads=[bpC], writes=[bED])
                P.op("act", lambda e: e.activation(out=ebl[:, :], in_=pD[:, 64:96], func=AF.Exp), reads=[bpD], writes=[bebl])
                P.op("dve", lambda e, n=n, q_t=q_t: e.scalar_tensor_tensor(out=qtl[:n], in0=q_t[:n, 0:512], scalar=128.0 ** -0.5, in1=EB[:n], op0=ALU.mult, op1=ALU.mult),
                     reads=[bq, bEB], writes=[bqtl])
                P.op("dve", lambda e, n=n, q_t=q_t: e.tensor_tensor(out=ktl[:n], in0=q_t[:n, 512:1024], in1=ENB[:n], op=ALU.mult), reads=[bq, bENB], writes=[bktl])
                P.op("dve", lambda e, n=n, q_t=q_t: e.tensor_tensor(out=kdc[:n], in0=q_t[:n, 512:1024], in1=ED[:n], op=ALU.mult), reads=[bq, bED], writes=[bkdc])
                if GLA_STAGE < 3:
                    continue
                for h in range(4):
                    P.op("pe", lambda e, n=n, h=h: e.transpose(out=pA[:, h * C:h * C + n], in_=qtl[:n, h * 128:(h + 1) * 128], identity=ident_f[:n, :n]),
                         reads=[bqtl, b_ident_f], writes=[bpA])
                for h in range(4):
                    P.op("pe", lambda e, n=n, h=h: e.transpose(out=pA[:, 256 + h * C:256 + h * C + n], in_=ktl[:n, h * 128:(h + 1) * 128], identity=ident_f[:n, :n]),
                         reads=[bktl, b_ident_f], writes=[bpA])
                if GLA_STAGE == 3 and GLA_SUB < 1:
                    continue
                P.op("dve", lambda e, n=n: e.tensor_copy(out=qT[:, :, :n], in_=pA[:, 0:256].rearrange("p (h c) -> p h c", h=4)[:, :, :n]), reads=[bpA], writes=[bqT])
                P.op("act", lambda e, n=n: e.activation(out=kT[:, :, :n], in_=pA[:, 256:512].rearrange("p (h c) -> p h c", h=4)[:, :, :n], func=AF.Copy), reads=[bpA], writes=[bkT])
                if GLA_STAGE == 3 and GLA_SUB < 2:
                    continue
                for h in range(4):
                    P.op("pe", lambda e, n=n, h=h: e.matmul(pB[:n, h * C:h * C + n], lhsT=kT[:, h, :n], rhs=qT[:, h, :n], start=True, stop=True),
                         reads=[bkT, bqT], writes=[bpB])
                if GLA_STAGE == 3 and GLA_SUB < 3:
                    continue
                P.op("dve", lambda e, n=n: e.tensor_tensor(out=att[:n, :, :n], in0=pB[:n, 0:256].rearrange("p (h c) -> p h c", h=4)[:, :, :n],
                                                           in1=tri01[:n, :n].unsqueeze(1).to_broadcast([n, 4, n]), op=ALU.mult), reads=[bpB, btri01], writes=[batt])
                if GLA_STAGE < 4:
                    continue
                for h in range(4):
                    P.op("pe", lambda e, n=n, h=h, q_t=q_t: e.matmul(pO[:n, h * 256:(h + 1) * 256], lhsT=att[:, h, :n], rhs=q_t[:, 1024 + h * 256:1024 + (h + 1) * 256],
                                                                  start=True, stop=False), reads=[batt, bq], writes=[bpO])
                    P.op("pe", lambda e, n=n, h=h: e.matmul(pO[:n, h * 256:(h + 1) * 256], lhsT=qT[:, h, :n], rhs=S[:, h, :], start=False, stop=True),
                         reads=[bqT, bS], writes=[bpO])
                for h in range(4):
                    P.op("pe", lambda e, n=n, h=h, q_t=q_t: e.matmul(pK[:, h * 256:(h + 1) * 256], lhsT=kdc[:, h * 128:(h + 1) * 128],
                                                                  rhs=q_t[:, 1024 + h * 256:1024 + (h + 1) * 256], start=True, stop=True), reads=[bkdc, bq], writes=[bpK])
                for h in range(4):
                    P.op("dve", lambda e, h=h: e.scalar_tensor_tensor(out=S[:, h, :], in0=S[:, h, :], scalar=ebl[:, 8 * h:8 * h + 1], in1=pK[:, h * 256:(h + 1) * 256],
                                                                      op0=ALU.mult, op1=ALU.add), reads=[bS, bebl, bpK], writes=[bS])
                if GLA_STAGE < 5:
                    continue
                P.op("act", lambda e, n=n: e.activation(out=osb[:n], in_=pO[:n, :].rearrange("p (h v) -> p h v", h=4), func=AF.Copy), reads=[bpO], writes=[bosb])
                P.op("dve", lambda e, n=n: e.tensor_tensor(out=osq[:n], in0=osb[:n], in1=osb[:n], op=ALU.mult), reads=[bosb], writes=[bosq])
                P.op("dve", lambda e, n=n: e.tensor_reduce(out=rs[:n, 0:4], in_=osq[:n], axis=AX.X, op=ALU.add), reads=[bosq], writes=[brs])
                P.op("act", lambda e, n=n: e.activation(out=rs[:n, 4:8], in_=rs[:n, 0:4], func=AF.Sqrt, bias=eps_t[:n, 0:1], scale=1.0 / 256), reads=[brs, b_eps], writes=[brs])
                P.op("dve", lambda e, n=n: e.reciprocal(out=rs[:n, 8:12], in_=rs[:n, 4:8]), reads=[brs], writes=[brs])
                P.op("act", lambda e, n=n, g_t=g_t: e.activation(out=sgl[:n], in_=g_t[:n], func=AF.Silu), reads=[bg], writes=[bsgl])
                P.op("dve", lambda e, n=n: e.tensor_tensor(out=osb[:n], in0=osb[:n], in1=rs[:n, 8:12].unsqueeze(2).to_broadcast([n, 4, 256]), op=ALU.mult),
                     reads=[bosb, brs], writes=[bosb])
                P.op("dve", lambda e, n=n: e.tensor_tensor(out=osb[:n], in0=osb[:n], in1=glag[:n, :].unsqueeze(1).to_broadcast([n, 4, 256]), op=ALU.mult),
                     reads=[bosb, bglag], writes=[bosb])
                P.op("dve", lambda e, n=n: e.tensor_tensor(out=osq[:n], in0=osb[:n], in1=sgl[:n].rearrange("p (h v) -> p h v", h=4), op=ALU.mult),
                     reads=[bosb, bsgl], writes=[bosq])
                P.dma("sp", lambda e, r0=r0, n=n: e.dma_start(out=OA[r0:r0 + n, :], in_=osq[:n].rearrange("p h v -> p (h v)")), reads=[bosq], writes=[db("OA")])
            P.dma("sp", lambda e: e.dma_start(out=gla_s.rearrange("h d v -> d h v"), in_=S[:]), reads=[bS], writes=[db("gla_s")])


    def layer0_conv_out():
        with Phase(P, "CD") as ph:
            OT, bOT = ph.sb("OT", [128, KC, NT], BF16)
            with Phase(P, "C") as pc:
                Cc, bC = pc.sb("Cc", [128, 8, NT])
                cwr, bcwr = pc.sb("cwr", [31, 1024])
                cwT, bcwT = pc.sb("cwT", [128, 8, 32])
                cvr, bcvr = pc.sb("cvr", [24, 128])
                cv, bcv = pc.sb("cv", [128, 24])
                pst, bpst = pc.ps("pst", [128, 512])
                P.dma("sp", lambda e: e.dma_start(out=cwr[:], in_=conv_w[:, :]), writes=[bcwr])
                P.dma("sp", lambda e: e.dma_start(out=cvr[:], in_=conv_vec[:, :]), writes=[bcvr])
                for j in range(8):
                    P.op("pe", lambda e, j=j: e.transpose(out=pst[:, j * 32:j * 32 + 31], in_=cwr[:31, j * 128:(j + 1) * 128], identity=ident_f[:31, :31]),
                         reads=[bcwr, b_ident_f], writes=[bpst])
                P.op("dve", lambda e: e.tensor_copy(out=cwT[:, :, 0:31], in_=pst[:, 0:256].rearrange("p (j k) -> p j k", j=8)[:, :, 0:31]), reads=[bpst], writes=[bcwT])
                P.op("pe", lambda e: e.transpose(out=pst[:, 256:280], in_=cvr[:24, :], identity=ident_f[:24, :24]), reads=[bcvr, b_ident_f], writes=[bpst])
                P.op("dve", lambda e: e.tensor_copy(out=cv[:], in_=pst[:, 256:280]), reads=[bpst], writes=[bcv])
                ut = [pc.sb("ut%d" % i, [128, 30 + NP]) for i in range(2)]
                uts = [pc.sb("uts%d" % i, [128, 32 + NS]) for i in range(2)]
                cbuf, bcbuf = pc.sb("cbuf", [32, 1024])
                cbufs, bcbufs = pc.sb("cbufs", [32, 1024])
                pcb, bpcb = pc.ps("pcb", [128, 512])
                for j in range(8):
                    u_t, bu = ut[j % 2]
                    s_t, bs = uts[j % 2]
                    P.dma("sp", lambda e, j=j, u_t=u_t: e.dma_start(out=u_t[:], in_=UT[j * 128:(j + 1) * 128, :]), reads=[db("UT")], writes=[bu])
                    P.dma("sp", lambda e, j=j, s_t=s_t: e.dma_start(out=s_t[:, 0:30 + NS], in_=UTS[j * 128:(j + 1) * 128, :]), reads=[db("UTS")], writes=[bs])
                    for (src, bsrc, L, c0) in ((u_t, bu, NP, 0), (s_t, bs, NS, NP)):
                        P.op("dve", lambda e, j=j, src=src, L=L, c0=c0: e.tensor_scalar(out=Cc[:, j, c0:c0 + L], in0=src[:, 0:L], scalar1=cwT[:, j, 0:1], scalar2=cv[:, j:j + 1],
                                                                                 op0=ALU.mult, op1=ALU.add), reads=[bsrc, bcwT, bcv], writes=[bC])
                        for k in range(1, 31):
                            P.op("dve", lambda e, j=j, k=k, src=src, L=L, c0=c0: e.scalar_tensor_tensor(out=Cc[:, j, c0:c0 + L], in0=src[:, k:k + L], scalar=cwT[:, j, k:k + 1],
                                                                                                   in1=Cc[:, j, c0:c0 + L], op0=ALU.mult, op1=ALU.add),
                                 reads=[bsrc, bcwT, bC], writes=[bC])
                    P.op("pe", lambda e, j=j, u_t=u_t: e.transpose(out=pcb[:30, j * 128:(j + 1) * 128] if j < 4 else pcb[:30, (j - 4) * 128:(j - 3) * 128],
                                                               in_=u_t[:, NP:NP + 30], identity=ident_f[:, :]), reads=[bu, b_ident_f], writes=[bpcb])
                    P.op("act", lambda e, j=j: e.activation(out=cbuf[:30, j * 128:(j + 1) * 128], in_=pcb[:30, (j % 4) * 128:(j % 4 + 1) * 128], func=AF.Copy), reads=[bpcb], writes=[bcbuf])
                    P.op("pe", lambda e, j=j, s_t=s_t: e.transpose(out=pcb[:30, (j % 4) * 128:(j % 4 + 1) * 128], in_=s_t[:, NS:NS + 30], identity=ident_f[:, :]),
                         reads=[bs, b_ident_f], writes=[bpcb])
                    P.op("act", lambda e, j=j: e.activation(out=cbufs[:30, j * 128:(j + 1) * 128], in_=pcb[:30, (j % 4) * 128:(j % 4 + 1) * 128], func=AF.Copy), reads=[bpcb], writes=[bcbufs])
                P.dma("sp", lambda e: e.dma_start(out=conv_p[:, :], in_=cbuf[:30, :]), reads=[bcbuf], writes=[db("conv_p")])
                P.dma("sp", lambda e: e.dma_start(out=conv_s[:, :], in_=cbufs[:30, :]), reads=[bcbufs], writes=[db("conv_s")])
                sq, bsq = pc.sb("sq", [128, 512])
                mean, bmean = pc.sb("mean", [128, 512])
                msq, bmsq = pc.sb("msq", [128, 512])
                rstd, brstd = pc.sb("rstd", [128, 512])
                tmp, btmp = pc.sb("tmp", [128, 512])
                pm, bpm = pc.ps("pm", [128, 512])
                pq, bpq = pc.ps("pq", [128, 512])
                for (t0, nt) in BLOCKS:
                    for j in range(8):
                        P.op("pe", lambda e, j=j, t0=t0, nt=nt: e.matmul(pm[:, :nt], lhsT=ones_f[:, :], rhs=Cc[:, j, t0:t0 + nt], start=(j == 0), stop=(j == 7)),
                             reads=[b_ones_f, bC], writes=[bpm])
                    for j in range(8):
                        P.op("act", lambda e, j=j, t0=t0, nt=nt: e.activation(out=sq[:, :nt], in_=Cc[:, j, t0:t0 + nt], func=AF.Square), reads=[bC], writes=[bsq])
                        P.op("pe", lambda e, j=j, nt=nt: e.matmul(pq[:, :nt], lhsT=ones_f[:, :], rhs=sq[:, :nt], start=(j == 0), stop=(j == 7)), reads=[b_ones_f, bsq], writes=[bpq])
                    P.op("act", lambda e, nt=nt: e.activation(out=mean[:, :nt], in_=pm[:, :nt], func=AF.Copy, scale=1.0 / 1024), reads=[bpm], writes=[bmean])
                    P.op("dve", lambda e, nt=nt: e.tensor_tensor(out=msq[:, :nt], in0=mean[:, :nt], in1=mean[:, :nt], op=ALU.mult), reads=[bmean], writes=[bmsq])
                    P.op("dve", lambda e, nt=nt: e.scalar_tensor_tensor(out=rstd[:, :nt], in0=pq[:, :nt], scalar=1.0 / 1024, in1=msq[:, :nt], op0=ALU.mult, op1=ALU.subtract),
                         reads=[bpq, bmsq], writes=[brstd])
                    P.op("act", lambda e, nt=nt: e.activation(out=rstd[:, :nt], in_=rstd[:, :nt], func=AF.Sqrt, bias=eps_t[:, 0:1], scale=1.0), reads=[brstd, b_eps], writes=[brstd])
                    P.op("dve", lambda e, nt=nt: e.reciprocal(out=rstd[:, :nt], in_=rstd[:, :nt]), reads=[brstd], writes=[brstd])
                    for j in range(8):
                        P.op("dve", lambda e, j=j, t0=t0, nt=nt: e.tensor_tensor(out=tmp[:, :nt], in0=Cc[:, j, t0:t0 + nt], in1=mean[:, :nt], op=ALU.subtract), reads=[bC, bmean], writes=[btmp])
                        P.op("dve", lambda e, nt=nt: e.tensor_tensor(out=tmp[:, :nt], in0=tmp[:, :nt], in1=rstd[:, :nt], op=ALU.mult), reads=[btmp, brstd], writes=[btmp])
                        P.op("dve", lambda e, j=j, nt=nt: e.tensor_scalar(out=tmp[:, :nt], in0=tmp[:, :nt], scalar1=cv[:, 8 + j:9 + j], scalar2=cv[:, 16 + j:17 + j], op0=ALU.mult, op1=ALU.add),
                             reads=[btmp, bcv], writes=[btmp])
                        P.op("act", lambda e, j=j, t0=t0, nt=nt: e.activation(out=OT[:, 8 + j, t0:t0 + nt], in_=tmp[:, :nt], func=AF.Silu), reads=[btmp], writes=[bOT])
            with Phase(P, "D0") as pd:
                oa = [pd.sb("oa%d" % i, [128, 1024]) for i in range(2)]
                ob = [pd.sb("ob%d" % i, [128, 1024], BF16) for i in range(2)]
                pstd = [pd.ps("pst%d" % i, [128, 4, 128], BF16) for i in range(2)]
                for ti, (r0, n) in enumerate(TILES):
                    a_t, ba = oa[ti % 2]
                    b_t, bb = ob[ti % 2]
                    P.dma("sp", lambda e, r0=r0, n=n, a_t=a_t: e.dma_start(out=a_t[:n], in_=OA[r0:r0 + n, :]), reads=[db("OA")], writes=[ba])
                    P.op("pool", lambda e, n=n, a_t=a_t, b_t=b_t: e.tensor_copy(out=b_t[:n], in_=a_t[:n]), reads=[ba], writes=[bb])
                    p_t, bp = pstd[ti % 2]
                    transpose_to_fm(b_t, bb, n, 8, OT, bOT, 0, r0, p_t, bp, evac=("act" if ti % 2 else "dve"))
            with Phase(P, "D1") as pd:
                proj_tok(pd, OT, bOT, KC, w_out_e, 0, D, resid_sink(pd, xin, db("xin"), X1, db("X1")))


    def peer_layer(L, Xsrc, bXsrc, Xdst, bXdst, tag):
        with Phase(P, "E" + tag) as ph:
            AT, bAT = ph.sb("AT", [128, KC, NT], BF16)
            norm_phase("E0" + tag, Xsrc, bXsrc, norm_ffn[L:L + 1, :], AT, bAT, hf_dst=HF, bhf=db("HF"))
            with Phase(P, "E1" + tag) as p1:
                proj_tok(p1, AT, bAT, KC, peer_wq[L], 0, D, store_sink(p1, QP, db("QP"), 0))
        with Phase(P, "F" + tag) as ph:
            NBUF = 4
            kraw, bkraw = ph.sb("kraw", [128, 16, 128])
            keysT, bkeysT = ph.sb("keysT", [128, 16, 128])
            iota16, biota = ph.sb("iota16", [128, 16])
            P.op("pool", lambda e: e.iota(iota16[:], pattern=[[1, 16]], base=0, channel_multiplier=0, allow_small_or_imprecise_dtypes=True), writes=[biota])
            P.dma("sp", lambda e: e.dma_start(out=kraw[:], in_=peer_keys[L].rearrange("g n d -> n g d")), writes=[bkraw])
            pt4 = [ph.ps("pt4_%d" % i, [128, 4, 128]) for i in range(2)]
            for g4 in range(4):
                p_t, bp = pt4[g4 % 2]
                for j in range(4):
                    P.op("pe", lambda e, g4=g4, j=j, p_t=p_t: e.transpose(out=p_t[:, j, :], in_=kraw[:, g4 * 4 + j, :], identity=ident_f[:, :]), reads=[bkraw, b_ident_f], writes=[bp])
                P.op("dve", lambda e, g4=g4, p_t=p_t: e.tensor_copy(out=keysT[:, g4 * 4:g4 * 4 + 4, :], in_=p_t[:]), reads=[bp], writes=[bkeysT])
            qt_, bqt = ph.sb("qt", [128, 2048])
            qT, bqT = ph.sb("qT", [128, 16, 128])
            ssb, bssb = ph.sb("ssb", [128, 16, 128])
            s2, bs2 = ph.sb("s2", [128, 16, 128])
            sv, bsv = ph.sb("sv", [128, 16, 16])
            si, bsi = ph.sb("si", [128, 16, 16], U32)
            sif, bsif = ph.sb("sif", [128, 16, 16])
            cand, bcand = ph.sb("cand", [128, 8, 256])
            cand2, bcand2 = ph.sb("cand2", [128, 8, 256])
            oh, boh = ph.sb("oh", [128, 8, 256])
            best, bbest = ph.sb("best", [128, 8, 16])
            pos, bpos = ph.sb("pos", [128, 8, 16], U32)
            pa_, bpa_ = ph.sb("pa", [128, 8, 16], U32)
            pb_, bpb_ = ph.sb("pb", [128, 8, 16], U32)
            paf, bpaf = ph.sb("paf", [128, 8, 16])
            pbf, bpbf = ph.sb("pbf", [128, 8, 16])
            isel, bisel = ph.sb("isel", [128, 8, 16])
            jsel, bjsel = ph.sb("jsel", [128, 8, 16])
            eidx, beidx = ph.sb("eidx", [128, 128], I32)
            gsum, bgsum = ph.sb("gsum", [128, 16])
            gate, bgate = ph.sb("gate", [128, 8, 16])
            apre, bapre = ph.sb("apre", [128, 128])
            wact, bwact = ph.sb("wact", [128, 128])
            hf_t, bhf_t = ph.sb("hf", [128, 2048])
            xt_, bxt_ = ph.sb("xt", [128, 2048])
            ub = [ph.sb("ub%d" % i, [128, 2048], BF16) for i in range(NBUF)]
            vb = [ph.sb("vb%d" % i, [128, 2048], BF16) for i in range(NBUF)]
            hb_t, bhb_t = ph.sb("hb", [128, 2048], BF16)
            junkb, bjunkb = ph.sb("junkb", [128, 2048], BF16)
            dg, bdg = ph.sb("dg", [128, 128, 128], BF16)
            psc = [ph.ps("psc%d" % i, [128, 4, 128]) for i in range(2)]
            py = [ph.ps("py%d" % i, [128, 512]) for i in range(4)]
            Utab = UVB[0]
            Vtab = UVB[1]
            bg_flush(L)
            eidxB, beidxB = ph.sb("eidxB", [128, 128], I32)
            xtB, bxtB = ph.sb("xtB", [128, 2048])
            eidx2 = [(eidx, beidx), (eidxB, beidxB)]
            xt2 = [(xt_, bxt_), (xtB, bxtB)]
            NTL = len(TILES)

            def S1(ti):
                r0, n = TILES[ti]
                ei, bei = eidx2[ti % 2]
                x_t, bx = xt2[ti % 2]
                P.dma("sp", lambda e, r0=r0, n=n: e.dma_start(out=qt_[:n], in_=QP[r0:r0 + n, :]), reads=[db("QP")], writes=[bqt])
                P.dma("sp", lambda e, r0=r0, n=n: e.dma_start(out=hf_t[:n], in_=HF[r0:r0 + n, :]), reads=[db("HF")], writes=[bhf_t])
                P.dma("sp", lambda e, r0=r0, n=n, x_t=x_t: e.dma_start(out=x_t[:n], in_=Xsrc[r0:r0 + n, :]), reads=[bXsrc], writes=[bx])
                for g4 in range(4):
                    p_t, bp = pt4[g4 % 2]
                    for j in range(4):
                        P.op("pe", lambda e, g4=g4, j=j, p_t=p_t, n=n: e.transpose(out=p_t[:, j, :n], in_=qt_[:n, (g4 * 4 + j) * 128:(g4 * 4 + j + 1) * 128], identity=ident_f[:n, :n]),
                             reads=[bqt, b_ident_f], writes=[bp])
                    P.op("act", lambda e, g4=g4, p_t=p_t, n=n: e.activation(out=qT[:, g4 * 4:g4 * 4 + 4, :n], in_=p_t[:, :, :n], func=AF.Copy), reads=[bp], writes=[bqT])
                for g4 in range(4):
                    p_t, bp = psc[g4 % 2]
                    for j in range(4):
                        P.op("pe", lambda e, g4=g4, j=j, p_t=p_t, n=n: e.matmul(p_t[:n, j, :], lhsT=qT[:, g4 * 4 + j, :n], rhs=keysT[:, g4 * 4 + j, :], start=True, stop=True),
                             reads=[bqT, bkeysT], writes=[bp])
                    P.op("dve", lambda e, g4=g4, p_t=p_t, n=n: e.tensor_copy(out=ssb[:n, g4 * 4:g4 * 4 + 4, :], in_=p_t[:n]), reads=[bp], writes=[bssb])
                for g in range(16):
                    P.op("dve", lambda e, g=g, n=n: e.max(out=sv[:n, g, 0:8], in_=ssb[:n, g, :]), reads=[bssb], writes=[bsv])
                    P.op("dve", lambda e, g=g, n=n: e.max_index(out=si[:n, g, 0:8], in_max=sv[:n, g, 0:8], in_values=ssb[:n, g, :]), reads=[bssb, bsv], writes=[bsi])
                    P.op("dve", lambda e, g=g, n=n: e.match_replace(out=s2[:n, g, :], in_to_replace=sv[:n, g, 0:8], in_values=ssb[:n, g, :], imm_value=-1e30), reads=[bssb, bsv], writes=[bs2])
                    P.op("dve", lambda e, g=g, n=n: e.max(out=sv[:n, g, 8:16], in_=s2[:n, g, :]), reads=[bs2], writes=[bsv])
                    P.op("dve", lambda e, g=g, n=n: e.max_index(out=si[:n, g, 8:16], in_max=sv[:n, g, 8:16], in_values=s2[:n, g, :]), reads=[bs2, bsv], writes=[bsi])
                P.op("dve", lambda e, n=n: e.tensor_copy(out=sif[:n], in_=si[:n]), reads=[bsi], writes=[bsif])
                sv4 = sv[:].rearrange("p (h c) k -> p h c k", c=2)
                sif4 = sif[:].rearrange("p (h c) k -> p h c k", c=2)
                cand4 = cand[:].rearrange("p h (a b) -> p h a b", a=16)
                oh4 = oh[:].rearrange("p h (a b) -> p h a b", a=16)
                P.op("dve", lambda e, n=n: e.tensor_tensor(out=cand4[:n], in0=sv4[:n, :, 0, :].unsqueeze(3).to_broadcast([n, 8, 16, 16]),
                                                           in1=sv4[:n, :, 1, :].unsqueeze(2).to_broadcast([n, 8, 16, 16]), op=ALU.add), reads=[bsv], writes=[bcand])
                for h in range(8):
                    P.op("dve", lambda e, h=h, n=n: e.max(out=best[:n, h, 0:8], in_=cand[:n, h, :]), reads=[bcand], writes=[bbest])
                    P.op("dve", lambda e, h=h, n=n: e.max_index(out=pos[:n, h, 0:8], in_max=best[:n, h, 0:8], in_values=cand[:n, h, :]), reads=[bcand, bbest], writes=[bpos])
                    P.op("dve", lambda e, h=h, n=n: e.match_replace(out=cand2[:n, h, :], in_to_replace=best[:n, h, 0:8], in_values=cand[:n, h, :], imm_value=-1e30), reads=[bcand, bbest], writes=[bcand2])
                    P.op("dve", lambda e, h=h, n=n: e.max(out=best[:n, h, 8:16], in_=cand2[:n, h, :]), reads=[bcand2], writes=[bbest])
                    P.op("dve", lambda e, h=h, n=n: e.max_index(out=pos[:n, h, 8:16], in_max=best[:n, h, 8:16], in_values=cand2[:n, h, :]), reads=[bcand2, bbest], writes=[bpos])
                P.op("dve", lambda e, n=n: e.tensor_single_scalar(out=pa_[:n], in_=pos[:n], scalar=4, op=ALU.arith_shift_right), reads=[bpos], writes=[bpa_])
                P.op("dve", lambda e, n=n: e.tensor_single_scalar(out=pb_[:n], in_=pos[:n], scalar=15, op=ALU.bitwise_and), reads=[bpos], writes=[bpb_])
                P.op("dve", lambda e, n=n: e.tensor_copy(out=paf[:n], in_=pa_[:n]), reads=[bpa_], writes=[bpaf])
                P.op("dve", lambda e, n=n: e.tensor_copy(out=pbf[:n], in_=pb_[:n]), reads=[bpb_], writes=[bpbf])
                io4 = iota16[:n, :].unsqueeze(1).unsqueeze(1).to_broadcast([n, 8, 16, 16])
                for (pf, bpf, c, dst, bdst) in ((paf, bpaf, 0, isel, bisel), (pbf, bpbf, 1, jsel, bjsel)):
                    P.op("dve", lambda e, n=n, pf=pf, io4=io4: e.tensor_tensor(out=oh4[:n], in0=pf[:n].unsqueeze(3).to_broadcast([n, 8, 16, 16]), in1=io4, op=ALU.is_equal),
                         reads=[bpf, biota], writes=[boh])
                    P.op("dve", lambda e, n=n, c=c: e.tensor_tensor(out=oh4[:n], in0=oh4[:n], in1=sif4[:n, :, c, :].unsqueeze(2).to_broadcast([n, 8, 16, 16]), op=ALU.mult),
                         reads=[boh, bsif], writes=[boh])
                    P.op("dve", lambda e, n=n, dst=dst: e.tensor_reduce(out=dst[:n], in_=oh4[:n], axis=AX.X, op=ALU.add), reads=[boh], writes=[bdst])
                P.op("dve", lambda e, n=n: e.scalar_tensor_tensor(out=isel[:n], in0=isel[:n], scalar=128.0, in1=jsel[:n], op0=ALU.mult, op1=ALU.add), reads=[bisel, bjsel], writes=[bisel])
                if L > 0:
                    P.op("dve", lambda e, n=n: e.tensor_scalar(out=isel[:n], in0=isel[:n], scalar1=float(L * NEXP), scalar2=None, op0=ALU.add), reads=[bisel], writes=[bisel])
                P.op("dve", lambda e, n=n, ei=ei: e.tensor_copy(out=ei[:n], in_=isel[:n].rearrange("p h k -> p (h k)")), reads=[bisel], writes=[bei])
                P.op("dve", lambda e, n=n: e.tensor_tensor(out=gate[:n], in0=best[:n], in1=best[:n, :, 0:1].to_broadcast([n, 8, 16]), op=ALU.subtract), reads=[bbest], writes=[bgate])
                P.op("act", lambda e, n=n: e.activation(out=gate[:n], in_=gate[:n], func=AF.Exp), reads=[bgate], writes=[bgate])
                P.op("dve", lambda e, n=n: e.tensor_reduce(out=gsum[:n, 0:8], in_=gate[:n], axis=AX.X, op=ALU.add), reads=[bgate], writes=[bgsum])
                P.op("dve", lambda e, n=n: e.reciprocal(out=gsum[:n, 8:16], in_=gsum[:n, 0:8]), reads=[bgsum], writes=[bgsum])
                P.op("dve", lambda e, n=n: e.tensor_tensor(out=gate[:n], in0=gate[:n], in1=gsum[:n, 8:16].unsqueeze(2).to_broadcast([n, 8, 16]), op=ALU.mult), reads=[bgate, bgsum], writes=[bgate])
                P.op("pool", lambda e, n=n: e.tensor_copy(out=hb_t[:n], in_=hf_t[:n]), reads=[bhf_t], writes=[bhb_t])
                P.op("dve", lambda e, n=n: e.memset(apre[:n], 0.0), writes=[bapre])

            def S2_slot(ti, sl):
                r0, n = TILES[ti]
                ei, bei = eidx2[ti % 2]
                u_t, bu = ub[sl % NBUF]
                P.dma("pool", lambda e, sl=sl, n=n, u_t=u_t, ei=ei: e.indirect_dma_start(out=u_t[:n], out_offset=None, in_=Utab, in_offset=bass.IndirectOffsetOnAxis(ap=ei[:n, sl:sl + 1], axis=0)),
                      reads=[bei, db("UVB%d" % L)], writes=[bu])
                P.op("dve", lambda e, sl=sl, n=n, u_t=u_t: e.scalar_tensor_tensor(out=junkb[:n], in0=u_t[:n], scalar=1.0, in1=hb_t[:n], op0=ALU.mult, op1=ALU.mult, accum_out=apre[:n, sl:sl + 1]),
                     reads=[bu, bhb_t, bapre], writes=[bjunkb, bapre])

            def S2_post(ti):
                r0, n = TILES[ti]
                P.op("act", lambda e, n=n: e.activation(out=wact[:n], in_=apre[:n], func=AF.Gelu_apprx_tanh), reads=[bapre], writes=[bwact])
                P.op("dve", lambda e, n=n: e.tensor_tensor(out=wact[:n], in0=wact[:n], in1=gate[:n].rearrange("p h k -> p (h k)"), op=ALU.mult), reads=[bwact, bgate], writes=[bwact])
                P.op("dve", lambda e, n=n: e.tensor_tensor(out=dg[:n], in0=ident_b[:n, :].unsqueeze(1).to_broadcast([n, 128, 128]),
                                                           in1=wact[:n, :].unsqueeze(2).to_broadcast([n, 128, 128]), op=ALU.mult), reads=[b_ident_b, bwact], writes=[bdg])

            def S3_slot(ti, sl):
                r0, n = TILES[ti]
                ei, bei = eidx2[ti % 2]
                v_t, bv = vb[sl % NBUF]
                P.dma("pool", lambda e, sl=sl, n=n, v_t=v_t, ei=ei: e.indirect_dma_start(out=v_t[:n], out_offset=None, in_=Vtab, in_offset=bass.IndirectOffsetOnAxis(ap=ei[:n, sl:sl + 1], axis=0)),
                      reads=[bei, db("UVB%d" % L)], writes=[bv])
                for c in range(4):
                    P.op("pe", lambda e, sl=sl, n=n, c=c, v_t=v_t: e.matmul(py[c][0][:n, :], lhsT=dg[:n, sl, :n], rhs=v_t[:n, c * 512:(c + 1) * 512], start=(sl == 0), stop=(sl == 127)),
                         reads=[bdg, bv], writes=[py[c][1]])

            def S3_post(ti):
                r0, n = TILES[ti]
                x_t, bx = xt2[ti % 2]
                for c in range(4):
                    P.op("dve", lambda e, n=n, c=c, x_t=x_t: e.tensor_tensor(out=x_t[:n, c * 512:(c + 1) * 512], in0=x_t[:n, c * 512:(c + 1) * 512], in1=py[c][0][:n, :], op=ALU.add),
                         reads=[bx, py[c][1]], writes=[bx])
                P.dma("sp", lambda e, r0=r0, n=n, x_t=x_t: e.dma_start(out=Xdst[r0:r0 + n, :], in_=x_t[:n]), reads=[bx], writes=[bXdst])

            S1(0)
            for sl in range(128):
                S2_slot(0, sl)
            S2_post(0)
            for ti in range(NTL):
                nxt = ti + 1 < NTL
                if nxt:
                    S1(ti + 1)
                for sl in range(128):
                    S3_slot(ti, sl)
                    if nxt:
                        S2_slot(ti + 1, sl)
                S3_post(ti)
                if nxt:
                    S2_post(ti + 1)


    def layer1_sample(ph, OT, bOT):
        with Phase(P, "Lc") as pc:
            biasb, bbias = pc.sb("biasb", [128, 16])
            P.dma("sp", lambda e: e.dma_start(out=biasb[:], in_=sb_bias[0:1, :].to_broadcast([128, 16])), writes=[bbias])
            biasr, bbiasr = pc.sb("biasr", [128, 16, 8])
            P.op("dve", lambda e: e.tensor_copy(out=biasr[:], in_=biasb[:].unsqueeze(2).to_broadcast([128, 16, 8])), reads=[bbias], writes=[bbiasr])
            ustr, bustr = pc.sb("ustr", [128, 128])
            P.op("pool", lambda e: e.memset(ustr[:], 1.0), writes=[bustr])
            P.op("pool", lambda e: e.affine_select(out=ustr[:], in_=ustr[:], pattern=[[-1, 128]], compare_op=ALU.is_ge, fill=0.0, base=-1, channel_multiplier=1),
                 reads=[bustr], writes=[bustr])
            mnew, bmnew = pc.sb("mnew", [128, 16, 8])
            P.op("pool", lambda e: e.memset(mnew[:], 1.0), writes=[bmnew])
            P.op("pool", lambda e: e.affine_select(out=mnew[:], in_=mnew[:], pattern=[[0, 16], [1, 8]], compare_op=ALU.is_ge, fill=0.0, base=-1, channel_multiplier=-1),
                 reads=[bmnew], writes=[bmnew])
            pti, bpti = pc.sb("pti", [128, NPAGES], I32)
            ptf, bptf = pc.sb("ptf", [128, NPAGES])
            pidx, bpidx = pc.sb("pidx", [128, NPAGES], I32)
            iop, biop = pc.sb("iop", [128, 1])
            P.dma("sp", lambda e: e.dma_start(out=pti[:], in_=ptab[0:1, :].to_broadcast([128, NPAGES])), writes=[bpti])
            P.op("pool", lambda e: e.iota(iop[:], pattern=[[0, 1]], base=0, channel_multiplier=1, allow_small_or_imprecise_dtypes=True), writes=[biop])
            P.op("dve", lambda e: e.tensor_copy(out=ptf[:], in_=pti[:]), reads=[bpti], writes=[bptf])
            P.op("dve", lambda e: e.tensor_scalar(out=ptf[:], in0=ptf[:], scalar1=128.0, scalar2=iop[:, 0:1], op0=ALU.mult, op1=ALU.add), reads=[bptf, biop], writes=[bptf])
            P.op("dve", lambda e: e.tensor_copy(out=pidx[:], in_=ptf[:]), reads=[bptf], writes=[bpidx])
            qs, bqs = pc.sb("qs", [128, 16, 8])
            ks, bks = pc.sb("ks", [128, 16, 8])
            qsb, bqsb = pc.sb("qsb", [128, 16, 8], BF16)
            ksb, bksb = pc.sb("ksb", [128, 16, 8], BF16)
            P.dma("sp", lambda e: e.dma_start(out=qs[:], in_=QT[:, NP:NP + 8].rearrange("(h p) t -> p h t", p=128)), reads=[db("QT")], writes=[bqs])
            P.dma("sp", lambda e: e.dma_start(out=ks[:], in_=KT[:, NP:NP + 8].rearrange("(h p) t -> p h t", p=128)), reads=[db("KT")], writes=[bks])
            P.op("dve", lambda e: e.tensor_copy(out=qsb[:], in_=qs[:]), reads=[bqs], writes=[bqsb])
            P.op("dve", lambda e: e.tensor_copy(out=ksb[:], in_=ks[:]), reads=[bks], writes=[bksb])
            kp = [pc.sb("kp%d" % i, [128, 2048]) for i in range(2)]
            vp = [pc.sb("vp%d" % i, [128, 2048]) for i in range(2)]
            vpb = [pc.sb("vpb%d" % i, [128, 2048], BF16) for i in range(2)]
            kTb, bkTb = pc.sb("kTb", [128, 16, 128], BF16)
            zb, bzb = pc.sb("zb", [128, 128])
            spt, bspt = pc.sb("spt", [128, 128])
            dt_, bdt = pc.sb("dt", [128, 128])
            Abt, bAbt = pc.sb("Abt", [128, 128], BF16)
            spsum, bsps = pc.sb("spsum", [128, 128])
            P.op("pool", lambda e: e.memset(spsum[:], 0.0), writes=[bsps])
            P.op("pool", lambda e: e.memset(spt[:], 0.0), writes=[bspt])
            P.op("pool", lambda e: e.memset(Abt[:], 0.0), writes=[bAbt])
            ptr = [pc.ps("ptr%d" % i, [128, 4, 128]) for i in range(2)]
            pz, bpz = pc.ps("pz", [128, 512])
            pT, bpT = pc.ps("pT", [128, 512])
            po, bpo = pc.ps("po", [128, 512])
            zbf, bzbf = pc.sb("zbf", [128, 128], BF16)
            P.op("pool", lambda e: e.memset(zbf[:], 0.0), writes=[bzbf])
            P.op("pe", lambda e: e.matmul(po[:, 0:128], lhsT=zbf[:, :], rhs=zbf[:, :], start=True, stop=False), reads=[bzbf], writes=[bpo])
            v_t, bv = vp[0]
            vb_t, bvb = vpb[0]
            P.op("pool", lambda e, v_t=v_t: e.memset(v_t[:], 0.0), writes=[bv])
            P.dma("sp", lambda e, v_t=v_t: e.dma_start(out=v_t[:8, :], in_=v_all[NP:NP + 8, :]), reads=[db("v_all")], writes=[bv])
            P.op("pool", lambda e, v_t=v_t, vb_t=vb_t: e.tensor_copy(out=vb_t[:], in_=v_t[:]), reads=[bv], writes=[bvb])
            blocks = [("new", 8)] + [(p, 128) for p in range(NPAGES - 1, -1, -1)]
            for bi, (pg, nk) in enumerate(blocks):
                first = bi == 0
                last = bi == len(blocks) - 1
                if pg != "new":
                    k_t, bk = kp[bi % 2]
                    v_t, bv = vp[bi % 2]
                    vb_t, bvb = vpb[bi % 2]
                    P.dma("pool", lambda e, pg=pg, k_t=k_t: e.indirect_dma_start(out=k_t[:], out_offset=None, in_=cache_k, in_offset=bass.IndirectOffsetOnAxis(ap=pidx[:, pg:pg + 1], axis=0)),
                          reads=[bpidx], writes=[bk])
                    P.dma("pool", lambda e, pg=pg, v_t=v_t: e.indirect_dma_start(out=v_t[:], out_offset=None, in_=cache_v, in_offset=bass.IndirectOffsetOnAxis(ap=pidx[:, pg:pg + 1], axis=0)),
                          reads=[bpidx], writes=[bv])
                    P.op("act", lambda e, v_t=v_t, vb_t=vb_t: e.activation(out=vb_t[:], in_=v_t[:], func=AF.Copy), reads=[bv], writes=[bvb])
                    for g4 in range(4):
                        p_t, bp = ptr[g4 % 2]
                        for j in range(4):
                            P.op("pe", lambda e, g4=g4, j=j, p_t=p_t, k_t=k_t: e.transpose(out=p_t[:, j, :], in_=k_t[:, (g4 * 4 + j) * 128:(g4 * 4 + j + 1) * 128], identity=ident_f[:, :]),
                                 reads=[bk, b_ident_f], writes=[bp])
                        P.op("dve", lambda e, g4=g4, p_t=p_t: e.tensor_copy(out=kTb[:, g4 * 4:g4 * 4 + 4, :], in_=p_t[:]), reads=[bp], writes=[bkTb])
                    for h in range(16):
                        P.op("pe", lambda e, h=h: e.matmul(pz[:, h * 8:(h + 1) * 8], lhsT=kTb[:, h, :], rhs=qsb[:, h, :], start=True, stop=True), reads=[bkTb, bqsb], writes=[bpz])
                else:
                    for h in range(16):
                        P.op("pe", lambda e, h=h: e.matmul(pz[:8, h * 8:(h + 1) * 8], lhsT=ksb[:, h, :], rhs=qsb[:, h, :], start=True, stop=True), reads=[bksb, bqsb], writes=[bpz])
                P.op("dve", lambda e, nk=nk: e.tensor_tensor(out=zb[:nk], in0=pz[:nk, 0:128], in1=biasr[:nk].rearrange("p h q -> p (h q)"), op=ALU.add), reads=[bpz, bbiasr], writes=[bzb])
                P.op("act", lambda e, nk=nk: e.activation(out=spt[:nk], in_=zb[:nk], func=AF.Exp), reads=[bzb], writes=[bspt])
                P.op("act", lambda e, nk=nk: e.activation(out=spt[:nk], in_=spt[:nk], func=AF.Ln, bias=1.0, scale=1.0), reads=[bspt], writes=[bspt])
                if first:
                    P.op("dve", lambda e, nk=nk: e.tensor_tensor(out=spt[:nk], in0=spt[:nk], in1=mnew[:nk].rearrange("p h q -> p (h q)"), op=ALU.mult), reads=[bspt, bmnew], writes=[bspt])
                P.op("pe", lambda e, nk=nk, first=first: e.matmul(pT[:nk, 0:128], lhsT=ustr[:, :nk], rhs=spt[:, :], start=True, stop=first), reads=[bustr, bspt], writes=[bpT])
                if not first:
                    P.op("pe", lambda e, nk=nk: e.matmul(pT[:nk, 0:128], lhsT=ones_f[:, :nk], rhs=spsum[:, :], start=False, stop=True), reads=[b_ones_f, bsps], writes=[bpT])
                P.op("dve", lambda e, nk=nk: e.tensor_tensor(out=dt_[:nk], in0=zb[:nk], in1=spt[:nk], op=ALU.subtract), reads=[bzb, bspt], writes=[bdt])
                P.op("dve", lambda e, nk=nk: e.tensor_tensor(out=dt_[:nk], in0=dt_[:nk], in1=pT[:nk, 0:128], op=ALU.subtract), reads=[bdt, bpT], writes=[bdt])
                if first:
                    P.op("act", lambda e, nk=nk: e.activation(out=dt_[:nk], in_=dt_[:nk], func=AF.Exp), reads=[bdt], writes=[bdt])
                    P.op("dve", lambda e, nk=nk: e.tensor_tensor(out=Abt[:nk], in0=dt_[:nk], in1=mnew[:nk].rearrange("p h q -> p (h q)"), op=ALU.mult), reads=[bdt, bmnew], writes=[bAbt])
                else:
                    P.op("act", lambda e, nk=nk: e.activation(out=Abt[:nk], in_=dt_[:nk], func=AF.Exp), reads=[bdt], writes=[bAbt])
                if not last:
                    P.op("pool", lambda e: e.tensor_tensor(out=spsum[:], in0=spsum[:], in1=spt[:], op=ALU.add), reads=[bsps, bspt], writes=[bsps])
                for h in range(16):
                    P.op("pe", lambda e, h=h, vb_t=vb_t, first=first, last=last: e.matmul(po[:, h * 8:(h + 1) * 8], lhsT=vb_t[:, h * 128:(h + 1) * 128], rhs=Abt[:, h * 8:(h + 1) * 8],
                                                                                       start=False, stop=(last and h == 15)), reads=[bvb, bAbt], writes=[bpo])
            P.op("act", lambda e: e.activation(out=OT[:, :, NP:NP + 8], in_=po[:, 0:128].rearrange("p (h q) -> p h q", h=16), func=AF.Copy), reads=[bpo], writes=[bOT])

    def layer1(Xsrc, bXsrc, Xdst, bXdst):
        NPT = (NP + 127) // 128
        PBLOCKS = [b for b in BLOCKS if b[0] < NP]
        with Phase(P, "L") as ph:
            with Phase(P, "La") as pa:
                AT, bAT = pa.sb("AT", [128, KC, NT], BF16)
                norm_phase("La0", Xsrc, bXsrc, norm_mix[1:2, :], AT, bAT)
                with Phase(P, "La1") as p1:
                    proj_tok(p1, AT, bAT, KC, w_qkv, 2048, 2048, store_sink(p1, k_all, db("k_all"), 0))
                with Phase(P, "La2") as p1:
                    proj_tok(p1, AT, bAT, KC, w_qkv, 4096, 2048, store_sink(p1, v_all, db("v_all"), 0))
                with Phase(P, "La3") as p1:
                    stg = [p1.sb("cs%d" % i, [128, 512]) for i in range(4)]
                    st_ = {"i": 0}

                    def mk_sink(dst, bdst, scale):
                        def sink(gi, bi, t0, nt, pts):
                            for j, (p_t, bp) in enumerate(pts):
                                s_t, bs = stg[st_["i"] % 4]
                                st_["i"] += 1
                                P.op("act", lambda e, s_t=s_t, p_t=p_t, nt=nt: e.activation(out=s_t[:, :nt], in_=p_t[:, :nt], func=AF.Copy, scale=scale), reads=[bp], writes=[bs])
                                P.dma("sp", lambda e, s_t=s_t, gi=gi, j=j, t0=t0, nt=nt: e.dma_start(out=dst[(gi * 2 + j) * 128:(gi * 2 + j + 1) * 128, t0:t0 + nt], in_=s_t[:, :nt]),
                                      reads=[bs], writes=[bdst])
                        return sink
                    proj_ch(p1, AT, bAT, KC, w_qkv, [[c, c + 128] for c in range(0, 2048, 256)], mk_sink(QT, db("QT"), 128.0 ** -0.5), tag="q")
                with Phase(P, "La4") as p1:
                    stg = [p1.sb("cs%d" % i, [128, 512]) for i in range(4)]
                    st_ = {"i": 0}
                    proj_ch(p1, AT, bAT, KC, w_qkv, [[c, c + 128] for c in range(2048, 4096, 256)], mk_sink(KT, db("KT"), 1.0), tag="k")
            OT, bOT = ph.sb("OT", [128, KC, NT], BF16)
            with Phase(P, "Lb") as pb:
                biasb, bbias = pb.sb("biasb", [128, 16])
                P.dma("sp", lambda e: e.dma_start(out=biasb[:], in_=sb_bias[0:1, :].to_broadcast([128, 16])), writes=[bbias])
                ustr, bustr = pb.sb("ustr", [128, 128])
                P.op("pool", lambda e: e.memset(ustr[:], 1.0), writes=[bustr])
                P.op("pool", lambda e: e.affine_select(out=ustr[:], in_=ustr[:], pattern=[[-1, 128]], compare_op=ALU.is_ge, fill=0.0, base=-1, channel_multiplier=1),
                     reads=[bustr], writes=[bustr])
                masks = []
                for r in range(4):
                    m_t, bm = pb.sb("mask%d" % r, [128, 512])
                    P.op("pool", lambda e, m_t=m_t: e.memset(m_t[:], 1.0), writes=[bm])
                    P.op("pool", lambda e, m_t=m_t, r=r: e.affine_select(out=m_t[:], in_=m_t[:], pattern=[[1, 512]], compare_op=ALU.is_ge, fill=0.0, base=-128 * r - 1, channel_multiplier=-1),
                         reads=[bm], writes=[bm])
                    masks.append((m_t, bm))
                qf = [pb.sb("qf%d" % i, [128, NP]) for i in range(2)]
                kf = [pb.sb("kf%d" % i, [128, NP]) for i in range(2)]
                vf = [pb.sb("vf%d" % i, [128, NPT, 128]) for i in range(2)]
                qb, bqb = pb.sb("qb", [128, NP], BF16)
                kb_, bkb = pb.sb("kb", [128, NP], BF16)
                vbb, bvbb = pb.sb("vbb", [128, NPT, 128], BF16)
                for v_t, bv in vf:
                    P.op("pool", lambda e, v_t=v_t: e.memset(v_t[:], 0.0), writes=[bv])
                esb = [pb.sb("esb%d" % i, [128, 512]) for i in range(2)]
                t1 = [pb.sb("t1%d" % i, [128, 512]) for i in range(2)]
                Ab = [pb.sb("Ab%d" % i, [128, 512], BF16) for i in range(2)]
                for a_t, ba in Ab:
                    P.op("pool", lambda e, a_t=a_t: e.memset(a_t[:], 0.0), writes=[ba])
                spsum, bsps = pb.sb("spsum", [128, 512])
                pz = [pb.ps("pz%d" % i, [128, 512]) for i in range(2)]
                pT = [pb.ps("pT%d" % i, [128, 512]) for i in range(2)]
                po = [pb.ps("po%d" % i, [128, 512]) for i in range(2)]
                nfull = NP // 128
                rem = NP - nfull * 128
                cnt = 0
                och = 0
                for h in range(16):
                    q_t, bq = qf[h % 2]
                    k_t, bk = kf[h % 2]
                    v_t, bv = vf[h % 2]
                    P.dma("sp", lambda e, h=h, q_t=q_t: e.dma_start(out=q_t[:], in_=QT[h * 128:(h + 1) * 128, 0:NP]), reads=[db("QT")], writes=[bq])
                    P.dma("sp", lambda e, h=h, k_t=k_t: e.dma_start(out=k_t[:], in_=KT[h * 128:(h + 1) * 128, 0:NP]), reads=[db("KT")], writes=[bk])
                    P.dma("sp", lambda e, h=h, v_t=v_t: e.dma_start(out=v_t[:, 0:nfull, :], in_=v_all[0:nfull * 128, h * 128:(h + 1) * 128].rearrange("(t p) d -> p t d", p=128)),
                          reads=[db("v_all")], writes=[bv])
                    if rem:
                        P.dma("sp", lambda e, h=h, v_t=v_t: e.dma_start(out=v_t[:rem, nfull, :], in_=v_all[nfull * 128:NP, h * 128:(h + 1) * 128]), reads=[db("v_all")], writes=[bv])
                    P.op("pool", lambda e, q_t=q_t: e.tensor_copy(out=qb[:], in_=q_t[:]), reads=[bq], writes=[bqb])
                    P.op("pool", lambda e, k_t=k_t: e.tensor_copy(out=kb_[:], in_=k_t[:]), reads=[bk], writes=[bkb])
                    P.op("pool", lambda e, v_t=v_t: e.tensor_copy(out=vbb[:], in_=v_t[:]), reads=[bv], writes=[bvbb])
                    for (q0, nq) in PBLOCKS:
                        kb_last = (q0 + nq - 1) // 128
                        o_t, bo = po[och % 2]
                        och += 1
                        P.op("pool", lambda e: e.memset(spsum[:], 0.0), writes=[bsps])
                        for kb in range(kb_last, -1, -1):
                            nk = min(128, NP - kb * 128)
                            z_t, bz = pz[cnt % 2]
                            T_t, bT = pT[cnt % 2]
                            e_t, be = esb[cnt % 2]
                            d_t, bd = t1[cnt % 2]
                            a_t, ba = Ab[cnt % 2]
                            cnt += 1
                            first = kb == kb_last
                            diag = kb * 128 + nk > q0
                            r = kb - q0 // 128
                            P.op("pe", lambda e, kb=kb, nk=nk, q0=q0, nq=nq, z_t=z_t: e.matmul(z_t[:nk, :nq], lhsT=kb_[:, kb * 128:kb * 128 + nk], rhs=qb[:, q0:q0 + nq], start=True, stop=True),
                                 reads=[bkb, bqb], writes=[bz])
                            if nk < 128:
                                P.op("dve", lambda e, e_t=e_t: e.memset(e_t[:], 0.0), writes=[be])
                            P.op("act", lambda e, h=h, nk=nk, nq=nq, z_t=z_t, e_t=e_t: e.activation(out=e_t[:nk, :nq], in_=z_t[:nk, :nq], func=AF.Exp, bias=biasb[:nk, h:h + 1], scale=1.0),
                                 reads=[bz, bbias], writes=[be])
                            P.op("act", lambda e, nk=nk, nq=nq, e_t=e_t: e.activation(out=e_t[:nk, :nq], in_=e_t[:nk, :nq], func=AF.Ln, bias=1.0, scale=1.0), reads=[be], writes=[be])
                            if diag:
                                m_t, bm = masks[r]
                                P.op("dve", lambda e, nk=nk, nq=nq, e_t=e_t, m_t=m_t: e.tensor_tensor(out=e_t[:nk, :nq], in0=e_t[:nk, :nq], in1=m_t[:nk, :nq], op=ALU.mult),
                                     reads=[be, bm], writes=[be])
                            P.op("pe", lambda e, nk=nk, nq=nq, T_t=T_t, e_t=e_t, first=first: e.matmul(T_t[:nk, :nq], lhsT=ustr[:, :nk], rhs=e_t[:, :nq], start=True, stop=first),
                                 reads=[bustr, be], writes=[bT])
                            if not first:
                                P.op("pe", lambda e, nk=nk, nq=nq, T_t=T_t: e.matmul(T_t[:nk, :nq], lhsT=ones_f[:, :nk], rhs=spsum[:, :nq], start=False, stop=True),
                                     reads=[b_ones_f, bsps], writes=[bT])
                            P.op("dve", lambda e, h=h, nk=nk, nq=nq, z_t=z_t, e_t=e_t, d_t=d_t: e.scalar_tensor_tensor(out=d_t[:nk, :nq], in0=z_t[:nk, :nq], scalar=biasb[:nk, h:h + 1], in1=e_t[:nk, :nq],
                                                                                                              op0=ALU.add, op1=ALU.subtract), reads=[bz, bbias, be], writes=[bd])
                            P.op("dve", lambda e, nk=nk, nq=nq, T_t=T_t, d_t=d_t: e.tensor_tensor(out=d_t[:nk, :nq], in0=d_t[:nk, :nq], in1=T_t[:nk, :nq], op=ALU.subtract), reads=[bd, bT], writes=[bd])
                            if diag:
                                P.op("act", lambda e, nk=nk, nq=nq, d_t=d_t: e.activation(out=d_t[:nk, :nq], in_=d_t[:nk, :nq], func=AF.Exp), reads=[bd], writes=[bd])
                                P.op("dve", lambda e, nk=nk, nq=nq, d_t=d_t, a_t=a_t, m_t=m_t: e.tensor_tensor(out=a_t[:nk, :nq], in0=d_t[:nk, :nq], in1=m_t[:nk, :nq], op=ALU.mult),
                                     reads=[bd, bm], writes=[ba])
                            else:
                                P.op("act", lambda e, nk=nk, nq=nq, d_t=d_t, a_t=a_t: e.activation(out=a_t[:nk, :nq], in_=d_t[:nk, :nq], func=AF.Exp), reads=[bd], writes=[ba])
                            if kb > 0:
                                P.op("pool", lambda e, nq=nq, e_t=e_t: e.tensor_tensor(out=spsum[:, :nq], in0=spsum[:, :nq], in1=e_t[:, :nq], op=ALU.add), reads=[bsps, be], writes=[bsps])
                            P.op("pe", lambda e, kb=kb, nk=nk, nq=nq, o_t=o_t, a_t=a_t, first=first: e.matmul(o_t[:, :nq], lhsT=vbb[:nk, kb, :], rhs=a_t[:nk, :nq], start=first, stop=(kb == 0)),
                                 reads=[bvbb, ba], writes=[bo])
                        P.op("act", lambda e, h=h, q0=q0, nq=nq, o_t=o_t: e.activation(out=OT[:, h, q0:q0 + nq], in_=o_t[:, :nq], func=AF.Copy), reads=[bo], writes=[bOT])
            if stop_after != "Lb":
                layer1_sample(ph, OT, bOT)
            with Phase(P, "Ld") as pd:
                proj_tok(pd, OT, bOT, KC, w_out_o, 0, D, resid_sink(pd, Xsrc, bXsrc, Xdst, bXdst))

    layer0_inproj()
    if stop_after == "A":
        G.__exit__(None, None, None)
        P.emit()
        return nc
    layer0_gla()
    if stop_after == "B":
        G.__exit__(None, None, None)
        P.emit()
        return nc
    layer0_conv_out()
    if stop_after == "D":
        G.__exit__(None, None, None)
        P.emit()
        return nc
    peer_layer(0, X1, db("X1"), X2, db("X2"), "0")
    if stop_after == "E":
        G.__exit__(None, None, None)
        P.emit()
        return nc
    layer1(X2, db("X2"), X3, db("X3"))
    if stop_after in ("Lb", "L"):
        G.__exit__(None, None, None)
        P.emit()
        return nc
    peer_layer(1, X3, db("X3"), X4, db("X4"), "1")
    with Phase(P, "Z") as ph:
        gain, bgain = gain_tile(ph, "gain", norm_final[0:1, :])
        xz = [ph.sb("x%d" % i, [128, D]) for i in range(2)]
        yz = [ph.sb("y%d" % i, [128, D]) for i in range(2)]
        junk, bjunk = ph.sb("junk", [128, D])
        ss, bss = ph.sb("ss", [128, 4])
        for ti, (r0, n) in enumerate(TILES):
            x_t, bx = xz[ti % 2]
            y_t, by = yz[ti % 2]
            P.dma("sp", lambda e, r0=r0, n=n, x_t=x_t: e.dma_start(out=x_t[:n], in_=X4[r0:r0 + n, :]), reads=[db("X4")], writes=[bx])
            rmsnorm_tile(ph, x_t, bx, n, gain, bgain, y_t, by, (junk, bjunk, ss, bss))
            P.dma("sp", lambda e, r0=r0, n=n, y_t=y_t: e.dma_start(out=y_out[r0:r0 + n, :], in_=y_t[:n]), reads=[by], writes=[db("y_out")])

    G.__exit__(None, None, None)
    P.emit()
    return nc


def make_in_maps(inp, ncores=8, SEQ=2048):
    f32 = np.float32
    a = lambda v: np.ascontiguousarray(np.asarray(v))
    npool = inp["cache_k"].shape[1]
    shared = {
        "cache_k": a(inp["cache_k"][0]).reshape(npool * 128, D),
        "cache_v": a(inp["cache_v"][0]).reshape(npool * 128, D),
        "norm_mix": a(inp["norm_mix"]),
        "norm_ffn": a(inp["norm_ffn"]),
        "norm_final": a(inp["norm_final"]).reshape(1, D),
        "w_in": a(inp["w_in_even"][0]),
        "w_lr": a(inp["w_gate_lr"][0]),
        "b_lr": a(inp["b_gate_lr"][0]).reshape(1, 512),
        "gla_norm": a(inp["gla_norm"][0]).reshape(1, 256),
        "conv_w": a(inp["conv_w"][0]),
        "conv_vec": a(np.concatenate([np.asarray(inp["conv_b"][0]).reshape(8, 128), np.asarray(inp["conv_norm_g"][0]).reshape(8, 128),
                                      np.asarray(inp["conv_norm_b"][0]).reshape(8, 128)], 0)),
        "w_out_e": a(inp["w_out_even"][0]),
        "w_qkv": a(inp["w_qkv_odd"][0]),
        "w_out_o": a(inp["w_out_odd"][0]),
        "sb_bias": a(inp["sb_bias"][0]).reshape(1, 16),
        "peer_wq": a(inp["peer_wq"]),
        "peer_keys": a(inp["peer_keys"]).reshape(2, 16, 128, 128),
        "peer_u": a(inp["peer_u"]),
        "peer_v": a(inp["peer_v"]),
    }
    maps = []
    for c in range(ncores):
        b = c % 4
        m = dict(shared)
        m["xin"] = a(np.concatenate([np.asarray(inp["meta_tokens"]), np.asarray(inp["x_prompt"][b]), np.asarray(inp["x_sample"][c])], 0).astype(f32))
        m["sgla"] = a(inp["state_gla"][0, c])
        m["sconv"] = a(inp["state_conv"][0, c])
        m["ptab"] = a(inp["page_table"][c:c + 1]).astype(np.int32)
        maps.append(m)
    return maps


_CACHE = {}


def kernel(**inputs):
    inp = {k: np.asarray(v) for k, v in inputs.items()}
    SEQ = inp["x_prompt"].shape[1]
    NPAGES = inp["page_table"].shape[1]
    NPOOL = inp["cache_k"].shape[1]
    NB = inp["x_prompt"].shape[0]
    NSB = inp["x_sample"].shape[0]
    NP = N_META + SEQ
    nc = build(SEQ=SEQ, NPAGES=NPAGES, NPOOL=NPOOL)
    in_maps = make_in_maps(inp, ncores=8, SEQ=SEQ)
    res = run_bass_kernel_spmd(nc, in_maps, core_ids=list(range(8)))
    r = res.results
    f32 = np.float32
    y_prompt = np.stack([r[b]["y_out"][N_META:NP] for b in range(NB)]).astype(f32)
    y_sample = np.stack([r[c]["y_out"][NP:NP + 8] for c in range(NSB)]).astype(f32)
    gla_prompt = np.stack([r[b]["gla_p"] for b in range(NB)])[None].astype(f32)
    gla_sample = np.stack([r[c]["gla_s"] for c in range(NSB)])[None].astype(f32)
    conv_prompt = np.stack([r[b]["conv_p"] for b in range(NB)])[None].astype(f32)
    conv_sample = np.stack([r[c]["conv_s"] for c in range(NSB)])[None].astype(f32)
    k_prompt = np.stack([r[b]["k_all"][:NP].reshape(NP, 16, 128) for b in range(NB)])[None].astype(f32)
    v_prompt = np.stack([r[b]["v_all"][:NP].reshape(NP, 16, 128) for b in range(NB)])[None].astype(f32)
    k_sample = np.stack([r[c]["k_all"][NP:NP + 8].reshape(8, 16, 128) for c in range(NSB)])[None].astype(f32)
    v_sample = np.stack([r[c]["v_all"][NP:NP + 8].reshape(8, 16, 128) for c in range(NSB)])[None].astype(f32)
    return (y_prompt, y_sample, gla_prompt, gla_sample, conv_prompt, conv_sample, k_prompt, v_prompt, k_sample, v_sample)
```

```python
import contextlib
import numpy as np
import concourse.bass as bass
import concourse.mybir as mybir
from concourse.bass_utils import run_bass_kernel_spmd

F32 = mybir.dt.float32
BF16 = mybir.dt.bfloat16
I32 = mybir.dt.int32
U32 = mybir.dt.uint32
ALU = mybir.AluOpType
AF = mybir.ActivationFunctionType
AX = mybir.AxisListType

D = 2048
KC = 16
N_META = 16
EPS = 1e-6
NDMASEM = 8
import os as _os
GLA_STAGE = int(_os.environ.get('GLA_STAGE', 9))
GLA_SUB = int(_os.environ.get('GLA_SUB', 9))


class Buf:
    __slots__ = ("name", "w", "r", "x", "multi", "mw")

    def __init__(self, name, init=None, x=False, multi=False):
        self.name = name
        self.w = None
        self.r = dict(init) if init else {}
        self.multi = multi
        self.mw = {}
        self.x = x


class Prog:
    ENGS = ("pe", "act", "dve", "pool", "sp")

    def __init__(self, nc):
        self.nc = nc
        self.ops = {e: [] for e in self.ENGS}
        self.cnt = {e: 0 for e in self.ENGS}
        self.dma_n = {e: 0 for e in self.ENGS}
        self.dma_last = {}
        self.waited = {e: {} for e in self.ENGS}
        self.barrier = {}

    def _deps(self, eng, reads, writes):
        deps = {}

        def add(k, v):
            if deps.get(k, 0) < v:
                deps[k] = v
        for b in reads:
            if b.multi:
                for k, v in b.mw.items():
                    add(k, v)
            elif b.w is not None:
                add(*b.w)
        for b in writes:
            if not b.multi and b.w is not None:
                add(*b.w)
            for k, v in b.r.items():
                add(k, v)
        out = []
        wd = self.waited[eng]
        for k, v in deps.items():
            if eng == "pe" and k == ("e", "pe"):
                continue
            if wd.get(k, 0) >= v:
                continue
            wd[k] = v
            out.append((k, v))
        return out

    def _commit(self, ev, reads, writes):
        k, v = ev
        for b in reads:
            if b.r.get(k, 0) < v:
                b.r[k] = v
        for b in writes:
            if b.multi:
                if b.mw.get(k, 0) < v:
                    b.mw[k] = v
            else:
                b.w = ev
                b.r = {}

    def op(self, eng, fn, reads=(), writes=()):
        xr = [b for b in reads if b.x]
        if xr:
            reads = [b for b in reads if not b.x]
            writes = list(writes) + [b for b in xr if b not in writes]
        waits = self._deps(eng, reads, writes)
        self.cnt[eng] += 1
        ev = (("e", eng), self.cnt[eng])
        self.ops[eng].append((waits, fn, ev, 1))
        self._commit(ev, reads, writes)

    def dma(self, eng, fn, reads=(), writes=()):
        n = self.dma_n[eng]
        k = ("d", eng, n % NDMASEM)
        waits = self._deps(eng, reads, writes)
        prev = self.dma_last.get(k)
        if prev is not None and self.waited[eng].get(k, 0) < prev:
            self.waited[eng][k] = prev
            waits.append((k, prev))
        val = (prev or 0) + 16
        self.dma_last[k] = val
        self.dma_n[eng] = n + 1
        ev = (k, val)
        self.ops[eng].append((waits, fn, ev, 16))
        self._commit(ev, reads, writes)

    def release(self, bufs):
        for b in bufs:
            evs = list(b.r.items())
            if b.w is not None:
                evs.append(b.w)
            for k, v in evs:
                if self.barrier.get(k, 0) < v:
                    self.barrier[k] = v

    def emit(self):
        nc = self.nc
        final_events = list(self.dma_last.items())
        with contextlib.ExitStack() as st:
            sems = {}
            for e in self.ENGS:
                sems[("e", e)] = st.enter_context(nc.semaphore("s_" + e))
                for s in range(NDMASEM):
                    sems[("d", e, s)] = st.enter_context(nc.semaphore("d_%s_%d" % (e, s)))
            block = st.enter_context(nc.Block())
            engmap = {"pe": "tensor", "act": "scalar", "dve": "vector", "pool": "gpsimd", "sp": "sync"}

            def make(ename):
                oplist = self.ops[ename]

                def body(eng):
                    for waits, fn, ev, inc in oplist:
                        for k, v in waits:
                            eng.wait_ge(sems[k], v)
                        fn(eng).then_inc(sems[ev[0]], inc)
                    if ename == "sp":
                        for k, v in final_events:
                            eng.wait_ge(sems[k], v)
                return body
            for ename in self.ENGS:
                getattr(block, engmap[ename])(make(ename))


class Phase:
    def __init__(self, P, name):
        self.P = P
        self.nc = P.nc
        self.name = name
        self.st = contextlib.ExitStack()
        self.bufs = []
        self.k = 0

    def __enter__(self):
        self.st.__enter__()
        return self

    def __exit__(self, *a):
        self.P.release(self.bufs)
        return self.st.__exit__(*a)

    def buf(self, name="b", x=False):
        b = Buf(name, self.P.barrier, x)
        self.bufs.append(b)
        return b

    def sb(self, name, shape, dt=F32, nb=None):
        t = self.st.enter_context(self.nc.sbuf_tensor("%s_%s" % (self.name, name), list(shape), dt))
        if nb is None:
            return t, self.buf(name)
        return t, [self.buf(name + str(i)) for i in range(nb)]

    def ps(self, name, shape, dt=F32):
        t = self.st.enter_context(self.nc.psum_tensor("%s_%s" % (self.name, name), list(shape), dt))
        return t, self.buf(name, x=True)


def tiles_of(NP, NS):
    tl = []
    r = 0
    while r + 128 <= NP:
        tl.append((r, 128))
        r += 128
    tl.append((r, NP - r + NS))
    return tl


def blocks_of(NP, NS, bs=512):
    bl = []
    r = 0
    while r < NP:
        n = min(bs, NP - r)
        bl.append((r, n))
        r += n
    bl.append((NP, NS))
    return bl


def build(SEQ=2048, NPAGES=128, NPOOL=1280, debug=False, stop_after=None, NEXP=16384):
    nc = bass.Bass("TRN2", target_bir_lowering=False)
    NP = N_META + SEQ
    NS = 8
    NT = NP + NS
    TILES = tiles_of(NP, NS)
    BLOCKS = blocks_of(NP, NS)
    NTILE = len(TILES)

    def din(name, shape, dt=F32):
        return nc.dram_tensor(name, list(shape), dt, kind="ExternalInput").ap()

    def dout(name, shape, dt=F32):
        return nc.dram_tensor(name, list(shape), dt, kind="ExternalOutput").ap()

    def dscr(name, shape, dt=F32):
        if debug:
            return nc.dram_tensor(name, list(shape), dt, kind="ExternalOutput").ap()
        return nc.dram_tensor(name, list(shape), dt).ap()

    xin = din("xin", [NT, D])
    sgla = din("sgla", [4, 128, 256])
    sconv = din("sconv", [30, 1024])
    cache_k = din("cache_k", [NPOOL * 128, D])
    cache_v = din("cache_v", [NPOOL * 128, D])
    ptab = din("ptab", [1, NPAGES], I32)
    norm_mix = din("norm_mix", [2, D])
    norm_ffn = din("norm_ffn", [2, D])
    norm_final = din("norm_final", [1, D])
    w_in = din("w_in", [D, 5136])
    w_lr = din("w_lr", [16, 512])
    b_lr = din("b_lr", [1, 512])
    gla_norm = din("gla_norm", [1, 256])
    conv_w = din("conv_w", [31, 1024])
    conv_vec = din("conv_vec", [24, 128])
    w_out_e = din("w_out_e", [D, D])
    w_qkv = din("w_qkv", [D, 3 * D])
    w_out_o = din("w_out_o", [D, D])
    sb_bias = din("sb_bias", [1, 16])
    peer_wq = din("peer_wq", [2, D, D])
    peer_keys = din("peer_keys", [2, 16, 128, 128])
    peer_u = din("peer_u", [2, NEXP, D])
    peer_v = din("peer_v", [2, NEXP, D])
    y_out = dout("y_out", [NT, D])
    gla_p = dout("gla_p", [4, 128, 256])
    gla_s = dout("gla_s", [4, 128, 256])
    conv_p = dout("conv_p", [30, 1024])
    conv_s = dout("conv_s", [30, 1024])
    k_all = dout("k_all", [NT, D])
    v_all = dout("v_all", [NT, D])
    PT = dscr("PT", [NT, 3088])
    UT = dscr("UT", [1024, 30 + NP])
    UTS = dscr("UTS", [1024, 30 + NS])
    OA = dscr("OA", [NT, 1024])
    X1 = dscr("X1", [NT, D])
    X2 = dscr("X2", [NT, D])
    X3 = dscr("X3", [NT, D])
    X4 = dscr("X4", [NT, D])
    HF = dscr("HF", [NT, D])
    QP = dscr("QP", [NT, D])
    QT = dscr("QT", [D, NT])
    KT = dscr("KT", [D, NT])

    global LASTP
    UVB = nc.dram_tensor("UVB16", [2 * NEXP, 2 * D], BF16).ap()

    P = Prog(nc)
    LASTP = P
    DB = {}

    def db(name):
        if name not in DB:
            DB[name] = Buf(name, multi=True)
        return DB[name]

    G = Phase(P, "g")
    G.__enter__()
    ident_f, b_ident_f = G.sb("ident_f", [128, 128])
    ident_b, b_ident_b = G.sb("ident_b", [128, 128], BF16)
    ones_f, b_ones_f = G.sb("ones_f", [128, 128])
    b_const = [b_ident_f, b_ident_b, b_ones_f]
    P.op("pool", lambda e: e.memset(ident_f[:], 0.0), writes=[b_ident_f])
    P.op("pool", lambda e: e.affine_select(out=ident_f[:], in_=ident_f[:], pattern=[[-1, 128]], compare_op=ALU.not_equal,
                                           fill=1.0, base=0, channel_multiplier=1), reads=[b_ident_f], writes=[b_ident_f])
    P.op("pool", lambda e: e.tensor_copy(out=ident_b[:], in_=ident_f[:]), reads=[b_ident_f], writes=[b_ident_b])
    P.op("pool", lambda e: e.memset(ones_f[:], 1.0), writes=[b_ones_f])
    eps_t, b_eps = G.sb("eps_t", [128, 1])
    P.op("pool", lambda e: e.memset(eps_t[:], EPS), writes=[b_eps])

    bgf = [G.sb("bgf%d" % i, [128, D]) for i in range(2)]
    bgb = [G.sb("bgb%d" % i, [128, D], BF16) for i in range(2)]
    NBG = NEXP // 128

    def bg_gen():
        i = 0
        for L in range(2):
            for tab, src in ((0, peer_u), (1, peer_v)):
                for t in range(NBG):
                    f_t, bf = bgf[i % 2]
                    b_t, bb = bgb[i % 2]
                    i += 1
                    P.dma("act", lambda e, L=L, t=t, src=src, f_t=f_t: e.dma_start(out=f_t[:], in_=src[L, t * 128:(t + 1) * 128, :]), writes=[bf])
                    if i % 2:
                        P.op("dve", lambda e, f_t=f_t, b_t=b_t: e.tensor_copy(out=b_t[:], in_=f_t[:]), reads=[bf], writes=[bb])
                    else:
                        P.op("act", lambda e, f_t=f_t, b_t=b_t: e.activation(out=b_t[:], in_=f_t[:], func=AF.Copy), reads=[bf], writes=[bb])
                    P.dma("act", lambda e, L=L, t=t, tab=tab, b_t=b_t: e.dma_start(out=UVB[L * NEXP + t * 128:L * NEXP + (t + 1) * 128, tab * D:(tab + 1) * D], in_=b_t[:]),
                          reads=[bb], writes=[db("UVB%d" % L)])
                    yield L
    bg_state = {"it": bg_gen(), "done": [0, 0]}

    def bg_step(k=1):
        for _ in range(k):
            try:
                L = next(bg_state["it"])
                bg_state["done"][L] += 1
            except StopIteration:
                return

    def bg_flush(L):
        while bg_state["done"][L] < 2 * NBG:
            bg_step()

    def gain_tile(ph, name, src_row):
        t, b = ph.sb(name, [128, D])
        P.dma("sp", lambda e: e.dma_start(out=t[:], in_=src_row.to_broadcast([128, D])), writes=[b])
        return t, b

    def rmsnorm_tile(ph, xt, bx, n, gain, bgain, out_t, bout, scr):
        junk, bjunk, ss, bss = scr
        P.op("dve", lambda e: e.memset(ss[:n], 0.0), writes=[bss])
        P.op("act", lambda e: e.activation(out=junk[:n], in_=xt[:n], func=AF.Square, accum_out=ss[:n, 0:1]),
             reads=[bx, bss], writes=[bjunk, bss])
        P.op("act", lambda e: e.activation(out=ss[:n, 1:2], in_=ss[:n, 0:1], func=AF.Sqrt, bias=eps_t[:n, 0:1], scale=1.0 / D),
             reads=[bss, b_eps], writes=[bss])
        P.op("dve", lambda e: e.reciprocal(out=ss[:n, 2:3], in_=ss[:n, 1:2]), reads=[bss], writes=[bss])
        P.op("dve", lambda e: e.scalar_tensor_tensor(out=out_t[:n], in0=xt[:n], scalar=ss[:n, 2:3], in1=gain[:n],
                                                     op0=ALU.mult, op1=ALU.mult), reads=[bx, bss, bgain], writes=[bout])

    def transpose_to_fm(src, bsrc, n, nk, dst, bdst, k0, c0, pst, bpst, dt_is_bf=True, evac="act"):
        idt = ident_b if dt_is_bf else ident_f
        bid = b_ident_b if dt_is_bf else b_ident_f
        for g0 in range(0, nk, 4):
            gn = min(4, nk - g0)
            for j in range(gn):
                k = g0 + j
                P.op("pe", lambda e, k=k, j=j: e.transpose(out=pst[:, j, :n], in_=src[:n, k * 128:(k + 1) * 128], identity=idt[:n, :n]),
                     reads=[bsrc, bid], writes=[bpst])
            if evac == "act":
                P.op("act", lambda e, g0=g0, gn=gn: e.activation(out=dst[:, k0 + g0:k0 + g0 + gn, c0:c0 + n], in_=pst[:, 0:gn, :n], func=AF.Copy),
                     reads=[bpst], writes=[bdst])
            else:
                P.op("dve", lambda e, g0=g0, gn=gn: e.tensor_copy(out=dst[:, k0 + g0:k0 + g0 + gn, c0:c0 + n], in_=pst[:, 0:gn, :n]),
                     reads=[bpst], writes=[bdst])

    def norm_phase(name, xsrc, bxsrc, gain_row, AT, bAT, hf_dst=None, bhf=None):
        with Phase(P, name) as ph:
            gain, bgain = gain_tile(ph, "gain", gain_row)
            xt = [ph.sb("x%d" % i, [128, D]) for i in range(2)]
            hb = [ph.sb("hb%d" % i, [128, D], BF16) for i in range(2)]
            hf = [ph.sb("hf%d" % i, [128, D]) for i in range(2)] if hf_dst is not None else None
            junk, bjunk = ph.sb("junk", [128, D])
            ss, bss = ph.sb("ss", [128, 4])
            pst = [ph.ps("pst%d" % i, [128, 4, 128], BF16) for i in range(2)]
            for ti, (r0, n) in enumerate(TILES):
                x_t, bx = xt[ti % 2]
                h_t, bh = hb[ti % 2]
                P.dma("sp", lambda e, r0=r0, n=n, x_t=x_t: e.dma_start(out=x_t[:n], in_=xsrc[r0:r0 + n, :]), reads=[bxsrc], writes=[bx])
                if hf_dst is not None:
                    f_t, bf = hf[ti % 2]
                    rmsnorm_tile(ph, x_t, bx, n, gain, bgain, f_t, bf, (junk, bjunk, ss, bss))
                    P.dma("sp", lambda e, r0=r0, n=n, f_t=f_t: e.dma_start(out=hf_dst[r0:r0 + n, :], in_=f_t[:n]), reads=[bf], writes=[bhf])
                    P.op("pool", lambda e, n=n, f_t=f_t, h_t=h_t: e.tensor_copy(out=h_t[:n], in_=f_t[:n]), reads=[bf], writes=[bh])
                else:
                    rmsnorm_tile(ph, x_t, bx, n, gain, bgain, h_t, bh, (junk, bjunk, ss, bss))
                p_t, bp = pst[ti % 2]
                transpose_to_fm(h_t, bh, n, KC, AT, bAT, 0, r0, p_t, bp, evac=("act" if ti % 2 else "dve"))

    def proj_tok(ph, AT, bAT, nk, W, col0, ncols, sink, CB=256, wscale=None):
        wf = [ph.sb("wf%d" % i, [128, nk, CB]) for i in range(2)]
        wb = [ph.sb("wb%d" % i, [128, nk, CB], BF16) for i in range(2)]
        pp = [ph.ps("pp%d" % i, [128, 512]) for i in range(4)]
        cnt = 0
        for bi, c0 in enumerate(range(col0, col0 + ncols, CB)):
            cn = min(CB, col0 + ncols - c0)
            wf_t, bwf = wf[bi % 2]
            wb_t, bwb = wb[bi % 2]
            P.dma("sp", lambda e, c0=c0, cn=cn, wf_t=wf_t: e.dma_start(out=wf_t[:, :, :cn], in_=W[:, c0:c0 + cn].rearrange("(k p) c -> p k c", p=128)),
                  writes=[bwf])
            hk = nk // 2
            P.op("dve", lambda e, cn=cn, wf_t=wf_t, wb_t=wb_t: e.tensor_copy(out=wb_t[:, :hk, :cn], in_=wf_t[:, :hk, :cn]), reads=[bwf], writes=[bwb])
            P.op("act", lambda e, cn=cn, wf_t=wf_t, wb_t=wb_t: e.activation(out=wb_t[:, hk:, :cn], in_=wf_t[:, hk:, :cn], func=AF.Copy), reads=[bwf], writes=[bwb])
            for ti, (r0, n) in enumerate(TILES):
                p_t, bp = pp[cnt % 4]
                cnt += 1
                for k in range(nk):
                    P.op("pe", lambda e, k=k, r0=r0, n=n, cn=cn, p_t=p_t, wb_t=wb_t: e.matmul(p_t[:n, :cn], lhsT=AT[:, k, r0:r0 + n], rhs=wb_t[:, k, :cn],
                                                                                         start=(k == 0), stop=(k == nk - 1)),
                         reads=[bAT, bwb], writes=[bp])
                sink(ti, r0, n, c0 - col0, cn, p_t, bp)
                bg_step()

    def proj_ch(ph, AT, bAT, nk, W, cols, sink, tag=""):
        ng = len(cols[0])
        wf = [ph.sb("cwf%s%d" % (tag, i), [128, nk, ng * 128]) for i in range(2)]
        wb = [ph.sb("cwb%s%d" % (tag, i), [128, nk, ng * 128], BF16) for i in range(2)]
        pp = [ph.ps("cpp%s%d" % (tag, i), [128, 512]) for i in range(2 * ng)]
        cnt = 0
        for gi, grp in enumerate(cols):
            wf_t, bwf = wf[gi % 2]
            wb_t, bwb = wb[gi % 2]
            for j, c0 in enumerate(grp):
                P.dma("sp", lambda e, c0=c0, j=j, wf_t=wf_t: e.dma_start(out=wf_t[:, :, j * 128:(j + 1) * 128],
                                                                       in_=W[:, c0:c0 + 128].rearrange("(k p) c -> p k c", p=128)), writes=[bwf])
            hk = nk // 2
            P.op("dve", lambda e, wf_t=wf_t, wb_t=wb_t: e.tensor_copy(out=wb_t[:, :hk, :], in_=wf_t[:, :hk, :]), reads=[bwf], writes=[bwb])
            P.op("act", lambda e, wf_t=wf_t, wb_t=wb_t: e.activation(out=wb_t[:, hk:, :], in_=wf_t[:, hk:, :], func=AF.Copy), reads=[bwf], writes=[bwb])
            for bi, (t0, nt) in enumerate(BLOCKS):
                pts = []
                for j in range(ng):
                    p_t, bp = pp[(cnt % 2) * ng + j]
                    for k in range(nk):
                        P.op("pe", lambda e, k=k, j=j, t0=t0, nt=nt, p_t=p_t, wb_t=wb_t: e.matmul(p_t[:, :nt], lhsT=wb_t[:, k, j * 128:(j + 1) * 128],
                                                                                             rhs=AT[:, k, t0:t0 + nt], start=(k == 0), stop=(k == nk - 1)),
                             reads=[bAT, bwb], writes=[bp])
                    pts.append((p_t, bp))
                cnt += 1
                sink(gi, bi, t0, nt, pts)
                bg_step()

    def store_sink(ph, dst, bdst, colbase, scale=None):
        stg = [ph.sb("stg%d" % i, [128, 256]) for i in range(4)]
        state = {"i": 0}

        def sink(ti, r0, n, c0, cn, p_t, bp):
            s_t, bs = stg[state["i"] % 4]
            use_act = state["i"] % 2 == 0
            state["i"] += 1
            if use_act:
                if scale is None:
                    P.op("act", lambda e: e.activation(out=s_t[:n, :cn], in_=p_t[:n, :cn], func=AF.Copy), reads=[bp], writes=[bs])
                else:
                    P.op("act", lambda e: e.activation(out=s_t[:n, :cn], in_=p_t[:n, :cn], func=AF.Copy, scale=scale), reads=[bp], writes=[bs])
            else:
                if scale is None:
                    P.op("dve", lambda e: e.tensor_copy(out=s_t[:n, :cn], in_=p_t[:n, :cn]), reads=[bp], writes=[bs])
                else:
                    P.op("dve", lambda e: e.tensor_scalar(out=s_t[:n, :cn], in0=p_t[:n, :cn], scalar1=scale, scalar2=None, op0=ALU.mult), reads=[bp], writes=[bs])
            P.dma("sp", lambda e: e.dma_start(out=dst[r0:r0 + n, colbase + c0:colbase + c0 + cn], in_=s_t[:n, :cn]), reads=[bs], writes=[bdst])
        return sink

    def resid_sink(ph, xsrc, bxsrc, dst, bdst):
        stg = [ph.sb("rs%d" % i, [128, 256]) for i in range(4)]
        xin_t = [ph.sb("rx%d" % i, [128, 256]) for i in range(4)]
        state = {"i": 0}

        def sink(ti, r0, n, c0, cn, p_t, bp):
            s_t, bs = stg[state["i"] % 4]
            x_t, bx = xin_t[state["i"] % 4]
            state["i"] += 1
            P.dma("sp", lambda e: e.dma_start(out=x_t[:n, :cn], in_=xsrc[r0:r0 + n, c0:c0 + cn]), reads=[bxsrc], writes=[bx])
            P.op("dve", lambda e: e.tensor_tensor(out=s_t[:n, :cn], in0=p_t[:n, :cn], in1=x_t[:n, :cn], op=ALU.add), reads=[bp, bx], writes=[bs])
            P.dma("sp", lambda e: e.dma_start(out=dst[r0:r0 + n, c0:c0 + cn], in_=s_t[:n, :cn]), reads=[bs], writes=[bdst])
        return sink

    def layer0_inproj():
        with Phase(P, "A") as ph:
            AT, bAT = ph.sb("AT", [128, KC, NT], BF16)
            norm_phase("A0", xin, db("xin"), norm_mix[0:1, :], AT, bAT)
            with Phase(P, "A1") as p1:
                proj_tok(p1, AT, bAT, KC, w_in, 0, 3088, store_sink(p1, PT, db("PT"), 0))
            with Phase(P, "A2") as p2:
                zt, bz = p2.sb("zt", [128, 30])
                P.op("dve", lambda e: e.memset(zt[:], 0.0), writes=[bz])
                for j in range(8):
                    P.dma("sp", lambda e, j=j: e.dma_start(out=UT[j * 128:(j + 1) * 128, 0:30], in_=zt[:]), reads=[bz], writes=[db("UT")])
                sc, bsc = p2.sb("sc", [30, 1024])
                P.dma("sp", lambda e: e.dma_start(out=sc[:], in_=sconv[:, :]), writes=[bsc])
                pst, bpst = p2.ps("pst", [128, 8, 32])
                sct, bsct = p2.sb("sct", [128, 8, 30])
                for j in range(8):
                    P.op("pe", lambda e, j=j: e.transpose(out=pst[:, j, :30], in_=sc[:30, j * 128:(j + 1) * 128], identity=ident_f[:30, :30]),
                         reads=[bsc, b_ident_f], writes=[bpst])
                P.op("dve", lambda e: e.tensor_copy(out=sct[:], in_=pst[:, :, :30]), reads=[bpst], writes=[bsct])
                P.dma("sp", lambda e: e.dma_start(out=UTS.rearrange("(j p) t -> p j t", p=128)[:, :, 0:30], in_=sct[:]), reads=[bsct], writes=[db("UTS")])
                sg = [p2.sb("sg%d" % i, [128, 512]) for i in range(2)]
                us = [p2.sb("us%d" % i, [128, 512]) for i in range(2)]
                state = {"i": 0}

                def sink(gi, bi, t0, nt, pts):
                    (pa, bpa), (pb, bpb) = pts
                    s_t, bs = sg[state["i"] % 2]
                    u_t, bu = us[state["i"] % 2]
                    state["i"] += 1
                    P.op("act", lambda e: e.activation(out=s_t[:, :nt], in_=pb[:, :nt], func=AF.Sigmoid), reads=[bpb], writes=[bs])
                    P.op("dve", lambda e: e.tensor_tensor(out=u_t[:, :nt], in0=pa[:, :nt], in1=s_t[:, :nt], op=ALU.mult), reads=[bpa, bs], writes=[bu])
                    if t0 < NP:
                        P.dma("sp", lambda e: e.dma_start(out=UT[gi * 128:(gi + 1) * 128, 30 + t0:30 + t0 + nt], in_=u_t[:, :nt]), reads=[bu], writes=[db("UT")])
                    else:
                        P.dma("sp", lambda e: e.dma_start(out=UTS[gi * 128:(gi + 1) * 128, 30:30 + nt], in_=u_t[:, :nt]), reads=[bu], writes=[db("UTS")])
                proj_ch(p2, AT, bAT, KC, w_in, [[3088 + 128 * j, 4112 + 128 * j] for j in range(8)], sink)

    def layer0_gla():
        with Phase(P, "B") as ph:
            C = 64
            SCALE = -1.0 / 16.0
            tri_s, btri = ph.sb("tri_s", [C, C])
            gtr_s, bgtr = ph.sb("gtr_s", [C, C])
            ones_s, bones = ph.sb("ones_s", [C, 8])
            tri01, btri01 = ph.sb("tri01", [C, C])
            P.op("pool", lambda e: e.memset(tri_s[:], SCALE), writes=[btri])
            P.op("pool", lambda e: e.affine_select(out=tri_s[:], in_=tri_s[:], pattern=[[1, C]], compare_op=ALU.is_ge, fill=0.0, base=0, channel_multiplier=-1),
                 reads=[btri], writes=[btri])
            P.op("pool", lambda e: e.memset(gtr_s[:], SCALE), writes=[bgtr])
            P.op("pool", lambda e: e.affine_select(out=gtr_s[:], in_=gtr_s[:], pattern=[[-1, C]], compare_op=ALU.is_ge, fill=0.0, base=-1, channel_multiplier=1),
                 reads=[bgtr], writes=[bgtr])
            P.op("pool", lambda e: e.memset(ones_s[:], SCALE), writes=[bones])
            P.op("pool", lambda e: e.memset(tri01[:], 1.0), writes=[btri01])
            P.op("pool", lambda e: e.affine_select(out=tri01[:], in_=tri01[:], pattern=[[1, C]], compare_op=ALU.is_ge, fill=0.0, base=0, channel_multiplier=-1),
                 reads=[btri01], writes=[btri01])
            wlr, bwlr = ph.sb("wlr", [32, 512])
            glag, bglag = ph.sb("glag", [C, 256])
            P.op("dve", lambda e: e.memset(wlr[:], 0.0), writes=[bwlr])
            P.dma("sp", lambda e: e.dma_start(out=wlr[0:16, :], in_=w_lr[:, :]), writes=[bwlr])
            P.dma("sp", lambda e: e.dma_start(out=wlr[16:17, :], in_=b_lr[:, :]), writes=[bwlr])
            P.dma("sp", lambda e: e.dma_start(out=glag[:], in_=gla_norm[0:1, :].to_broadcast([C, 256])), writes=[bglag])
            S, bS = ph.sb("S", [128, 4, 256])
            P.op("dve", lambda e: e.memset(S[:], 0.0), writes=[bS])
            qkv = [ph.sb("qkv%d" % i, [C, 2048]) for i in range(2)]
            gg = [ph.sb("gg%d" % i, [C, 1024]) for i in range(2)]
            lr = [ph.sb("lr%d" % i, [C, 32]) for i in range(2)]
            for l_t, bl in lr:
                P.op("dve", lambda e, l_t=l_t: e.memset(l_t[:], 0.0), writes=[bl])
                P.op("dve", lambda e, l_t=l_t: e.memset(l_t[:, 16:17], 1.0), writes=[bl])
            lrT, blrT = ph.sb("lrT", [32, C])
            lsp, blsp = ph.sb("lsp", [C, 512])
            EB, bEB = ph.sb("EB", [C, 512])
            ENB, bENB = ph.sb("ENB", [C, 512])
            ED, bED = ph.sb("ED", [C, 512])
            ebl, bebl = ph.sb("ebl", [128, 32])
            qtl, bqtl = ph.sb("qtl", [C, 512])
            ktl, bktl = ph.sb("ktl", [C, 512])
            kdc, bkdc = ph.sb("kdc", [C, 512])
            qT, bqT = ph.sb("qT", [128, 4, C])
            kT, bkT = ph.sb("kT", [128, 4, C])
            att, batt = ph.sb("att", [C, 4, C])
            osb, bosb = ph.sb("osb", [C, 4, 256])
            osq, bosq = ph.sb("osq", [C, 4, 256])
            rs, brs = ph.sb("rs", [C, 12])
            sgl, bsgl = ph.sb("sgl", [C, 1024])
            pA, bpA = ph.ps("pA", [128, 512])
            pB, bpB = ph.ps("pB", [128, 512])
            pC, bpC = ph.ps("pC", [128, 512])
            pD, bpD = ph.ps("pD", [128, 512])
            pO, bpO = ph.ps("pO", [128, 1024])
            pK, bpK = ph.ps("pK", [128, 1024])
            chunks = [(r, min(C, NP - r)) for r in range(0, NP, C)] + [(NP, NS)]
            bPT = db("PT")
            for (t_, b_) in qkv + [(lsp, blsp), (att, batt), (kdc, bkdc)]:
                P.op("pool", lambda e, t_=t_: e.memset(t_[:], 0.0), writes=[b_])
            for ci, (r0, n) in enumerate(chunks):
                q_t, bq = qkv[ci % 2]
                g_t, bg = gg[ci % 2]
                l_t, bl = lr[ci % 2]
                if n < C:
                    for (t_, b_) in [(lsp, blsp), (att, batt), (kdc, bkdc)]:
                        P.op("pool", lambda e, t_=t_: e.memset(t_[:], 0.0), writes=[b_])
                if r0 == NP:
                    P.dma("sp", lambda e: e.dma_start(out=gla_p.rearrange("h d v -> d h v"), in_=S[:]), reads=[bS], writes=[db("gla_p")])
                    P.dma("sp", lambda e: e.dma_start(out=S[:], in_=sgla.rearrange("h d v -> d h v")), writes=[bS])
                P.dma("sp", lambda e, r0=r0, n=n, q_t=q_t: e.dma_start(out=q_t[:n], in_=PT[r0:r0 + n, 0:2048]), reads=[bPT], writes=[bq])
                P.dma("sp", lambda e, r0=r0, n=n, g_t=g_t: e.dma_start(out=g_t[:n], in_=PT[r0:r0 + n, 2048:3072]), reads=[bPT], writes=[bg])
                P.dma("sp", lambda e, r0=r0, n=n, l_t=l_t: e.dma_start(out=l_t[:n, 0:16], in_=PT[r0:r0 + n, 3072:3088]), reads=[bPT], writes=[bl])
                P.op("pe", lambda e, n=n, l_t=l_t: e.transpose(out=pD[:32, :n], in_=l_t[:n, :32], identity=ident_f[:n, :n]), reads=[bl, b_ident_f], writes=[bpD])
                P.op("dve", lambda e, n=n: e.tensor_copy(out=lrT[:, :n], in_=pD[:32, :n]), reads=[bpD], writes=[blrT])
                P.op("pe", lambda e, n=n: e.matmul(pA[:n, :512], lhsT=lrT[:32, :n], rhs=wlr[:32, :], start=True, stop=True), reads=[blrT, bwlr], writes=[bpA])
                if GLA_STAGE < 2:
                    continue
                P.op("act", lambda e, n=n: e.activation(out=lsp[:n], in_=pA[:n, :512], func=AF.Exp, scale=-1.0), reads=[bpA], writes=[blsp])
                P.op("act", lambda e, n=n: e.activation(out=lsp[:n], in_=lsp[:n], func=AF.Ln, bias=1.0, scale=1.0), reads=[blsp], writes=[blsp])
                P.op("pe", lambda e, n=n: e.matmul(pB[:n, :512], lhsT=tri_s[:, :n], rhs=lsp[:, :], start=True, stop=True), reads=[btri, blsp], writes=[bpB])
                P.op("pe", lambda e, n=n: e.matmul(pC[:n, :512], lhsT=gtr_s[:, :n], rhs=lsp[:, :], start=True, stop=True), reads=[bgtr, blsp], writes=[bpC])
                for h in range(4):
                    P.op("pe", lambda e, n=n, h=h: e.matmul(pD[:, 64 + 8 * h:72 + 8 * h], lhsT=lsp[:, h * 128:(h + 1) * 128], rhs=ones_s[:, 0:8], start=True, stop=True),
                         reads=[blsp, bones], writes=[bpD])
                P.op("act", lambda e, n=n: e.activation(out=EB[:n], in_=pB[:n, :512], func=AF.Exp), reads=[bpB], writes=[bEB])
                P.op("act", lambda e, n=n: e.activation(out=ENB[:n], in_=pB[:n, :512], func=AF.Exp, scale=-1.0), reads=[bpB], writes=[bENB])
                P.op("act", lambda e, n=n: e.activation(out=ED[:n], in_=pC[:n, :512], func=AF.Exp), reads=[bpC], writes=[bED])
                P.op("act", lambda e: e.activation(out=ebl[:, :], in_=pD[:, 64:96], func=AF.Exp), reads=[bpD], writes=[bebl])
                P.op("dve", lambda e, n=n, q_t=q_t: e.scalar_tensor_tensor(out=qtl[:n], in0=q_t[:n, 0:512], scalar=128.0 ** -0.5, in1=EB[:n], op0=ALU.mult, op1=ALU.mult),
                     reads=[bq, bEB], writes=[bqtl])
                P.op("dve", lambda e, n=n, q_t=q_t: e.tensor_tensor(out=ktl[:n], in0=q_t[:n, 512:1024], in1=ENB[:n], op=ALU.mult), reads=[bq, bENB], writes=[bktl])
                P.op("dve", lambda e, n=n, q_t=q_t: e.tensor_tensor(out=kdc[:n], in0=q_t[:n, 512:1024], in1=ED[:n], op=ALU.mult), reads=[bq, bED], writes=[bkdc])
                if GLA_STAGE < 3:
                    continue
                for h in range(4):
                    P.op("pe", lambda e, n=n, h=h: e.transpose(out=pA[:, h * C:h * C + n], in_=qtl[:n, h * 128:(h + 1) * 128], identity=ident_f[:n, :n]),
                         reads=[bqtl, b_ident_f], writes=[bpA])
                for h in range(4):
                    P.op("pe", lambda e, n=n, h=h: e.transpose(out=pA[:, 256 + h * C:256 + h * C + n], in_=ktl[:n, h * 128:(h + 1) * 128], identity=ident_f[:n, :n]),
                         reads=[bktl, b_ident_f], writes=[bpA])
                if GLA_STAGE == 3 and GLA_SUB < 1:
                    continue
                P.op("dve", lambda e, n=n: e.tensor_copy(out=qT[:, :, :n], in_=pA[:, 0:256].rearrange("p (h c) -> p h c", h=4)[:, :, :n]), reads=[bpA], writes=[bqT])
                P.op("act", lambda e, n=n: e.activation(out=kT[:, :, :n], in_=pA[:, 256:512].rearrange("p (h c) -> p h c", h=4)[:, :, :n], func=AF.Copy), reads=[bpA], writes=[bkT])
                if GLA_STAGE == 3 and GLA_SUB < 2:
                    continue
                for h in range(4):
                    P.op("pe", lambda e, n=n, h=h: e.matmul(pB[:n, h * C:h * C + n], lhsT=kT[:, h, :n], rhs=qT[:, h, :n], start=True, stop=True),
                         reads=[bkT, bqT], writes=[bpB])
                if GLA_STAGE == 3 and GLA_SUB < 3:
                    continue
                P.op("dve", lambda e, n=n: e.tensor_tensor(out=att[:n, :, :n], in0=pB[:n, 0:256].rearrange("p (h c) -> p h c", h=4)[:, :, :n],
                                                           in1=tri01[:n, :n].unsqueeze(1).to_broadcast([n, 4, n]), op=ALU.mult), reads=[bpB, btri01], writes=[batt])
                if GLA_STAGE < 4:
                    continue
                for h in range(4):
                    P.op("pe", lambda e, n=n, h=h, q_t=q_t: e.matmul(pO[:n, h * 256:(h + 1) * 256], lhsT=att[:, h, :n], rhs=q_t[:, 1024 + h * 256:1024 + (h + 1) * 256],
                                                                  start=True, stop=False), reads=[batt, bq], writes=[bpO])
                    P.op("pe", lambda e, n=n, h=h: e.matmul(pO[:n, h * 256:(h + 1) * 256], lhsT=qT[:, h, :n], rhs=S[:, h, :], start=False, stop=True),
                         reads=[bqT, bS], writes=[bpO])
                for h in range(4):
                    P.op("pe", lambda e, n=n, h=h, q_t=q_t: e.matmul(pK[:, h * 256:(h + 1) * 256], lhsT=kdc[:, h * 128:(h + 1) * 128],
                                                                  rhs=q_t[:, 1024 + h * 256:1024 + (h + 1) * 256], start=True, stop=True), reads=[bkdc, bq], writes=[bpK])
                for h in range(4):
                    P.op("dve", lambda e, h=h: e.scalar_tensor_tensor(out=S[:, h, :], in0=S[:, h, :], scalar=ebl[:, 8 * h:8 * h + 1], in1=pK[:, h * 256:(h + 1) * 256],
                                                                      op0=ALU.mult, op1=ALU.add), reads=[bS, bebl, bpK], writes=[bS])
                if GLA_STAGE < 5:
                    continue
                P.op("act", lambda e, n=n: e.activation(out=osb[:n], in_=pO[:n, :].rearrange("p (h v) -> p h v", h=4), func=AF.Copy), reads=[bpO], writes=[bosb])
                P.op("dve", lambda e, n=n: e.tensor_tensor(out=osq[:n], in0=osb[:n], in1=osb[:n], op=ALU.mult), reads=[bosb], writes=[bosq])
                P.op("dve", lambda e, n=n: e.tensor_reduce(out=rs[:n, 0:4], in_=osq[:n], axis=AX.X, op=ALU.add), reads=[bosq], writes=[brs])
                P.op("act", lambda e, n=n: e.activation(out=rs[:n, 4:8], in_=rs[:n, 0:4], func=AF.Sqrt, bias=eps_t[:n, 0:1], scale=1.0 / 256), reads=[brs, b_eps], writes=[brs])
                P.op("dve", lambda e, n=n: e.reciprocal(out=rs[:n, 8:12], in_=rs[:n, 4:8]), reads=[brs], writes=[brs])
                P.op("act", lambda e, n=n, g_t=g_t: e.activation(out=sgl[:n], in_=g_t[:n], func=AF.Silu), reads=[bg], writes=[bsgl])
                P.op("dve", lambda e, n=n: e.tensor_tensor(out=osb[:n], in0=osb[:n], in1=rs[:n, 8:12].unsqueeze(2).to_broadcast([n, 4, 256]), op=ALU.mult),
                     reads=[bosb, brs], writes=[bosb])
                P.op("dve", lambda e, n=n: e.tensor_tensor(out=osb[:n], in0=osb[:n], in1=glag[:n, :].unsqueeze(1).to_broadcast([n, 4, 256]), op=ALU.mult),
                     reads=[bosb, bglag], writes=[bosb])
                P.op("dve", lambda e, n=n: e.tensor_tensor(out=osq[:n], in0=osb[:n], in1=sgl[:n].rearrange("p (h v) -> p h v", h=4), op=ALU.mult),
                     reads=[bosb, bsgl], writes=[bosq])
                P.dma("sp", lambda e, r0=r0, n=n: e.dma_start(out=OA[r0:r0 + n, :], in_=osq[:n].rearrange("p h v -> p (h v)")), reads=[bosq], writes=[db("OA")])
            P.dma("sp", lambda e: e.dma_start(out=gla_s.rearrange("h d v -> d h v"), in_=S[:]), reads=[bS], writes=[db("gla_s")])


    def layer0_conv_out():
        with Phase(P, "CD") as ph:
            OT, bOT = ph.sb("OT", [128, KC, NT], BF16)
            with Phase(P, "C") as pc:
                Cc, bC = pc.sb("Cc", [128, 8, NT])
                cwr, bcwr = pc.sb("cwr", [31, 1024])
                cwT, bcwT = pc.sb("cwT", [128, 8, 32])
                cvr, bcvr = pc.sb("cvr", [24, 128])
                cv, bcv = pc.sb("cv", [128, 24])
                pst, bpst = pc.ps("pst", [128, 512])
                P.dma("sp", lambda e: e.dma_start(out=cwr[:], in_=conv_w[:, :]), writes=[bcwr])
                P.dma("sp", lambda e: e.dma_start(out=cvr[:], in_=conv_vec[:, :]), writes=[bcvr])
                for j in range(8):
                    P.op("pe", lambda e, j=j: e.transpose(out=pst[:, j * 32:j * 32 + 31], in_=cwr[:31, j * 128:(j + 1) * 128], identity=ident_f[:31, :31]),
                         reads=[bcwr, b_ident_f], writes=[bpst])
                P.op("dve", lambda e: e.tensor_copy(out=cwT[:, :, 0:31], in_=pst[:, 0:256].rearrange("p (j k) -> p j k", j=8)[:, :, 0:31]), reads=[bpst], writes=[bcwT])
                P.op("pe", lambda e: e.transpose(out=pst[:, 256:280], in_=cvr[:24, :], identity=ident_f[:24, :24]), reads=[bcvr, b_ident_f], writes=[bpst])
                P.op("dve", lambda e: e.tensor_copy(out=cv[:], in_=pst[:, 256:280]), reads=[bpst], writes=[bcv])
                ut = [pc.sb("ut%d" % i, [128, 30 + NP]) for i in range(2)]
                uts = [pc.sb("uts%d" % i, [128, 32 + NS]) for i in range(2)]
                cbuf, bcbuf = pc.sb("cbuf", [32, 1024])
                cbufs, bcbufs = pc.sb("cbufs", [32, 1024])
                pcb, bpcb = pc.ps("pcb", [128, 512])
                for j in range(8):
                    u_t, bu = ut[j % 2]
                    s_t, bs = uts[j % 2]
                    P.dma("sp", lambda e, j=j, u_t=u_t: e.dma_start(out=u_t[:], in_=UT[j * 128:(j + 1) * 128, :]), reads=[db("UT")], writes=[bu])
                    P.dma("sp", lambda e, j=j, s_t=s_t: e.dma_start(out=s_t[:, 0:30 + NS], in_=UTS[j * 128:(j + 1) * 128, :]), reads=[db("UTS")], writes=[bs])
                    for (src, bsrc, L, c0) in ((u_t, bu, NP, 0), (s_t, bs, NS, NP)):
                        P.op("dve", lambda e, j=j, src=src, L=L, c0=c0: e.tensor_scalar(out=Cc[:, j, c0:c0 + L], in0=src[:, 0:L], scalar1=cwT[:, j, 0:1], scalar2=cv[:, j:j + 1],
                                                                                 op0=ALU.mult, op1=ALU.add), reads=[bsrc, bcwT, bcv], writes=[bC])
                        for k in range(1, 31):
                            P.op("dve", lambda e, j=j, k=k, src=src, L=L, c0=c0: e.scalar_tensor_tensor(out=Cc[:, j, c0:c0 + L], in0=src[:, k:k + L], scalar=cwT[:, j, k:k + 1],
                                                                                                   in1=Cc[:, j, c0:c0 + L], op0=ALU.mult, op1=ALU.add),
                                 reads=[bsrc, bcwT, bC], writes=[bC])
                    P.op("pe", lambda e, j=j, u_t=u_t: e.transpose(out=pcb[:30, j * 128:(j + 1) * 128] if j < 4 else pcb[:30, (j - 4) * 128:(j - 3) * 128],
                                                               in_=u_t[:, NP:NP + 30], identity=ident_f[:, :]), reads=[bu, b_ident_f], writes=[bpcb])
                    P.op("act", lambda e, j=j: e.activation(out=cbuf[:30, j * 128:(j + 1) * 128], in_=pcb[:30, (j % 4) * 128:(j % 4 + 1) * 128], func=AF.Copy), reads=[bpcb], writes=[bcbuf])
                    P.op("pe", lambda e, j=j, s_t=s_t: e.transpose(out=pcb[:30, (j % 4) * 128:(j % 4 + 1) * 128], in_=s_t[:, NS:NS + 30], identity=ident_f[:, :]),
                         reads=[bs, b_ident_f], writes=[bpcb])
                    P.op("act", lambda e, j=j: e.activation(out=cbufs[:30, j * 128:(j + 1) * 128], in_=pcb[:30, (j % 4) * 128:(j % 4 + 1) * 128], func=AF.Copy), reads=[bpcb], writes=[bcbufs])
                P.dma("sp", lambda e: e.dma_start(out=conv_p[:, :], in_=cbuf[:30, :]), reads=[bcbuf], writes=[db("conv_p")])
                P.dma("sp", lambda e: e.dma_start(out=conv_s[:, :], in_=cbufs[:30, :]), reads=[bcbufs], writes=[db("conv_s")])
                sq, bsq = pc.sb("sq", [128, 512])
                mean, bmean = pc.sb("mean", [128, 512])
                msq, bmsq = pc.sb("msq", [128, 512])
                rstd, brstd = pc.sb("rstd", [128, 512])
                tmp, btmp = pc.sb("tmp", [128, 512])
                pm, bpm = pc.ps("pm", [128, 512])
                pq, bpq = pc.ps("pq", [128, 512])
                for (t0, nt) in BLOCKS:
                    for j in range(8):
                        P.op("pe", lambda e, j=j, t0=t0, nt=nt: e.matmul(pm[:, :nt], lhsT=ones_f[:, :], rhs=Cc[:, j, t0:t0 + nt], start=(j == 0), stop=(j == 7)),
                             reads=[b_ones_f, bC], writes=[bpm])
                    for j in range(8):
                        P.op("act", lambda e, j=j, t0=t0, nt=nt: e.activation(out=sq[:, :nt], in_=Cc[:, j, t0:t0 + nt], func=AF.Square), reads=[bC], writes=[bsq])
                        P.op("pe", lambda e, j=j, nt=nt: e.matmul(pq[:, :nt], lhsT=ones_f[:, :], rhs=sq[:, :nt], start=(j == 0), stop=(j == 7)), reads=[b_ones_f, bsq], writes=[bpq])
                    P.op("act", lambda e, nt=nt: e.activation(out=mean[:, :nt], in_=pm[:, :nt], func=AF.Copy, scale=1.0 / 1024), reads=[bpm], writes=[bmean])
                    P.op("dve", lambda e, nt=nt: e.tensor_tensor(out=msq[:, :nt], in0=mean[:, :nt], in1=mean[:, :nt], op=ALU.mult), reads=[bmean], writes=[bmsq])
                    P.op("dve", lambda e, nt=nt: e.scalar_tensor_tensor(out=rstd[:, :nt], in0=pq[:, :nt], scalar=1.0 / 1024, in1=msq[:, :nt], op0=ALU.mult, op1=ALU.subtract),
                         reads=[bpq, bmsq], writes=[brstd])
                    P.op("act", lambda e, nt=nt: e.activation(out=rstd[:, :nt], in_=rstd[:, :nt], func=AF.Sqrt, bias=eps_t[:, 0:1], scale=1.0), reads=[brstd, b_eps], writes=[brstd])
                    P.op("dve", lambda e, nt=nt: e.reciprocal(out=rstd[:, :nt], in_=rstd[:, :nt]), reads=[brstd], writes=[brstd])
                    for j in range(8):
                        P.op("dve", lambda e, j=j, t0=t0, nt=nt: e.tensor_tensor(out=tmp[:, :nt], in0=Cc[:, j, t0:t0 + nt], in1=mean[:, :nt], op=ALU.subtract), reads=[bC, bmean], writes=[btmp])
                        P.op("dve", lambda e, nt=nt: e.tensor_tensor(out=tmp[:, :nt], in0=tmp[:, :nt], in1=rstd[:, :nt], op=ALU.mult), reads=[btmp, brstd], writes=[btmp])
                        P.op("dve", lambda e, j=j, nt=nt: e.tensor_scalar(out=tmp[:, :nt], in0=tmp[:, :nt], scalar1=cv[:, 8 + j:9 + j], scalar2=cv[:, 16 + j:17 + j], op0=ALU.mult, op1=ALU.add),
                             reads=[btmp, bcv], writes=[btmp])
                        P.op("act", lambda e, j=j, t0=t0, nt=nt: e.activation(out=OT[:, 8 + j, t0:t0 + nt], in_=tmp[:, :nt], func=AF.Silu), reads=[btmp], writes=[bOT])
            with Phase(P, "D0") as pd:
                oa = [pd.sb("oa%d" % i, [128, 1024]) for i in range(2)]
                ob = [pd.sb("ob%d" % i, [128, 1024], BF16) for i in range(2)]
                pstd = [pd.ps("pst%d" % i, [128, 4, 128], BF16) for i in range(2)]
                for ti, (r0, n) in enumerate(TILES):
                    a_t, ba = oa[ti % 2]
                    b_t, bb = ob[ti % 2]
                    P.dma("sp", lambda e, r0=r0, n=n, a_t=a_t: e.dma_start(out=a_t[:n], in_=OA[r0:r0 + n, :]), reads=[db("OA")], writes=[ba])
                    P.op("pool", lambda e, n=n, a_t=a_t, b_t=b_t: e.tensor_copy(out=b_t[:n], in_=a_t[:n]), reads=[ba], writes=[bb])
                    p_t, bp = pstd[ti % 2]
                    transpose_to_fm(b_t, bb, n, 8, OT, bOT, 0, r0, p_t, bp, evac=("act" if ti % 2 else "dve"))
            with Phase(P, "D1") as pd:
                proj_tok(pd, OT, bOT, KC, w_out_e, 0, D, resid_sink(pd, xin, db("xin"), X1, db("X1")))


    def peer_layer(L, Xsrc, bXsrc, Xdst, bXdst, tag):
        with Phase(P, "E" + tag) as ph:
            AT, bAT = ph.sb("AT", [128, KC, NT], BF16)
            norm_phase("E0" + tag, Xsrc, bXsrc, norm_ffn[L:L + 1, :], AT, bAT, hf_dst=HF, bhf=db("HF"))
            with Phase(P, "E1" + tag) as p1:
                proj_tok(p1, AT, bAT, KC, peer_wq[L], 0, D, store_sink(p1, QP, db("QP"), 0))
        with Phase(P, "F" + tag) as ph:
            NBUF = 4
            kraw, bkraw = ph.sb("kraw", [128, 16, 128])
            keysT, bkeysT = ph.sb("keysT", [128, 16, 128])
            iota16, biota = ph.sb("iota16", [128, 16])
            P.op("pool", lambda e: e.iota(iota16[:], pattern=[[1, 16]], base=0, channel_multiplier=0, allow_small_or_imprecise_dtypes=True), writes=[biota])
            P.dma("sp", lambda e: e.dma_start(out=kraw[:], in_=peer_keys[L].rearrange("g n d -> n g d")), writes=[bkraw])
            pt4 = [ph.ps("pt4_%d" % i, [128, 4, 128]) for i in range(2)]
            for g4 in range(4):
                p_t, bp = pt4[g4 % 2]
                for j in range(4):
                    P.op("pe", lambda e, g4=g4, j=j, p_t=p_t: e.transpose(out=p_t[:, j, :], in_=kraw[:, g4 * 4 + j, :], identity=ident_f[:, :]), reads=[bkraw, b_ident_f], writes=[bp])
                P.op("dve", lambda e, g4=g4, p_t=p_t: e.tensor_copy(out=keysT[:, g4 * 4:g4 * 4 + 4, :], in_=p_t[:]), reads=[bp], writes=[bkeysT])
            qt_, bqt = ph.sb("qt", [128, 2048])
            qT, bqT = ph.sb("qT", [128, 16, 128])
            ssb, bssb = ph.sb("ssb", [128, 16, 128])
            s2, bs2 = ph.sb("s2", [128, 16, 128])
            sv, bsv = ph.sb("sv", [128, 16, 16])
            si, bsi = ph.sb("si", [128, 16, 16], U32)
            sif, bsif = ph.sb("sif", [128, 16, 16])
            cand, bcand = ph.sb("cand", [128, 8, 256])
            cand2, bcand2 = s2[:].rearrange("p (h c) k -> p h (c k)", c=2), bs2
            oh, boh = ph.sb("oh", [128, 8, 256])
            best, bbest = ph.sb("best", [128, 8, 16])
            pos, bpos = ph.sb("pos", [128, 8, 16], U32)
            pa_, bpa_ = ph.sb("pa", [128, 8, 16], U32)
            pb_, bpb_ = ph.sb("pb", [128, 8, 16], U32)
            paf, bpaf = ph.sb("paf", [128, 8, 16])
            pbf, bpbf = ph.sb("pbf", [128, 8, 16])
            isel, bisel = ph.sb("isel", [128, 8, 16])
            jsel, bjsel = ph.sb("jsel", [128, 8, 16])
            eidx, beidx = ph.sb("eidx", [128, 128], I32)
            gsum, bgsum = ph.sb("gsum", [128, 16])
            gate, bgate = ph.sb("gate", [128, 8, 16])
            apre, bapre = ph.sb("apre", [128, 128])
            wact, bwact = ph.sb("wact", [128, 128])
            hf_t, bhf_t = ph.sb("hf", [128, 2048])
            xt_, bxt_ = ph.sb("xt", [128, 2048])
            NUV = 8
            uv = [ph.sb("uv%d" % i, [128, 2 * D], BF16) for i in range(NUV)]
            hb2 = [ph.sb("hb%d" % i, [128, 2048], BF16) for i in range(2)]
            gate2 = [(gate, bgate), ph.sb("gateB", [128, 8, 16])]
            junkb, bjunkb = ph.sb("junkb", [128, 2048], BF16)
            dgs = [ph.sb("dg%d" % i, [128, 4, 128], BF16) for i in range(4)]
            wg, bwg = ph.sb("wg", [128, 128])
            psc = [ph.ps("psc%d" % i, [128, 4, 128]) for i in range(2)]
            py = [ph.ps("py%d" % i, [128, 512]) for i in range(4)]
            bg_flush(L)
            eidxB, beidxB = ph.sb("eidxB", [128, 128], I32)
            xtB, bxtB = ph.sb("xtB", [128, 2048])
            eidx2 = [(eidx, beidx), (eidxB, beidxB)]
            xt2 = [(xt_, bxt_), (xtB, bxtB)]
            NTL = len(TILES)

            def S1(ti):
                r0, n = TILES[ti]
                ei, bei = eidx2[ti % 2]
                x_t, bx = xt2[ti % 2]
                gt, bgt = gate2[ti % 2]
                hbt, bhbt = hb2[ti % 2]
                P.dma("sp", lambda e, r0=r0, n=n: e.dma_start(out=qt_[:n], in_=QP[r0:r0 + n, :]), reads=[db("QP")], writes=[bqt])
                P.dma("sp", lambda e, r0=r0, n=n: e.dma_start(out=hf_t[:n], in_=HF[r0:r0 + n, :]), reads=[db("HF")], writes=[bhf_t])
                P.dma("sp", lambda e, r0=r0, n=n, x_t=x_t: e.dma_start(out=x_t[:n], in_=Xsrc[r0:r0 + n, :]), reads=[bXsrc], writes=[bx])
                for g4 in range(4):
                    p_t, bp = pt4[g4 % 2]
                    for j in range(4):
                        P.op("pe", lambda e, g4=g4, j=j, p_t=p_t, n=n: e.transpose(out=p_t[:, j, :n], in_=qt_[:n, (g4 * 4 + j) * 128:(g4 * 4 + j + 1) * 128], identity=ident_f[:n, :n]),
                             reads=[bqt, b_ident_f], writes=[bp])
                    P.op("act", lambda e, g4=g4, p_t=p_t, n=n: e.activation(out=qT[:, g4 * 4:g4 * 4 + 4, :n], in_=p_t[:, :, :n], func=AF.Copy), reads=[bp], writes=[bqT])
                for g4 in range(4):
                    p_t, bp = psc[g4 % 2]
                    for j in range(4):
                        P.op("pe", lambda e, g4=g4, j=j, p_t=p_t, n=n: e.matmul(p_t[:n, j, :], lhsT=qT[:, g4 * 4 + j, :n], rhs=keysT[:, g4 * 4 + j, :], start=True, stop=True),
                             reads=[bqT, bkeysT], writes=[bp])
                    P.op("dve", lambda e, g4=g4, p_t=p_t, n=n: e.tensor_copy(out=ssb[:n, g4 * 4:g4 * 4 + 4, :], in_=p_t[:n]), reads=[bp], writes=[bssb])
                for g in range(16):
                    P.op("dve", lambda e, g=g, n=n: e.max(out=sv[:n, g, 0:8], in_=ssb[:n, g, :]), reads=[bssb], writes=[bsv])
                    P.op("dve", lambda e, g=g, n=n: e.max_index(out=si[:n, g, 0:8], in_max=sv[:n, g, 0:8], in_values=ssb[:n, g, :]), reads=[bssb, bsv], writes=[bsi])
                    P.op("dve", lambda e, g=g, n=n: e.match_replace(out=s2[:n, g, :], in_to_replace=sv[:n, g, 0:8], in_values=ssb[:n, g, :], imm_value=-1e30), reads=[bssb, bsv], writes=[bs2])
                    P.op("dve", lambda e, g=g, n=n: e.max(out=sv[:n, g, 8:16], in_=s2[:n, g, :]), reads=[bs2], writes=[bsv])
                    P.op("dve", lambda e, g=g, n=n: e.max_index(out=si[:n, g, 8:16], in_max=sv[:n, g, 8:16], in_values=s2[:n, g, :]), reads=[bs2, bsv], writes=[bsi])
                P.op("dve", lambda e, n=n: e.tensor_copy(out=sif[:n], in_=si[:n]), reads=[bsi], writes=[bsif])
                sv4 = sv[:].rearrange("p (h c) k -> p h c k", c=2)
                sif4 = sif[:].rearrange("p (h c) k -> p h c k", c=2)
                cand4 = cand[:].rearrange("p h (a b) -> p h a b", a=16)
                oh4 = oh[:].rearrange("p h (a b) -> p h a b", a=16)
                P.op("dve", lambda e, n=n: e.tensor_tensor(out=cand4[:n], in0=sv4[:n, :, 0, :].unsqueeze(3).to_broadcast([n, 8, 16, 16]),
                                                           in1=sv4[:n, :, 1, :].unsqueeze(2).to_broadcast([n, 8, 16, 16]), op=ALU.add), reads=[bsv], writes=[bcand])
                for h in range(8):
                    P.op("dve", lambda e, h=h, n=n: e.max(out=best[:n, h, 0:8], in_=cand[:n, h, :]), reads=[bcand], writes=[bbest])
                    P.op("dve", lambda e, h=h, n=n: e.max_index(out=pos[:n, h, 0:8], in_max=best[:n, h, 0:8], in_values=cand[:n, h, :]), reads=[bcand, bbest], writes=[bpos])
                    P.op("dve", lambda e, h=h, n=n: e.match_replace(out=cand2[:n, h, :], in_to_replace=best[:n, h, 0:8], in_values=cand[:n, h, :], imm_value=-1e30), reads=[bcand, bbest], writes=[bcand2])
                    P.op("dve", lambda e, h=h, n=n: e.max(out=best[:n, h, 8:16], in_=cand2[:n, h, :]), reads=[bcand2], writes=[bbest])
                    P.op("dve", lambda e, h=h, n=n: e.max_index(out=pos[:n, h, 8:16], in_max=best[:n, h, 8:16], in_values=cand2[:n, h, :]), reads=[bcand2, bbest], writes=[bpos])
                P.op("dve", lambda e, n=n: e.tensor_single_scalar(out=pa_[:n], in_=pos[:n], scalar=4, op=ALU.arith_shift_right), reads=[bpos], writes=[bpa_])
                P.op("dve", lambda e, n=n: e.tensor_single_scalar(out=pb_[:n], in_=pos[:n], scalar=15, op=ALU.bitwise_and), reads=[bpos], writes=[bpb_])
                P.op("dve", lambda e, n=n: e.tensor_copy(out=paf[:n], in_=pa_[:n]), reads=[bpa_], writes=[bpaf])
                P.op("dve", lambda e, n=n: e.tensor_copy(out=pbf[:n], in_=pb_[:n]), reads=[bpb_], writes=[bpbf])
                io4 = iota16[:n, :].unsqueeze(1).unsqueeze(1).to_broadcast([n, 8, 16, 16])
                for (pf, bpf, c, dst, bdst) in ((paf, bpaf, 0, isel, bisel), (pbf, bpbf, 1, jsel, bjsel)):
                    P.op("dve", lambda e, n=n, pf=pf, io4=io4: e.tensor_tensor(out=oh4[:n], in0=pf[:n].unsqueeze(3).to_broadcast([n, 8, 16, 16]), in1=io4, op=ALU.is_equal),
                         reads=[bpf, biota], writes=[boh])
                    P.op("dve", lambda e, n=n, c=c: e.tensor_tensor(out=oh4[:n], in0=oh4[:n], in1=sif4[:n, :, c, :].unsqueeze(2).to_broadcast([n, 8, 16, 16]), op=ALU.mult),
                         reads=[boh, bsif], writes=[boh])
                    P.op("dve", lambda e, n=n, dst=dst: e.tensor_reduce(out=dst[:n], in_=oh4[:n], axis=AX.X, op=ALU.add), reads=[boh], writes=[bdst])
                P.op("dve", lambda e, n=n: e.scalar_tensor_tensor(out=isel[:n], in0=isel[:n], scalar=128.0, in1=jsel[:n], op0=ALU.mult, op1=ALU.add), reads=[bisel, bjsel], writes=[bisel])
                if L > 0:
                    P.op("dve", lambda e, n=n: e.tensor_scalar(out=isel[:n], in0=isel[:n], scalar1=float(L * NEXP), scalar2=None, op0=ALU.add), reads=[bisel], writes=[bisel])
                P.op("dve", lambda e, n=n, ei=ei: e.tensor_copy(out=ei[:n], in_=isel[:n].rearrange("p h k -> p (h k)")), reads=[bisel], writes=[bei])
                P.op("dve", lambda e, n=n, gt=gt: e.tensor_tensor(out=gt[:n], in0=best[:n], in1=best[:n, :, 0:1].to_broadcast([n, 8, 16]), op=ALU.subtract), reads=[bbest], writes=[bgt])
                P.op("act", lambda e, n=n, gt=gt: e.activation(out=gt[:n], in_=gt[:n], func=AF.Exp), reads=[bgt], writes=[bgt])
                P.op("dve", lambda e, n=n, gt=gt: e.tensor_reduce(out=gsum[:n, 0:8], in_=gt[:n], axis=AX.X, op=ALU.add), reads=[bgt], writes=[bgsum])
                P.op("dve", lambda e, n=n: e.reciprocal(out=gsum[:n, 8:16], in_=gsum[:n, 0:8]), reads=[bgsum], writes=[bgsum])
                P.op("dve", lambda e, n=n, gt=gt: e.tensor_tensor(out=gt[:n], in0=gt[:n], in1=gsum[:n, 8:16].unsqueeze(2).to_broadcast([n, 8, 16]), op=ALU.mult), reads=[bgt, bgsum], writes=[bgt])
                P.op("pool", lambda e, n=n, hbt=hbt: e.tensor_copy(out=hbt[:n], in_=hf_t[:n]), reads=[bhf_t], writes=[bhbt])

            def tile_body(ti):
                r0, n = TILES[ti]
                ei, bei = eidx2[ti % 2]
                x_t, bx = xt2[ti % 2]
                gt, bgt = gate2[ti % 2]
                hbt, bhbt = hb2[ti % 2]
                gflat = gt[:].rearrange("p h k -> p (h k)")
                P.op("dve", lambda e, n=n: e.memset(apre[:n], 0.0), writes=[bapre])
                for g in range(32):
                    d_t, bd = dgs[g % 4]
                    tiles_ = []
                    for j in range(4):
                        sl = g * 4 + j
                        u_t, bu = uv[sl % NUV]
                        tiles_.append((u_t, bu))
                        P.dma("pool", lambda e, sl=sl, n=n, u_t=u_t, ei=ei: e.indirect_dma_start(out=u_t[:n], out_offset=None, in_=UVB, in_offset=bass.IndirectOffsetOnAxis(ap=ei[:n, sl:sl + 1], axis=0)),
                              reads=[bei, db("UVB%d" % L)], writes=[bu])
                        P.op("dve", lambda e, sl=sl, n=n, u_t=u_t, hbt=hbt: e.scalar_tensor_tensor(out=junkb[:n], in0=u_t[:n, 0:D], scalar=1.0, in1=hbt[:n], op0=ALU.mult, op1=ALU.mult,
                                                                                              accum_out=apre[:n, sl:sl + 1]), reads=[bu, bhbt, bapre], writes=[bjunkb, bapre])
                    P.op("act", lambda e, g=g, n=n: e.activation(out=wg[:n, g * 4:g * 4 + 4], in_=apre[:n, g * 4:g * 4 + 4], func=AF.Gelu_apprx_tanh), reads=[bapre], writes=[bwg])
                    P.op("dve", lambda e, g=g, n=n, gflat=gflat: e.tensor_tensor(out=wg[:n, g * 4:g * 4 + 4], in0=wg[:n, g * 4:g * 4 + 4], in1=gflat[:n, g * 4:g * 4 + 4], op=ALU.mult),
                         reads=[bwg, bgt], writes=[bwg])
                    P.op("dve", lambda e, g=g, n=n, d_t=d_t: e.tensor_tensor(out=d_t[:n], in0=ident_b[:n, :].unsqueeze(1).to_broadcast([n, 4, 128]),
                                                                             in1=wg[:n, g * 4:g * 4 + 4].unsqueeze(2).to_broadcast([n, 4, 128]), op=ALU.mult),
                         reads=[b_ident_b, bwg], writes=[bd])
                    for j in range(4):
                        sl = g * 4 + j
                        u_t, bu = tiles_[j]
                        for c in range(4):
                            P.op("pe", lambda e, sl=sl, j=j, n=n, c=c, u_t=u_t, d_t=d_t: e.matmul(py[c][0][:n, :], lhsT=d_t[:n, j, :n], rhs=u_t[:n, D + c * 512:D + (c + 1) * 512],
                                                                                             start=(sl == 0), stop=(sl == 127)), reads=[bd, bu], writes=[py[c][1]])
                for c in range(4):
                    P.op("dve", lambda e, n=n, c=c, x_t=x_t: e.tensor_tensor(out=x_t[:n, c * 512:(c + 1) * 512], in0=x_t[:n, c * 512:(c + 1) * 512], in1=py[c][0][:n, :], op=ALU.add),
                         reads=[bx, py[c][1]], writes=[bx])
                P.dma("sp", lambda e, r0=r0, n=n, x_t=x_t: e.dma_start(out=Xdst[r0:r0 + n, :], in_=x_t[:n]), reads=[bx], writes=[bXdst])

            S1(0)
            for ti in range(NTL):
                if ti + 1 < NTL:
                    S1(ti + 1)
                tile_body(ti)


    def layer1_sample(ph, OT, bOT):
        with Phase(P, "Lc") as pc:
            biasb, bbias = pc.sb("biasb", [128, 16])
            P.dma("sp", lambda e: e.dma_start(out=biasb[:], in_=sb_bias[0:1, :].to_broadcast([128, 16])), writes=[bbias])
            biasr, bbiasr = pc.sb("biasr", [128, 16, 8])
            P.op("dve", lambda e: e.tensor_copy(out=biasr[:], in_=biasb[:].unsqueeze(2).to_broadcast([128, 16, 8])), reads=[bbias], writes=[bbiasr])
            ustr, bustr = pc.sb("ustr", [128, 128])
            P.op("pool", lambda e: e.memset(ustr[:], 1.0), writes=[bustr])
            P.op("pool", lambda e: e.affine_select(out=ustr[:], in_=ustr[:], pattern=[[-1, 128]], compare_op=ALU.is_ge, fill=0.0, base=-1, channel_multiplier=1),
                 reads=[bustr], writes=[bustr])
            mnew, bmnew = pc.sb("mnew", [128, 16, 8])
            P.op("pool", lambda e: e.memset(mnew[:], 1.0), writes=[bmnew])
            P.op("pool", lambda e: e.affine_select(out=mnew[:], in_=mnew[:], pattern=[[0, 16], [1, 8]], compare_op=ALU.is_ge, fill=0.0, base=-1, channel_multiplier=-1),
                 reads=[bmnew], writes=[bmnew])
            pti, bpti = pc.sb("pti", [128, NPAGES], I32)
            ptf, bptf = pc.sb("ptf", [128, NPAGES])
            pidx, bpidx = pc.sb("pidx", [128, NPAGES], I32)
            iop, biop = pc.sb("iop", [128, 1])
            P.dma("sp", lambda e: e.dma_start(out=pti[:], in_=ptab[0:1, :].to_broadcast([128, NPAGES])), writes=[bpti])
            P.op("pool", lambda e: e.iota(iop[:], pattern=[[0, 1]], base=0, channel_multiplier=1, allow_small_or_imprecise_dtypes=True), writes=[biop])
            P.op("dve", lambda e: e.tensor_copy(out=ptf[:], in_=pti[:]), reads=[bpti], writes=[bptf])
            P.op("dve", lambda e: e.tensor_scalar(out=ptf[:], in0=ptf[:], scalar1=128.0, scalar2=iop[:, 0:1], op0=ALU.mult, op1=ALU.add), reads=[bptf, biop], writes=[bptf])
            P.op("dve", lambda e: e.tensor_copy(out=pidx[:], in_=ptf[:]), reads=[bptf], writes=[bpidx])
            qs, bqs = pc.sb("qs", [128, 16, 8])
            ks, bks = pc.sb("ks", [128, 16, 8])
            qsb, bqsb = pc.sb("qsb", [128, 16, 8], BF16)
            ksb, bksb = pc.sb("ksb", [128, 16, 8], BF16)
            P.dma("sp", lambda e: e.dma_start(out=qs[:], in_=QT[:, NP:NP + 8].rearrange("(h p) t -> p h t", p=128)), reads=[db("QT")], writes=[bqs])
            P.dma("sp", lambda e: e.dma_start(out=ks[:], in_=KT[:, NP:NP + 8].rearrange("(h p) t -> p h t", p=128)), reads=[db("KT")], writes=[bks])
            P.op("dve", lambda e: e.tensor_copy(out=qsb[:], in_=qs[:]), reads=[bqs], writes=[bqsb])
            P.op("dve", lambda e: e.tensor_copy(out=ksb[:], in_=ks[:]), reads=[bks], writes=[bksb])
            kp = [pc.sb("kp%d" % i, [128, 2048]) for i in range(2)]
            vp = [pc.sb("vp%d" % i, [128, 2048]) for i in range(2)]
            vpb = [pc.sb("vpb%d" % i, [128, 2048], BF16) for i in range(2)]
            kTb, bkTb = pc.sb("kTb", [128, 16, 128], BF16)
            zb, bzb = pc.sb("zb", [128, 128])
            spt, bspt = pc.sb("spt", [128, 128])
            dt_, bdt = pc.sb("dt", [128, 128])
            Abt, bAbt = pc.sb("Abt", [128, 128], BF16)
            spsum, bsps = pc.sb("spsum", [128, 128])
            P.op("pool", lambda e: e.memset(spsum[:], 0.0), writes=[bsps])
            P.op("pool", lambda e: e.memset(spt[:], 0.0), writes=[bspt])
            P.op("pool", lambda e: e.memset(Abt[:], 0.0), writes=[bAbt])
            ptr = [pc.ps("ptr%d" % i, [128, 4, 128]) for i in range(2)]
            pz, bpz = pc.ps("pz", [128, 512])
            pT, bpT = pc.ps("pT", [128, 512])
            po, bpo = pc.ps("po", [128, 512])
            zbf, bzbf = pc.sb("zbf", [128, 128], BF16)
            P.op("pool", lambda e: e.memset(zbf[:], 0.0), writes=[bzbf])
            P.op("pe", lambda e: e.matmul(po[:, 0:128], lhsT=zbf[:, :], rhs=zbf[:, :], start=True, stop=False), reads=[bzbf], writes=[bpo])
            v_t, bv = vp[0]
            vb_t, bvb = vpb[0]
            P.op("pool", lambda e, v_t=v_t: e.memset(v_t[:], 0.0), writes=[bv])
            P.dma("sp", lambda e, v_t=v_t: e.dma_start(out=v_t[:8, :], in_=v_all[NP:NP + 8, :]), reads=[db("v_all")], writes=[bv])
            P.op("pool", lambda e, v_t=v_t, vb_t=vb_t: e.tensor_copy(out=vb_t[:], in_=v_t[:]), reads=[bv], writes=[bvb])
            blocks = [("new", 8)] + [(p, 128) for p in range(NPAGES - 1, -1, -1)]
            for bi, (pg, nk) in enumerate(blocks):
                first = bi == 0
                last = bi == len(blocks) - 1
                if pg != "new":
                    k_t, bk = kp[bi % 2]
                    v_t, bv = vp[bi % 2]
                    vb_t, bvb = vpb[bi % 2]
                    P.dma("pool", lambda e, pg=pg, k_t=k_t: e.indirect_dma_start(out=k_t[:], out_offset=None, in_=cache_k, in_offset=bass.IndirectOffsetOnAxis(ap=pidx[:, pg:pg + 1], axis=0)),
                          reads=[bpidx], writes=[bk])
                    P.dma("pool", lambda e, pg=pg, v_t=v_t: e.indirect_dma_start(out=v_t[:], out_offset=None, in_=cache_v, in_offset=bass.IndirectOffsetOnAxis(ap=pidx[:, pg:pg + 1], axis=0)),
                          reads=[bpidx], writes=[bv])
                    P.op("act", lambda e, v_t=v_t, vb_t=vb_t: e.activation(out=vb_t[:], in_=v_t[:], func=AF.Copy), reads=[bv], writes=[bvb])
                    for g4 in range(4):
                        p_t, bp = ptr[g4 % 2]
                        for j in range(4):
                            P.op("pe", lambda e, g4=g4, j=j, p_t=p_t, k_t=k_t: e.transpose(out=p_t[:, j, :], in_=k_t[:, (g4 * 4 + j) * 128:(g4 * 4 + j + 1) * 128], identity=ident_f[:, :]),
                                 reads=[bk, b_ident_f], writes=[bp])
                        P.op("dve", lambda e, g4=g4, p_t=p_t: e.tensor_copy(out=kTb[:, g4 * 4:g4 * 4 + 4, :], in_=p_t[:]), reads=[bp], writes=[bkTb])
                    for h in range(16):
                        P.op("pe", lambda e, h=h: e.matmul(pz[:, h * 8:(h + 1) * 8], lhsT=kTb[:, h, :], rhs=qsb[:, h, :], start=True, stop=True), reads=[bkTb, bqsb], writes=[bpz])
                else:
                    for h in range(16):
                        P.op("pe", lambda e, h=h: e.matmul(pz[:8, h * 8:(h + 1) * 8], lhsT=ksb[:, h, :], rhs=qsb[:, h, :], start=True, stop=True), reads=[bksb, bqsb], writes=[bpz])
                P.op("dve", lambda e, nk=nk: e.tensor_tensor(out=zb[:nk], in0=pz[:nk, 0:128], in1=biasr[:nk].rearrange("p h q -> p (h q)"), op=ALU.add), reads=[bpz, bbiasr], writes=[bzb])
                P.op("act", lambda e, nk=nk: e.activation(out=spt[:nk], in_=zb[:nk], func=AF.Exp), reads=[bzb], writes=[bspt])
                P.op("act", lambda e, nk=nk: e.activation(out=spt[:nk], in_=spt[:nk], func=AF.Ln, bias=1.0, scale=1.0), reads=[bspt], writes=[bspt])
                if first:
                    P.op("dve", lambda e, nk=nk: e.tensor_tensor(out=spt[:nk], in0=spt[:nk], in1=mnew[:nk].rearrange("p h q -> p (h q)"), op=ALU.mult), reads=[bspt, bmnew], writes=[bspt])
                P.op("pe", lambda e, nk=nk, first=first: e.matmul(pT[:nk, 0:128], lhsT=ustr[:, :nk], rhs=spt[:, :], start=True, stop=first), reads=[bustr, bspt], writes=[bpT])
                if not first:
                    P.op("pe", lambda e, nk=nk: e.matmul(pT[:nk, 0:128], lhsT=ones_f[:, :nk], rhs=spsum[:, :], start=False, stop=True), reads=[b_ones_f, bsps], writes=[bpT])
                P.op("dve", lambda e, nk=nk: e.tensor_tensor(out=dt_[:nk], in0=zb[:nk], in1=spt[:nk], op=ALU.subtract), reads=[bzb, bspt], writes=[bdt])
                P.op("dve", lambda e, nk=nk: e.tensor_tensor(out=dt_[:nk], in0=dt_[:nk], in1=pT[:nk, 0:128], op=ALU.subtract), reads=[bdt, bpT], writes=[bdt])
                if first:
                    P.op("act", lambda e, nk=nk: e.activation(out=dt_[:nk], in_=dt_[:nk], func=AF.Exp), reads=[bdt], writes=[bdt])
                    P.op("dve", lambda e, nk=nk: e.tensor_tensor(out=Abt[:nk], in0=dt_[:nk], in1=mnew[:nk].rearrange("p h q -> p (h q)"), op=ALU.mult), reads=[bdt, bmnew], writes=[bAbt])
                else:
                    P.op("act", lambda e, nk=nk: e.activation(out=Abt[:nk], in_=dt_[:nk], func=AF.Exp), reads=[bdt], writes=[bAbt])
                if not last:
                    P.op("pool", lambda e: e.tensor_tensor(out=spsum[:], in0=spsum[:], in1=spt[:], op=ALU.add), reads=[bsps, bspt], writes=[bsps])
                for h in range(16):
                    P.op("pe", lambda e, h=h, vb_t=vb_t, first=first, last=last: e.matmul(po[:, h * 8:(h + 1) * 8], lhsT=vb_t[:, h * 128:(h + 1) * 128], rhs=Abt[:, h * 8:(h + 1) * 8],
                                                                                       start=False, stop=(last and h == 15)), reads=[bvb, bAbt], writes=[bpo])
            P.op("act", lambda e: e.activation(out=OT[:, :, NP:NP + 8], in_=po[:, 0:128].rearrange("p (h q) -> p h q", h=16), func=AF.Copy), reads=[bpo], writes=[bOT])

    def layer1(Xsrc, bXsrc, Xdst, bXdst):
        NPT = (NP + 127) // 128
        PBLOCKS = [b for b in BLOCKS if b[0] < NP]
        with Phase(P, "L") as ph:
            with Phase(P, "La") as pa:
                AT, bAT = pa.sb("AT", [128, KC, NT], BF16)
                norm_phase("La0", Xsrc, bXsrc, norm_mix[1:2, :], AT, bAT)
                with Phase(P, "La1") as p1:
                    proj_tok(p1, AT, bAT, KC, w_qkv, 2048, 2048, store_sink(p1, k_all, db("k_all"), 0))
                with Phase(P, "La2") as p1:
                    proj_tok(p1, AT, bAT, KC, w_qkv, 4096, 2048, store_sink(p1, v_all, db("v_all"), 0))
                with Phase(P, "La3") as p1:
                    stg = [p1.sb("cs%d" % i, [128, 512]) for i in range(4)]
                    st_ = {"i": 0}

                    def mk_sink(dst, bdst, scale):
                        def sink(gi, bi, t0, nt, pts):
                            for j, (p_t, bp) in enumerate(pts):
                                s_t, bs = stg[st_["i"] % 4]
                                st_["i"] += 1
                                P.op("act", lambda e, s_t=s_t, p_t=p_t, nt=nt: e.activation(out=s_t[:, :nt], in_=p_t[:, :nt], func=AF.Copy, scale=scale), reads=[bp], writes=[bs])
                                P.dma("sp", lambda e, s_t=s_t, gi=gi, j=j, t0=t0, nt=nt: e.dma_start(out=dst[(gi * 2 + j) * 128:(gi * 2 + j + 1) * 128, t0:t0 + nt], in_=s_t[:, :nt]),
                                      reads=[bs], writes=[bdst])
                        return sink
                    proj_ch(p1, AT, bAT, KC, w_qkv, [[c, c + 128] for c in range(0, 2048, 256)], mk_sink(QT, db("QT"), 128.0 ** -0.5), tag="q")
                with Phase(P, "La4") as p1:
                    stg = [p1.sb("cs%d" % i, [128, 512]) for i in range(4)]
                    st_ = {"i": 0}
                    proj_ch(p1, AT, bAT, KC, w_qkv, [[c, c + 128] for c in range(2048, 4096, 256)], mk_sink(KT, db("KT"), 1.0), tag="k")
            OT, bOT = ph.sb("OT", [128, KC, NT], BF16)
            with Phase(P, "Lb") as pb:
                biasb, bbias = pb.sb("biasb", [128, 16])
                P.dma("sp", lambda e: e.dma_start(out=biasb[:], in_=sb_bias[0:1, :].to_broadcast([128, 16])), writes=[bbias])
                ustr, bustr = pb.sb("ustr", [128, 128])
                P.op("pool", lambda e: e.memset(ustr[:], 1.0), writes=[bustr])
                P.op("pool", lambda e: e.affine_select(out=ustr[:], in_=ustr[:], pattern=[[-1, 128]], compare_op=ALU.is_ge, fill=0.0, base=-1, channel_multiplier=1),
                     reads=[bustr], writes=[bustr])
                masks = []
                for r in range(4):
                    m_t, bm = pb.sb("mask%d" % r, [128, 512])
                    P.op("pool", lambda e, m_t=m_t: e.memset(m_t[:], 1.0), writes=[bm])
                    P.op("pool", lambda e, m_t=m_t, r=r: e.affine_select(out=m_t[:], in_=m_t[:], pattern=[[1, 512]], compare_op=ALU.is_ge, fill=0.0, base=-128 * r - 1, channel_multiplier=-1),
                         reads=[bm], writes=[bm])
                    masks.append((m_t, bm))
                qf = [pb.sb("qf%d" % i, [128, NP]) for i in range(2)]
                kf = [pb.sb("kf%d" % i, [128, NP]) for i in range(2)]
                vf = [pb.sb("vf%d" % i, [128, NPT, 128]) for i in range(2)]
                qb, bqb = pb.sb("qb", [128, NP], BF16)
                kb_, bkb = pb.sb("kb", [128, NP], BF16)
                vbb, bvbb = pb.sb("vbb", [128, NPT, 128], BF16)
                for v_t, bv in vf:
                    P.op("pool", lambda e, v_t=v_t: e.memset(v_t[:], 0.0), writes=[bv])
                esb = [pb.sb("esb%d" % i, [128, 512]) for i in range(2)]
                t1 = [pb.sb("t1%d" % i, [128, 512]) for i in range(2)]
                Ab = [pb.sb("Ab%d" % i, [128, 512], BF16) for i in range(2)]
                for a_t, ba in Ab:
                    P.op("pool", lambda e, a_t=a_t: e.memset(a_t[:], 0.0), writes=[ba])
                spsum, bsps = pb.sb("spsum", [128, 512])
                pz = [pb.ps("pz%d" % i, [128, 512]) for i in range(2)]
                pT = [pb.ps("pT%d" % i, [128, 512]) for i in range(2)]
                po = [pb.ps("po%d" % i, [128, 512]) for i in range(2)]
                nfull = NP // 128
                rem = NP - nfull * 128
                cnt = 0
                och = 0
                for h in range(16):
                    q_t, bq = qf[h % 2]
                    k_t, bk = kf[h % 2]
                    v_t, bv = vf[h % 2]
                    P.dma("sp", lambda e, h=h, q_t=q_t: e.dma_start(out=q_t[:], in_=QT[h * 128:(h + 1) * 128, 0:NP]), reads=[db("QT")], writes=[bq])
                    P.dma("sp", lambda e, h=h, k_t=k_t: e.dma_start(out=k_t[:], in_=KT[h * 128:(h + 1) * 128, 0:NP]), reads=[db("KT")], writes=[bk])
                    P.dma("sp", lambda e, h=h, v_t=v_t: e.dma_start(out=v_t[:, 0:nfull, :], in_=v_all[0:nfull * 128, h * 128:(h + 1) * 128].rearrange("(t p) d -> p t d", p=128)),
                          reads=[db("v_all")], writes=[bv])
                    if rem:
                        P.dma("sp", lambda e, h=h, v_t=v_t: e.dma_start(out=v_t[:rem, nfull, :], in_=v_all[nfull * 128:NP, h * 128:(h + 1) * 128]), reads=[db("v_all")], writes=[bv])
                    P.op("pool", lambda e, q_t=q_t: e.tensor_copy(out=qb[:], in_=q_t[:]), reads=[bq], writes=[bqb])
                    P.op("pool", lambda e, k_t=k_t: e.tensor_copy(out=kb_[:], in_=k_t[:]), reads=[bk], writes=[bkb])
                    P.op("pool", lambda e, v_t=v_t: e.tensor_copy(out=vbb[:], in_=v_t[:]), reads=[bv], writes=[bvbb])
                    for (q0, nq) in PBLOCKS:
                        kb_last = (q0 + nq - 1) // 128
                        o_t, bo = po[och % 2]
                        och += 1
                        P.op("pool", lambda e: e.memset(spsum[:], 0.0), writes=[bsps])
                        for kb in range(kb_last, -1, -1):
                            nk = min(128, NP - kb * 128)
                            z_t, bz = pz[cnt % 2]
                            T_t, bT = pT[cnt % 2]
                            e_t, be = esb[cnt % 2]
                            d_t, bd = t1[cnt % 2]
                            a_t, ba = Ab[cnt % 2]
                            cnt += 1
                            first = kb == kb_last
                            diag = kb * 128 + nk > q0
                            r = kb - q0 // 128
                            P.op("pe", lambda e, kb=kb, nk=nk, q0=q0, nq=nq, z_t=z_t: e.matmul(z_t[:nk, :nq], lhsT=kb_[:, kb * 128:kb * 128 + nk], rhs=qb[:, q0:q0 + nq], start=True, stop=True),
                                 reads=[bkb, bqb], writes=[bz])
                            if nk < 128:
                                P.op("dve", lambda e, e_t=e_t: e.memset(e_t[:], 0.0), writes=[be])
                            P.op("act", lambda e, h=h, nk=nk, nq=nq, z_t=z_t, e_t=e_t: e.activation(out=e_t[:nk, :nq], in_=z_t[:nk, :nq], func=AF.Exp, bias=biasb[:nk, h:h + 1], scale=1.0),
                                 reads=[bz, bbias], writes=[be])
                            P.op("act", lambda e, nk=nk, nq=nq, e_t=e_t: e.activation(out=e_t[:nk, :nq], in_=e_t[:nk, :nq], func=AF.Ln, bias=1.0, scale=1.0), reads=[be], writes=[be])
                            if diag:
                                m_t, bm = masks[r]
                                P.op("dve", lambda e, nk=nk, nq=nq, e_t=e_t, m_t=m_t: e.tensor_tensor(out=e_t[:nk, :nq], in0=e_t[:nk, :nq], in1=m_t[:nk, :nq], op=ALU.mult),
                                     reads=[be, bm], writes=[be])
                            P.op("pe", lambda e, nk=nk, nq=nq, T_t=T_t, e_t=e_t, first=first: e.matmul(T_t[:nk, :nq], lhsT=ustr[:, :nk], rhs=e_t[:, :nq], start=True, stop=first),
                                 reads=[bustr, be], writes=[bT])
                            if not first:
                                P.op("pe", lambda e, nk=nk, nq=nq, T_t=T_t: e.matmul(T_t[:nk, :nq], lhsT=ones_f[:, :nk], rhs=spsum[:, :nq], start=False, stop=True),
                                     reads=[b_ones_f, bsps], writes=[bT])
                            P.op("dve", lambda e, h=h, nk=nk, nq=nq, z_t=z_t, e_t=e_t, d_t=d_t: e.scalar_tensor_tensor(out=d_t[:nk, :nq], in0=z_t[:nk, :nq], scalar=biasb[:nk, h:h + 1], in1=e_t[:nk, :nq],
                                                                                                              op0=ALU.add, op1=ALU.subtract), reads=[bz, bbias, be], writes=[bd])
                            P.op("dve", lambda e, nk=nk, nq=nq, T_t=T_t, d_t=d_t: e.tensor_tensor(out=d_t[:nk, :nq], in0=d_t[:nk, :nq], in1=T_t[:nk, :nq], op=ALU.subtract), reads=[bd, bT], writes=[bd])
                            if diag:
                                P.op("act", lambda e, nk=nk, nq=nq, d_t=d_t: e.activation(out=d_t[:nk, :nq], in_=d_t[:nk, :nq], func=AF.Exp), reads=[bd], writes=[bd])
                                P.op("dve", lambda e, nk=nk, nq=nq, d_t=d_t, a_t=a_t, m_t=m_t: e.tensor_tensor(out=a_t[:nk, :nq], in0=d_t[:nk, :nq], in1=m_t[:nk, :nq], op=ALU.mult),
                                     reads=[bd, bm], writes=[ba])
                            else:
                                P.op("act", lambda e, nk=nk, nq=nq, d_t=d_t, a_t=a_t: e.activation(out=a_t[:nk, :nq], in_=d_t[:nk, :nq], func=AF.Exp), reads=[bd], writes=[ba])
                            if kb > 0:
                                P.op("pool", lambda e, nq=nq, e_t=e_t: e.tensor_tensor(out=spsum[:, :nq], in0=spsum[:, :nq], in1=e_t[:, :nq], op=ALU.add), reads=[bsps, be], writes=[bsps])
                            P.op("pe", lambda e, kb=kb, nk=nk, nq=nq, o_t=o_t, a_t=a_t, first=first: e.matmul(o_t[:, :nq], lhsT=vbb[:nk, kb, :], rhs=a_t[:nk, :nq], start=first, stop=(kb == 0)),
                                 reads=[bvbb, ba], writes=[bo])
                        P.op("act", lambda e, h=h, q0=q0, nq=nq, o_t=o_t: e.activation(out=OT[:, h, q0:q0 + nq], in_=o_t[:, :nq], func=AF.Copy), reads=[bo], writes=[bOT])
            if stop_after != "Lb":
                layer1_sample(ph, OT, bOT)
            with Phase(P, "Ld") as pd:
                proj_tok(pd, OT, bOT, KC, w_out_o, 0, D, resid_sink(pd, Xsrc, bXsrc, Xdst, bXdst))

    layer0_inproj()
    if stop_after == "A":
        G.__exit__(None, None, None)
        P.emit()
        return nc
    layer0_gla()
    if stop_after == "B":
        G.__exit__(None, None, None)
        P.emit()
        return nc
    layer0_conv_out()
    if stop_after == "D":
        G.__exit__(None, None, None)
        P.emit()
        return nc
    peer_layer(0, X1, db("X1"), X2, db("X2"), "0")
    if stop_after == "E":
        G.__exit__(None, None, None)
        P.emit()
        return nc
    layer1(X2, db("X2"), X3, db("X3"))
    if stop_after in ("Lb", "L"):
        G.__exit__(None, None, None)
        P.emit()
        return nc
    peer_layer(1, X3, db("X3"), X4, db("X4"), "1")
    with Phase(P, "Z") as ph:
        gain, bgain = gain_tile(ph, "gain", norm_final[0:1, :])
        xz = [ph.sb("x%d" % i, [128, D]) for i in range(2)]
        yz = [ph.sb("y%d" % i, [128, D]) for i in range(2)]
        junk, bjunk = ph.sb("junk", [128, D])
        ss, bss = ph.sb("ss", [128, 4])
        for ti, (r0, n) in enumerate(TILES):
            x_t, bx = xz[ti % 2]
            y_t, by = yz[ti % 2]
            P.dma("sp", lambda e, r0=r0, n=n, x_t=x_t: e.dma_start(out=x_t[:n], in_=X4[r0:r0 + n, :]), reads=[db("X4")], writes=[bx])
            rmsnorm_tile(ph, x_t, bx, n, gain, bgain, y_t, by, (junk, bjunk, ss, bss))
            P.dma("sp", lambda e, r0=r0, n=n, y_t=y_t: e.dma_start(out=y_out[r0:r0 + n, :], in_=y_t[:n]), reads=[by], writes=[db("y_out")])

    G.__exit__(None, None, None)
    P.emit()
    return nc


def make_in_maps(inp, ncores=8, SEQ=2048):
    f32 = np.float32
    a = lambda v: np.ascontiguousarray(np.asarray(v))
    npool = inp["cache_k"].shape[1]
    shared = {
        "cache_k": a(inp["cache_k"][0]).reshape(npool * 128, D),
        "cache_v": a(inp["cache_v"][0]).reshape(npool * 128, D),
        "norm_mix": a(inp["norm_mix"]),
        "norm_ffn": a(inp["norm_ffn"]),
        "norm_final": a(inp["norm_final"]).reshape(1, D),
        "w_in": a(inp["w_in_even"][0]),
        "w_lr": a(inp["w_gate_lr"][0]),
        "b_lr": a(inp["b_gate_lr"][0]).reshape(1, 512),
        "gla_norm": a(inp["gla_norm"][0]).reshape(1, 256),
        "conv_w": a(inp["conv_w"][0]),
        "conv_vec": a(np.concatenate([np.asarray(inp["conv_b"][0]).reshape(8, 128), np.asarray(inp["conv_norm_g"][0]).reshape(8, 128),
                                      np.asarray(inp["conv_norm_b"][0]).reshape(8, 128)], 0)),
        "w_out_e": a(inp["w_out_even"][0]),
        "w_qkv": a(inp["w_qkv_odd"][0]),
        "w_out_o": a(inp["w_out_odd"][0]),
        "sb_bias": a(inp["sb_bias"][0]).reshape(1, 16),
        "peer_wq": a(inp["peer_wq"]),
        "peer_keys": a(inp["peer_keys"]).reshape(2, 16, 128, 128),
        "peer_u": a(inp["peer_u"]),
        "peer_v": a(inp["peer_v"]),
    }
    maps = []
    for c in range(ncores):
        b = c % 4
        m = dict(shared)
        m["xin"] = a(np.concatenate([np.asarray(inp["meta_tokens"]), np.asarray(inp["x_prompt"][b]), np.asarray(inp["x_sample"][c])], 0).astype(f32))
        m["sgla"] = a(inp["state_gla"][0, c])
        m["sconv"] = a(inp["state_conv"][0, c])
        m["ptab"] = a(inp["page_table"][c:c + 1]).astype(np.int32)
        maps.append(m)
    return maps


_CACHE = {}


def kernel(**inputs):
    inp = {k: np.asarray(v) for k, v in inputs.items()}
    SEQ = inp["x_prompt"].shape[1]
    NPAGES = inp["page_table"].shape[1]
    NPOOL = inp["cache_k"].shape[1]
    NB = inp["x_prompt"].shape[0]
    NSB = inp["x_sample"].shape[0]
    NP = N_META + SEQ
    nc = build(SEQ=SEQ, NPAGES=NPAGES, NPOOL=NPOOL)
    in_maps = make_in_maps(inp, ncores=8, SEQ=SEQ)
    res = run_bass_kernel_spmd(nc, in_maps, core_ids=list(range(8)))
    r = res.results
    f32 = np.float32
    y_prompt = np.stack([r[b]["y_out"][N_META:NP] for b in range(NB)]).astype(f32)
    y_sample = np.stack([r[c]["y_out"][NP:NP + 8] for c in range(NSB)]).astype(f32)
    gla_prompt = np.stack([r[b]["gla_p"] for b in range(NB)])[None].astype(f32)
    gla_sample = np.stack([r[c]["gla_s"] for c in range(NSB)])[None].astype(f32)
    conv_prompt = np.stack([r[b]["conv_p"] for b in range(NB)])[None].astype(f32)
    conv_sample = np.stack([r[c]["conv_s"] for c in range(NSB)])[None].astype(f32)
    k_prompt = np.stack([r[b]["k_all"][:NP].reshape(NP, 16, 128) for b in range(NB)])[None].astype(f32)
    v_prompt = np.stack([r[b]["v_all"][:NP].reshape(NP, 16, 128) for b in range(NB)])[None].astype(f32)
    k_sample = np.stack([r[c]["k_all"][NP:NP + 8].reshape(8, 16, 128) for c in range(NSB)])[None].astype(f32)
    v_sample = np.stack([r[c]["v_all"][NP:NP + 8].reshape(8, 16, 128) for c in range(NSB)])[None].astype(f32)
    return (y_prompt, y_sample, gla_prompt, gla_sample, conv_prompt, conv_sample, k_prompt, v_prompt, k_sample, v_sample)
```

```python
import contextlib
import numpy as np
import concourse.bass as bass
import concourse.mybir as mybir
from concourse.bass_utils import run_bass_kernel_spmd

F32 = mybir.dt.float32
BF16 = mybir.dt.bfloat16
I32 = mybir.dt.int32
U32 = mybir.dt.uint32
ALU = mybir.AluOpType
AF = mybir.ActivationFunctionType
AX = mybir.AxisListType

D = 2048
KC = 16
N_META = 16
EPS = 1e-6
NDMASEM = 8
import os as _os
GLA_STAGE = int(_os.environ.get('GLA_STAGE', 9))
GLA_SUB = int(_os.environ.get('GLA_SUB', 9))


class Buf:
    __slots__ = ("name", "w", "r", "x", "multi", "mw")

    def __init__(self, name, init=None, x=False, multi=False):
        self.name = name
        self.w = None
        self.r = dict(init) if init else {}
        self.multi = multi
        self.mw = {}
        self.x = x


class Prog:
    ENGS = ("pe", "act", "dve", "pool", "sp")

    def __init__(self, nc):
        self.nc = nc
        self.ops = {e: [] for e in self.ENGS}
        self.cnt = {e: 0 for e in self.ENGS}
        self.dma_n = {e: 0 for e in self.ENGS}
        self.dma_last = {}
        self.waited = {e: {} for e in self.ENGS}
        self.barrier = {}

    def _deps(self, eng, reads, writes):
        deps = {}

        def add(k, v):
            if deps.get(k, 0) < v:
                deps[k] = v
        for b in reads:
            if b.multi:
                for k, v in b.mw.items():
                    add(k, v)
            elif b.w is not None:
                add(*b.w)
        for b in writes:
            if not b.multi and b.w is not None:
                add(*b.w)
            for k, v in b.r.items():
                add(k, v)
        out = []
        wd = self.waited[eng]
        for k, v in deps.items():
            if eng == "pe" and k == ("e", "pe"):
                continue
            if wd.get(k, 0) >= v:
                continue
            wd[k] = v
            out.append((k, v))
        return out

    def _commit(self, ev, reads, writes):
        k, v = ev
        for b in reads:
            if b.r.get(k, 0) < v:
                b.r[k] = v
        for b in writes:
            if b.multi:
                if b.mw.get(k, 0) < v:
                    b.mw[k] = v
            else:
                b.w = ev
                b.r = {}

    def op(self, eng, fn, reads=(), writes=()):
        xr = [b for b in reads if b.x]
        if xr:
            reads = [b for b in reads if not b.x]
            writes = list(writes) + [b for b in xr if b not in writes]
        waits = self._deps(eng, reads, writes)
        self.cnt[eng] += 1
        ev = (("e", eng), self.cnt[eng])
        self.ops[eng].append((waits, fn, ev, 1))
        self._commit(ev, reads, writes)

    def dma(self, eng, fn, reads=(), writes=()):
        n = self.dma_n[eng]
        k = ("d", eng, n % NDMASEM)
        waits = self._deps(eng, reads, writes)
        prev = self.dma_last.get(k)
        if prev is not None and self.waited[eng].get(k, 0) < prev:
            self.waited[eng][k] = prev
            waits.append((k, prev))
        val = (prev or 0) + 16
        self.dma_last[k] = val
        self.dma_n[eng] = n + 1
        ev = (k, val)
        self.ops[eng].append((waits, fn, ev, 16))
        self._commit(ev, reads, writes)

    def release(self, bufs):
        for b in bufs:
            evs = list(b.r.items())
            if b.w is not None:
                evs.append(b.w)
            for k, v in evs:
                if self.barrier.get(k, 0) < v:
                    self.barrier[k] = v

    def emit(self):
        nc = self.nc
        final_events = list(self.dma_last.items())
        with contextlib.ExitStack() as st:
            sems = {}
            for e in self.ENGS:
                sems[("e", e)] = st.enter_context(nc.semaphore("s_" + e))
                for s in range(NDMASEM):
                    sems[("d", e, s)] = st.enter_context(nc.semaphore("d_%s_%d" % (e, s)))
            block = st.enter_context(nc.Block())
            engmap = {"pe": "tensor", "act": "scalar", "dve": "vector", "pool": "gpsimd", "sp": "sync"}

            def make(ename):
                oplist = self.ops[ename]

                def body(eng):
                    for waits, fn, ev, inc in oplist:
                        for k, v in waits:
                            eng.wait_ge(sems[k], v)
                        fn(eng).then_inc(sems[ev[0]], inc)
                    if ename == "sp":
                        for k, v in final_events:
                            eng.wait_ge(sems[k], v)
                return body
            for ename in self.ENGS:
                getattr(block, engmap[ename])(make(ename))


class Phase:
    def __init__(self, P, name):
        self.P = P
        self.nc = P.nc
        self.name = name
        self.st = contextlib.ExitStack()
        self.bufs = []
        self.k = 0

    def __enter__(self):
        self.st.__enter__()
        return self

    def __exit__(self, *a):
        self.P.release(self.bufs)
        return self.st.__exit__(*a)

    def buf(self, name="b", x=False):
        b = Buf(name, self.P.barrier, x)
        self.bufs.append(b)
        return b

    def sb(self, name, shape, dt=F32, nb=None):
        t = self.st.enter_context(self.nc.sbuf_tensor("%s_%s" % (self.name, name), list(shape), dt))
        if nb is None:
            return t, self.buf(name)
        return t, [self.buf(name + str(i)) for i in range(nb)]

    def ps(self, name, shape, dt=F32):
        t = self.st.enter_context(self.nc.psum_tensor("%s_%s" % (self.name, name), list(shape), dt))
        return t, self.buf(name, x=True)


def tiles_of(NP, NS):
    tl = []
    r = 0
    while r + 128 <= NP:
        tl.append((r, 128))
        r += 128
    tl.append((r, NP - r + NS))
    return tl


def blocks_of(NP, NS, bs=512):
    bl = []
    r = 0
    while r < NP:
        n = min(bs, NP - r)
        bl.append((r, n))
        r += n
    bl.append((NP, NS))
    return bl


def build(SEQ=2048, NPAGES=128, NPOOL=1280, debug=False, stop_after=None, NEXP=16384):
    nc = bass.Bass("TRN2", target_bir_lowering=False)
    NP = N_META + SEQ
    NS = 8
    NT = NP + NS
    TILES = tiles_of(NP, NS)
    BLOCKS = blocks_of(NP, NS)
    NTILE = len(TILES)

    def din(name, shape, dt=F32):
        return nc.dram_tensor(name, list(shape), dt, kind="ExternalInput").ap()

    def dout(name, shape, dt=F32):
        return nc.dram_tensor(name, list(shape), dt, kind="ExternalOutput").ap()

    def dscr(name, shape, dt=F32):
        if debug:
            return nc.dram_tensor(name, list(shape), dt, kind="ExternalOutput").ap()
        return nc.dram_tensor(name, list(shape), dt).ap()

    xin = din("xin", [NT, D])
    sgla = din("sgla", [4, 128, 256])
    sconv = din("sconv", [30, 1024])
    cache_k = din("cache_k", [NPOOL * 128, D])
    cache_v = din("cache_v", [NPOOL * 128, D])
    ptab = din("ptab", [1, NPAGES], I32)
    norm_mix = din("norm_mix", [2, D])
    norm_ffn = din("norm_ffn", [2, D])
    norm_final = din("norm_final", [1, D])
    w_in = din("w_in", [D, 5136])
    w_lr = din("w_lr", [16, 512])
    b_lr = din("b_lr", [1, 512])
    gla_norm = din("gla_norm", [1, 256])
    conv_w = din("conv_w", [31, 1024])
    conv_vec = din("conv_vec", [24, 128])
    w_out_e = din("w_out_e", [D, D])
    w_qkv = din("w_qkv", [D, 3 * D])
    w_out_o = din("w_out_o", [D, D])
    sb_bias = din("sb_bias", [1, 16])
    peer_wq = din("peer_wq", [2, D, D])
    peer_keys = din("peer_keys", [2, 16, 128, 128])
    peer_u = din("peer_u", [2, NEXP, D])
    peer_v = din("peer_v", [2, NEXP, D])
    y_out = dout("y_out", [NT, D])
    gla_p = dout("gla_p", [4, 128, 256])
    gla_s = dout("gla_s", [4, 128, 256])
    conv_p = dout("conv_p", [30, 1024])
    conv_s = dout("conv_s", [30, 1024])
    k_all = dout("k_all", [NT, D])
    v_all = dout("v_all", [NT, D])
    PT = dscr("PT", [NT, 3088])
    UT = dscr("UT", [1024, 30 + NP])
    UTS = dscr("UTS", [1024, 30 + NS])
    OA = dscr("OA", [NT, 1024])
    X1 = dscr("X1", [NT, D])
    X2 = dscr("X2", [NT, D])
    X3 = dscr("X3", [NT, D])
    X4 = dscr("X4", [NT, D])
    HF = dscr("HF", [NT, D])
    QP = dscr("QP", [NT, D])
    QT = dscr("QT", [D, NT])
    KT = dscr("KT", [D, NT])

    global LASTP
    UVB = nc.dram_tensor("UVB16", [2 * NEXP, 2 * D], BF16).ap()

    P = Prog(nc)
    LASTP = P
    DB = {}

    def db(name):
        if name not in DB:
            DB[name] = Buf(name, multi=True)
        return DB[name]

    G = Phase(P, "g")
    G.__enter__()
    ident_f, b_ident_f = G.sb("ident_f", [128, 128])
    ident_b, b_ident_b = G.sb("ident_b", [128, 128], BF16)
    ones_f, b_ones_f = G.sb("ones_f", [128, 128])
    b_const = [b_ident_f, b_ident_b, b_ones_f]
    P.op("pool", lambda e: e.memset(ident_f[:], 0.0), writes=[b_ident_f])
    P.op("pool", lambda e: e.affine_select(out=ident_f[:], in_=ident_f[:], pattern=[[-1, 128]], compare_op=ALU.not_equal,
                                           fill=1.0, base=0, channel_multiplier=1), reads=[b_ident_f], writes=[b_ident_f])
    P.op("pool", lambda e: e.tensor_copy(out=ident_b[:], in_=ident_f[:]), reads=[b_ident_f], writes=[b_ident_b])
    P.op("pool", lambda e: e.memset(ones_f[:], 1.0), writes=[b_ones_f])
    eps_t, b_eps = G.sb("eps_t", [128, 1])
    P.op("pool", lambda e: e.memset(eps_t[:], EPS), writes=[b_eps])

    NBG = NEXP // 128
    bg_ctx = {"f": None, "b": None, "i": 0}

    def bg_alloc(ph, nbuf=4):
        bg_ctx["f"] = [ph.sb("bgf%d" % i, [128, D]) for i in range(nbuf)]
        bg_ctx["b"] = [ph.sb("bgb%d" % i, [128, D], BF16) for i in range(nbuf)]

    def bg_gen():
        for L in range(2):
            for tab, src in ((0, peer_u), (1, peer_v)):
                for t in range(NBG):
                    i = bg_ctx["i"]
                    bg_ctx["i"] += 1
                    f_t, bf = bg_ctx["f"][i % len(bg_ctx["f"])]
                    b_t, bb = bg_ctx["b"][i % len(bg_ctx["b"])]
                    P.dma("pool", lambda e, L=L, t=t, src=src, f_t=f_t: e.dma_start(out=f_t[:], in_=src[L, t * 128:(t + 1) * 128, :]), writes=[bf])
                    if i % 2:
                        P.op("dve", lambda e, f_t=f_t, b_t=b_t: e.tensor_copy(out=b_t[:], in_=f_t[:]), reads=[bf], writes=[bb])
                    else:
                        P.op("act", lambda e, f_t=f_t, b_t=b_t: e.activation(out=b_t[:], in_=f_t[:], func=AF.Copy), reads=[bf], writes=[bb])
                    P.dma("pool", lambda e, L=L, t=t, tab=tab, b_t=b_t: e.dma_start(out=UVB[L * NEXP + t * 128:L * NEXP + (t + 1) * 128, tab * D:(tab + 1) * D], in_=b_t[:]),
                          reads=[bb], writes=[db("UVB%d" % L)])
                    yield L
    bg_state = {"it": bg_gen(), "done": [0, 0]}

    def bg_step(k=1):
        if bg_ctx["f"] is None:
            return
        for _ in range(k):
            try:
                L = next(bg_state["it"])
                bg_state["done"][L] += 1
            except StopIteration:
                return

    def bg_flush(L):
        while bg_state["done"][L] < 2 * NBG:
            bg_step()

    class BG:
        def __init__(self, ph, nbuf=4, flush=None):
            self.ph, self.nbuf, self.flush = ph, nbuf, flush

        def __enter__(self):
            bg_alloc(self.ph, self.nbuf)

        def __exit__(self, *a):
            if self.flush is not None:
                bg_flush(self.flush)
            bg_ctx["f"] = None
            bg_ctx["b"] = None

    def gain_tile(ph, name, src_row):
        t, b = ph.sb(name, [128, D])
        P.dma("sp", lambda e: e.dma_start(out=t[:], in_=src_row.to_broadcast([128, D])), writes=[b])
        return t, b

    def rmsnorm_tile(ph, xt, bx, n, gain, bgain, out_t, bout, scr):
        junk, bjunk, ss, bss = scr
        P.op("dve", lambda e: e.memset(ss[:n], 0.0), writes=[bss])
        P.op("act", lambda e: e.activation(out=junk[:n], in_=xt[:n], func=AF.Square, accum_out=ss[:n, 0:1]),
             reads=[bx, bss], writes=[bjunk, bss])
        P.op("act", lambda e: e.activation(out=ss[:n, 1:2], in_=ss[:n, 0:1], func=AF.Sqrt, bias=eps_t[:n, 0:1], scale=1.0 / D),
             reads=[bss, b_eps], writes=[bss])
        P.op("dve", lambda e: e.reciprocal(out=ss[:n, 2:3], in_=ss[:n, 1:2]), reads=[bss], writes=[bss])
        P.op("dve", lambda e: e.scalar_tensor_tensor(out=out_t[:n], in0=xt[:n], scalar=ss[:n, 2:3], in1=gain[:n],
                                                     op0=ALU.mult, op1=ALU.mult), reads=[bx, bss, bgain], writes=[bout])

    def transpose_to_fm(src, bsrc, n, nk, dst, bdst, k0, c0, pst, bpst, dt_is_bf=True, evac="act"):
        idt = ident_b if dt_is_bf else ident_f
        bid = b_ident_b if dt_is_bf else b_ident_f
        for g0 in range(0, nk, 4):
            gn = min(4, nk - g0)
            for j in range(gn):
                k = g0 + j
                P.op("pe", lambda e, k=k, j=j: e.transpose(out=pst[:, j, :n], in_=src[:n, k * 128:(k + 1) * 128], identity=idt[:n, :n]),
                     reads=[bsrc, bid], writes=[bpst])
            if evac == "act":
                P.op("act", lambda e, g0=g0, gn=gn: e.activation(out=dst[:, k0 + g0:k0 + g0 + gn, c0:c0 + n], in_=pst[:, 0:gn, :n], func=AF.Copy),
                     reads=[bpst], writes=[bdst])
            else:
                P.op("dve", lambda e, g0=g0, gn=gn: e.tensor_copy(out=dst[:, k0 + g0:k0 + g0 + gn, c0:c0 + n], in_=pst[:, 0:gn, :n]),
                     reads=[bpst], writes=[bdst])

    def norm_phase(name, xsrc, bxsrc, gain_row, AT, bAT, hf_dst=None, bhf=None):
        with Phase(P, name) as ph:
            gain, bgain = gain_tile(ph, "gain", gain_row)
            xt = [ph.sb("x%d" % i, [128, D]) for i in range(2)]
            hb = [ph.sb("hb%d" % i, [128, D], BF16) for i in range(2)]
            hf = [ph.sb("hf%d" % i, [128, D]) for i in range(2)] if hf_dst is not None else None
            junk, bjunk = ph.sb("junk", [128, D])
            ss, bss = ph.sb("ss", [128, 4])
            pst = [ph.ps("pst%d" % i, [128, 4, 128], BF16) for i in range(2)]
            for ti, (r0, n) in enumerate(TILES):
                x_t, bx = xt[ti % 2]
                h_t, bh = hb[ti % 2]
                P.dma("sp", lambda e, r0=r0, n=n, x_t=x_t: e.dma_start(out=x_t[:n], in_=xsrc[r0:r0 + n, :]), reads=[bxsrc], writes=[bx])
                if hf_dst is not None:
                    f_t, bf = hf[ti % 2]
                    rmsnorm_tile(ph, x_t, bx, n, gain, bgain, f_t, bf, (junk, bjunk, ss, bss))
                    P.dma("sp", lambda e, r0=r0, n=n, f_t=f_t: e.dma_start(out=hf_dst[r0:r0 + n, :], in_=f_t[:n]), reads=[bf], writes=[bhf])
                    P.op("pool", lambda e, n=n, f_t=f_t, h_t=h_t: e.tensor_copy(out=h_t[:n], in_=f_t[:n]), reads=[bf], writes=[bh])
                else:
                    rmsnorm_tile(ph, x_t, bx, n, gain, bgain, h_t, bh, (junk, bjunk, ss, bss))
                p_t, bp = pst[ti % 2]
                transpose_to_fm(h_t, bh, n, KC, AT, bAT, 0, r0, p_t, bp, evac=("act" if ti % 2 else "dve"))

    def proj_tok(ph, AT, bAT, nk, W, col0, ncols, sink, CB=256, wscale=None):
        wf = [ph.sb("wf%d" % i, [128, nk, CB]) for i in range(2)]
        wb = [ph.sb("wb%d" % i, [128, nk, CB], BF16) for i in range(2)]
        pp = [ph.ps("pp%d" % i, [128, 512]) for i in range(4)]
        cnt = 0
        for bi, c0 in enumerate(range(col0, col0 + ncols, CB)):
            cn = min(CB, col0 + ncols - c0)
            wf_t, bwf = wf[bi % 2]
            wb_t, bwb = wb[bi % 2]
            P.dma("sp", lambda e, c0=c0, cn=cn, wf_t=wf_t: e.dma_start(out=wf_t[:, :, :cn], in_=W[:, c0:c0 + cn].rearrange("(k p) c -> p k c", p=128)),
                  writes=[bwf])
            hk = nk // 2
            P.op("dve", lambda e, cn=cn, wf_t=wf_t, wb_t=wb_t: e.tensor_copy(out=wb_t[:, :hk, :cn], in_=wf_t[:, :hk, :cn]), reads=[bwf], writes=[bwb])
            P.op("act", lambda e, cn=cn, wf_t=wf_t, wb_t=wb_t: e.activation(out=wb_t[:, hk:, :cn], in_=wf_t[:, hk:, :cn], func=AF.Copy), reads=[bwf], writes=[bwb])
            for ti, (r0, n) in enumerate(TILES):
                p_t, bp = pp[cnt % 4]
                cnt += 1
                for k in range(nk):
                    P.op("pe", lambda e, k=k, r0=r0, n=n, cn=cn, p_t=p_t, wb_t=wb_t: e.matmul(p_t[:n, :cn], lhsT=AT[:, k, r0:r0 + n], rhs=wb_t[:, k, :cn],
                                                                                         start=(k == 0), stop=(k == nk - 1)),
                         reads=[bAT, bwb], writes=[bp])
                sink(ti, r0, n, c0 - col0, cn, p_t, bp)
                bg_step()

    def proj_ch(ph, AT, bAT, nk, W, cols, sink, tag=""):
        ng = len(cols[0])
        wf = [ph.sb("cwf%s%d" % (tag, i), [128, nk, ng * 128]) for i in range(2)]
        wb = [ph.sb("cwb%s%d" % (tag, i), [128, nk, ng * 128], BF16) for i in range(2)]
        pp = [ph.ps("cpp%s%d" % (tag, i), [128, 512]) for i in range(2 * ng)]
        cnt = 0
        for gi, grp in enumerate(cols):
            wf_t, bwf = wf[gi % 2]
            wb_t, bwb = wb[gi % 2]
            for j, c0 in enumerate(grp):
                P.dma("sp", lambda e, c0=c0, j=j, wf_t=wf_t: e.dma_start(out=wf_t[:, :, j * 128:(j + 1) * 128],
                                                                       in_=W[:, c0:c0 + 128].rearrange("(k p) c -> p k c", p=128)), writes=[bwf])
            hk = nk // 2
            P.op("dve", lambda e, wf_t=wf_t, wb_t=wb_t: e.tensor_copy(out=wb_t[:, :hk, :], in_=wf_t[:, :hk, :]), reads=[bwf], writes=[bwb])
            P.op("act", lambda e, wf_t=wf_t, wb_t=wb_t: e.activation(out=wb_t[:, hk:, :], in_=wf_t[:, hk:, :], func=AF.Copy), reads=[bwf], writes=[bwb])
            for bi, (t0, nt) in enumerate(BLOCKS):
                pts = []
                for j in range(ng):
                    p_t, bp = pp[(cnt % 2) * ng + j]
                    for k in range(nk):
                        P.op("pe", lambda e, k=k, j=j, t0=t0, nt=nt, p_t=p_t, wb_t=wb_t: e.matmul(p_t[:, :nt], lhsT=wb_t[:, k, j * 128:(j + 1) * 128],
                                                                                             rhs=AT[:, k, t0:t0 + nt], start=(k == 0), stop=(k == nk - 1)),
                             reads=[bAT, bwb], writes=[bp])
                    pts.append((p_t, bp))
                cnt += 1
                sink(gi, bi, t0, nt, pts)
                bg_step()

    def store_sink(ph, dst, bdst, colbase, scale=None):
        stg = [ph.sb("stg%d" % i, [128, 256]) for i in range(4)]
        state = {"i": 0}

        def sink(ti, r0, n, c0, cn, p_t, bp):
            s_t, bs = stg[state["i"] % 4]
            use_act = state["i"] % 2 == 0
            state["i"] += 1
            if use_act:
                if scale is None:
                    P.op("act", lambda e: e.activation(out=s_t[:n, :cn], in_=p_t[:n, :cn], func=AF.Copy), reads=[bp], writes=[bs])
                else:
                    P.op("act", lambda e: e.activation(out=s_t[:n, :cn], in_=p_t[:n, :cn], func=AF.Copy, scale=scale), reads=[bp], writes=[bs])
            else:
                if scale is None:
                    P.op("dve", lambda e: e.tensor_copy(out=s_t[:n, :cn], in_=p_t[:n, :cn]), reads=[bp], writes=[bs])
                else:
                    P.op("dve", lambda e: e.tensor_scalar(out=s_t[:n, :cn], in0=p_t[:n, :cn], scalar1=scale, scalar2=None, op0=ALU.mult), reads=[bp], writes=[bs])
            P.dma("sp", lambda e: e.dma_start(out=dst[r0:r0 + n, colbase + c0:colbase + c0 + cn], in_=s_t[:n, :cn]), reads=[bs], writes=[bdst])
        return sink

    def resid_sink(ph, xsrc, bxsrc, dst, bdst):
        stg = [ph.sb("rs%d" % i, [128, 256]) for i in range(4)]
        xin_t = [ph.sb("rx%d" % i, [128, 256]) for i in range(4)]
        state = {"i": 0}

        def sink(ti, r0, n, c0, cn, p_t, bp):
            s_t, bs = stg[state["i"] % 4]
            x_t, bx = xin_t[state["i"] % 4]
            state["i"] += 1
            P.dma("sp", lambda e: e.dma_start(out=x_t[:n, :cn], in_=xsrc[r0:r0 + n, c0:c0 + cn]), reads=[bxsrc], writes=[bx])
            P.op("dve", lambda e: e.tensor_tensor(out=s_t[:n, :cn], in0=p_t[:n, :cn], in1=x_t[:n, :cn], op=ALU.add), reads=[bp, bx], writes=[bs])
            P.dma("sp", lambda e: e.dma_start(out=dst[r0:r0 + n, c0:c0 + cn], in_=s_t[:n, :cn]), reads=[bs], writes=[bdst])
        return sink

    def layer0_inproj():
        with Phase(P, "A") as ph:
            AT, bAT = ph.sb("AT", [128, KC, NT], BF16)
            norm_phase("A0", xin, db("xin"), norm_mix[0:1, :], AT, bAT)
            with Phase(P, "A1") as p1, BG(p1):
                proj_tok(p1, AT, bAT, KC, w_in, 0, 3088, store_sink(p1, PT, db("PT"), 0))
            with Phase(P, "A2") as p2, BG(p2):
                zt, bz = p2.sb("zt", [128, 30])
                P.op("dve", lambda e: e.memset(zt[:], 0.0), writes=[bz])
                for j in range(8):
                    P.dma("sp", lambda e, j=j: e.dma_start(out=UT[j * 128:(j + 1) * 128, 0:30], in_=zt[:]), reads=[bz], writes=[db("UT")])
                sc, bsc = p2.sb("sc", [30, 1024])
                P.dma("sp", lambda e: e.dma_start(out=sc[:], in_=sconv[:, :]), writes=[bsc])
                pst, bpst = p2.ps("pst", [128, 8, 32])
                sct, bsct = p2.sb("sct", [128, 8, 30])
                for j in range(8):
                    P.op("pe", lambda e, j=j: e.transpose(out=pst[:, j, :30], in_=sc[:30, j * 128:(j + 1) * 128], identity=ident_f[:30, :30]),
                         reads=[bsc, b_ident_f], writes=[bpst])
                P.op("dve", lambda e: e.tensor_copy(out=sct[:], in_=pst[:, :, :30]), reads=[bpst], writes=[bsct])
                P.dma("sp", lambda e: e.dma_start(out=UTS.rearrange("(j p) t -> p j t", p=128)[:, :, 0:30], in_=sct[:]), reads=[bsct], writes=[db("UTS")])
                sg = [p2.sb("sg%d" % i, [128, 512]) for i in range(2)]
                us = [p2.sb("us%d" % i, [128, 512]) for i in range(2)]
                state = {"i": 0}

                def sink(gi, bi, t0, nt, pts):
                    (pa, bpa), (pb, bpb) = pts
                    s_t, bs = sg[state["i"] % 2]
                    u_t, bu = us[state["i"] % 2]
                    state["i"] += 1
                    P.op("act", lambda e: e.activation(out=s_t[:, :nt], in_=pb[:, :nt], func=AF.Sigmoid), reads=[bpb], writes=[bs])
                    P.op("dve", lambda e: e.tensor_tensor(out=u_t[:, :nt], in0=pa[:, :nt], in1=s_t[:, :nt], op=ALU.mult), reads=[bpa, bs], writes=[bu])
                    if t0 < NP:
                        P.dma("sp", lambda e: e.dma_start(out=UT[gi * 128:(gi + 1) * 128, 30 + t0:30 + t0 + nt], in_=u_t[:, :nt]), reads=[bu], writes=[db("UT")])
                    else:
                        P.dma("sp", lambda e: e.dma_start(out=UTS[gi * 128:(gi + 1) * 128, 30:30 + nt], in_=u_t[:, :nt]), reads=[bu], writes=[db("UTS")])
                proj_ch(p2, AT, bAT, KC, w_in, [[3088 + 128 * j, 4112 + 128 * j] for j in range(8)], sink)

    def layer0_gla():
        with Phase(P, "B") as ph:
            C = 64
            SCALE = -1.0 / 16.0
            tri_s, btri = ph.sb("tri_s", [C, C])
            gtr_s, bgtr = ph.sb("gtr_s", [C, C])
            ones_s, bones = ph.sb("ones_s", [C, 8])
            tri01, btri01 = ph.sb("tri01", [C, C])
            P.op("pool", lambda e: e.memset(tri_s[:], SCALE), writes=[btri])
            P.op("pool", lambda e: e.affine_select(out=tri_s[:], in_=tri_s[:], pattern=[[1, C]], compare_op=ALU.is_ge, fill=0.0, base=0, channel_multiplier=-1),
                 reads=[btri], writes=[btri])
            P.op("pool", lambda e: e.memset(gtr_s[:], SCALE), writes=[bgtr])
            P.op("pool", lambda e: e.affine_select(out=gtr_s[:], in_=gtr_s[:], pattern=[[-1, C]], compare_op=ALU.is_ge, fill=0.0, base=-1, channel_multiplier=1),
                 reads=[bgtr], writes=[bgtr])
            P.op("pool", lambda e: e.memset(ones_s[:], SCALE), writes=[bones])
            P.op("pool", lambda e: e.memset(tri01[:], 1.0), writes=[btri01])
            P.op("pool", lambda e: e.affine_select(out=tri01[:], in_=tri01[:], pattern=[[1, C]], compare_op=ALU.is_ge, fill=0.0, base=0, channel_multiplier=-1),
                 reads=[btri01], writes=[btri01])
            wlr, bwlr = ph.sb("wlr", [32, 512])
            glag, bglag = ph.sb("glag", [C, 256])
            P.op("dve", lambda e: e.memset(wlr[:], 0.0), writes=[bwlr])
            P.dma("sp", lambda e: e.dma_start(out=wlr[0:16, :], in_=w_lr[:, :]), writes=[bwlr])
            P.dma("sp", lambda e: e.dma_start(out=wlr[16:17, :], in_=b_lr[:, :]), writes=[bwlr])
            P.dma("sp", lambda e: e.dma_start(out=glag[:], in_=gla_norm[0:1, :].to_broadcast([C, 256])), writes=[bglag])
            S, bS = ph.sb("S", [128, 4, 256])
            P.op("dve", lambda e: e.memset(S[:], 0.0), writes=[bS])
            qkv = [ph.sb("qkv%d" % i, [C, 2048]) for i in range(2)]
            gg = [ph.sb("gg%d" % i, [C, 1024]) for i in range(2)]
            lr = [ph.sb("lr%d" % i, [C, 32]) for i in range(2)]
            for l_t, bl in lr:
                P.op("dve", lambda e, l_t=l_t: e.memset(l_t[:], 0.0), writes=[bl])
                P.op("dve", lambda e, l_t=l_t: e.memset(l_t[:, 16:17], 1.0), writes=[bl])
            lrT, blrT = ph.sb("lrT", [32, C])
            lsp, blsp = ph.sb("lsp", [C, 512])
            EB, bEB = ph.sb("EB", [C, 512])
            ENB, bENB = ph.sb("ENB", [C, 512])
            ED, bED = ph.sb("ED", [C, 512])
            ebl, bebl = ph.sb("ebl", [128, 32])
            qtl, bqtl = ph.sb("qtl", [C, 512])
            ktl, bktl = ph.sb("ktl", [C, 512])
            kdc, bkdc = ph.sb("kdc", [C, 512])
            qT, bqT = ph.sb("qT", [128, 4, C])
            kT, bkT = ph.sb("kT", [128, 4, C])
            att, batt = ph.sb("att", [C, 4, C])
            osb, bosb = ph.sb("osb", [C, 4, 256])
            osq, bosq = ph.sb("osq", [C, 4, 256])
            rs, brs = ph.sb("rs", [C, 12])
            sgl, bsgl = ph.sb("sgl", [C, 1024])
            pA, bpA = ph.ps("pA", [128, 512])
            pB, bpB = ph.ps("pB", [128, 512])
            pC, bpC = ph.ps("pC", [128, 512])
            pD, bpD = ph.ps("pD", [128, 512])
            pO, bpO = ph.ps("pO", [128, 1024])
            pK, bpK = ph.ps("pK", [128, 1024])
            chunks = [(r, min(C, NP - r)) for r in range(0, NP, C)] + [(NP, NS)]
            bPT = db("PT")
            for (t_, b_) in qkv + [(lsp, blsp), (att, batt), (kdc, bkdc)]:
                P.op("pool", lambda e, t_=t_: e.memset(t_[:], 0.0), writes=[b_])
            for ci, (r0, n) in enumerate(chunks):
                q_t, bq = qkv[ci % 2]
                g_t, bg = gg[ci % 2]
                l_t, bl = lr[ci % 2]
                if n < C:
                    for (t_, b_) in [(lsp, blsp), (att, batt), (kdc, bkdc)]:
                        P.op("pool", lambda e, t_=t_: e.memset(t_[:], 0.0), writes=[b_])
                if r0 == NP:
                    P.dma("sp", lambda e: e.dma_start(out=gla_p.rearrange("h d v -> d h v"), in_=S[:]), reads=[bS], writes=[db("gla_p")])
                    P.dma("sp", lambda e: e.dma_start(out=S[:], in_=sgla.rearrange("h d v -> d h v")), writes=[bS])
                P.dma("sp", lambda e, r0=r0, n=n, q_t=q_t: e.dma_start(out=q_t[:n], in_=PT[r0:r0 + n, 0:2048]), reads=[bPT], writes=[bq])
                P.dma("sp", lambda e, r0=r0, n=n, g_t=g_t: e.dma_start(out=g_t[:n], in_=PT[r0:r0 + n, 2048:3072]), reads=[bPT], writes=[bg])
                P.dma("sp", lambda e, r0=r0, n=n, l_t=l_t: e.dma_start(out=l_t[:n, 0:16], in_=PT[r0:r0 + n, 3072:3088]), reads=[bPT], writes=[bl])
                P.op("pe", lambda e, n=n, l_t=l_t: e.transpose(out=pD[:32, :n], in_=l_t[:n, :32], identity=ident_f[:n, :n]), reads=[bl, b_ident_f], writes=[bpD])
                P.op("dve", lambda e, n=n: e.tensor_copy(out=lrT[:, :n], in_=pD[:32, :n]), reads=[bpD], writes=[blrT])
                P.op("pe", lambda e, n=n: e.matmul(pA[:n, :512], lhsT=lrT[:32, :n], rhs=wlr[:32, :], start=True, stop=True), reads=[blrT, bwlr], writes=[bpA])
                if GLA_STAGE < 2:
                    continue
                P.op("act", lambda e, n=n: e.activation(out=lsp[:n], in_=pA[:n, :512], func=AF.Exp, scale=-1.0), reads=[bpA], writes=[blsp])
                P.op("act", lambda e, n=n: e.activation(out=lsp[:n], in_=lsp[:n], func=AF.Ln, bias=1.0, scale=1.0), reads=[blsp], writes=[blsp])
                P.op("pe", lambda e, n=n: e.matmul(pB[:n, :512], lhsT=tri_s[:, :n], rhs=lsp[:, :], start=True, stop=True), reads=[btri, blsp], writes=[bpB])
                P.op("pe", lambda e, n=n: e.matmul(pC[:n, :512], lhsT=gtr_s[:, :n], rhs=lsp[:, :], start=True, stop=True), reads=[bgtr, blsp], writes=[bpC])
                for h in range(4):
                    P.op("pe", lambda e, n=n, h=h: e.matmul(pD[:, 64 + 8 * h:72 + 8 * h], lhsT=lsp[:, h * 128:(h + 1) * 128], rhs=ones_s[:, 0:8], start=True, stop=True),
                         reads=[blsp, bones], writes=[bpD])
                P.op("act", lambda e, n=n: e.activation(out=EB[:n], in_=pB[:n, :512], func=AF.Exp), reads=[bpB], writes=[bEB])
                P.op("act", lambda e, n=n: e.activation(out=ENB[:n], in_=pB[:n, :512], func=AF.Exp, scale=-1.0), reads=[bpB], writes=[bENB])
                P.op("act", lambda e, n=n: e.activation(out=ED[:n], in_=pC[:n, :512], func=AF.Exp), reads=[bpC], writes=[bED])
                P.op("act", lambda e: e.activation(out=ebl[:, :], in_=pD[:, 64:96], func=AF.Exp), reads=[bpD], writes=[bebl])
                P.op("dve", lambda e, n=n, q_t=q_t: e.scalar_tensor_tensor(out=qtl[:n], in0=q_t[:n, 0:512], scalar=128.0 ** -0.5, in1=EB[:n], op0=ALU.mult, op1=ALU.mult),
                     reads=[bq, bEB], writes=[bqtl])
                P.op("dve", lambda e, n=n, q_t=q_t: e.tensor_tensor(out=ktl[:n], in0=q_t[:n, 512:1024], in1=ENB[:n], op=ALU.mult), reads=[bq, bENB], writes=[bktl])
                P.op("dve", lambda e, n=n, q_t=q_t: e.tensor_tensor(out=kdc[:n], in0=q_t[:n, 512:1024], in1=ED[:n], op=ALU.mult), reads=[bq, bED], writes=[bkdc])
                if GLA_STAGE < 3:
                    continue
                for h in range(4):
                    P.op("pe", lambda e, n=n, h=h: e.transpose(out=pA[:, h * C:h * C + n], in_=qtl[:n, h * 128:(h + 1) * 128], identity=ident_f[:n, :n]),
                         reads=[bqtl, b_ident_f], writes=[bpA])
                for h in range(4):
                    P.op("pe", lambda e, n=n, h=h: e.transpose(out=pA[:, 256 + h * C:256 + h * C + n], in_=ktl[:n, h * 128:(h + 1) * 128], identity=ident_f[:n, :n]),
                         reads=[bktl, b_ident_f], writes=[bpA])
                if GLA_STAGE == 3 and GLA_SUB < 1:
                    continue
                P.op("dve", lambda e, n=n: e.tensor_copy(out=qT[:, :, :n], in_=pA[:, 0:256].rearrange("p (h c) -> p h c", h=4)[:, :, :n]), reads=[bpA], writes=[bqT])
                P.op("act", lambda e, n=n: e.activation(out=kT[:, :, :n], in_=pA[:, 256:512].rearrange("p (h c) -> p h c", h=4)[:, :, :n], func=AF.Copy), reads=[bpA], writes=[bkT])
                if GLA_STAGE == 3 and GLA_SUB < 2:
                    continue
                for h in range(4):
                    P.op("pe", lambda e, n=n, h=h: e.matmul(pB[:n, h * C:h * C + n], lhsT=kT[:, h, :n], rhs=qT[:, h, :n], start=True, stop=True),
                         reads=[bkT, bqT], writes=[bpB])
                if GLA_STAGE == 3 and GLA_SUB < 3:
                    continue
                P.op("dve", lambda e, n=n: e.tensor_tensor(out=att[:n, :, :n], in0=pB[:n, 0:256].rearrange("p (h c) -> p h c", h=4)[:, :, :n],
                                                           in1=tri01[:n, :n].unsqueeze(1).to_broadcast([n, 4, n]), op=ALU.mult), reads=[bpB, btri01], writes=[batt])
                if GLA_STAGE < 4:
                    continue
                for h in range(4):
                    P.op("pe", lambda e, n=n, h=h, q_t=q_t: e.matmul(pO[:n, h * 256:(h + 1) * 256], lhsT=att[:, h, :n], rhs=q_t[:, 1024 + h * 256:1024 + (h + 1) * 256],
                                                                  start=True, stop=False), reads=[batt, bq], writes=[bpO])
                    P.op("pe", lambda e, n=n, h=h: e.matmul(pO[:n, h * 256:(h + 1) * 256], lhsT=qT[:, h, :n], rhs=S[:, h, :], start=False, stop=True),
                         reads=[bqT, bS], writes=[bpO])
                for h in range(4):
                    P.op("pe", lambda e, n=n, h=h, q_t=q_t: e.matmul(pK[:, h * 256:(h + 1) * 256], lhsT=kdc[:, h * 128:(h + 1) * 128],
                                                                  rhs=q_t[:, 1024 + h * 256:1024 + (h + 1) * 256], start=True, stop=True), reads=[bkdc, bq], writes=[bpK])
                for h in range(4):
                    P.op("dve", lambda e, h=h: e.scalar_tensor_tensor(out=S[:, h, :], in0=S[:, h, :], scalar=ebl[:, 8 * h:8 * h + 1], in1=pK[:, h * 256:(h + 1) * 256],
                                                                      op0=ALU.mult, op1=ALU.add), reads=[bS, bebl, bpK], writes=[bS])
                if GLA_STAGE < 5:
                    continue
                P.op("act", lambda e, n=n: e.activation(out=osb[:n], in_=pO[:n, :].rearrange("p (h v) -> p h v", h=4), func=AF.Copy), reads=[bpO], writes=[bosb])
                P.op("dve", lambda e, n=n: e.tensor_tensor(out=osq[:n], in0=osb[:n], in1=osb[:n], op=ALU.mult), reads=[bosb], writes=[bosq])
                P.op("dve", lambda e, n=n: e.tensor_reduce(out=rs[:n, 0:4], in_=osq[:n], axis=AX.X, op=ALU.add), reads=[bosq], writes=[brs])
                P.op("act", lambda e, n=n: e.activation(out=rs[:n, 4:8], in_=rs[:n, 0:4], func=AF.Sqrt, bias=eps_t[:n, 0:1], scale=1.0 / 256), reads=[brs, b_eps], writes=[brs])
                P.op("dve", lambda e, n=n: e.reciprocal(out=rs[:n, 8:12], in_=rs[:n, 4:8]), reads=[brs], writes=[brs])
                P.op("act", lambda e, n=n, g_t=g_t: e.activation(out=sgl[:n], in_=g_t[:n], func=AF.Silu), reads=[bg], writes=[bsgl])
                P.op("dve", lambda e, n=n: e.tensor_tensor(out=osb[:n], in0=osb[:n], in1=rs[:n, 8:12].unsqueeze(2).to_broadcast([n, 4, 256]), op=ALU.mult),
                     reads=[bosb, brs], writes=[bosb])
                P.op("dve", lambda e, n=n: e.tensor_tensor(out=osb[:n], in0=osb[:n], in1=glag[:n, :].unsqueeze(1).to_broadcast([n, 4, 256]), op=ALU.mult),
                     reads=[bosb, bglag], writes=[bosb])
                P.op("dve", lambda e, n=n: e.tensor_tensor(out=osq[:n], in0=osb[:n], in1=sgl[:n].rearrange("p (h v) -> p h v", h=4), op=ALU.mult),
                     reads=[bosb, bsgl], writes=[bosq])
                P.dma("sp", lambda e, r0=r0, n=n: e.dma_start(out=OA[r0:r0 + n, :], in_=osq[:n].rearrange("p h v -> p (h v)")), reads=[bosq], writes=[db("OA")])
            P.dma("sp", lambda e: e.dma_start(out=gla_s.rearrange("h d v -> d h v"), in_=S[:]), reads=[bS], writes=[db("gla_s")])


    def layer0_conv_out():
        with Phase(P, "CD") as ph:
            OT, bOT = ph.sb("OT", [128, KC, NT], BF16)
            with Phase(P, "C") as pc:
                Cc, bC = pc.sb("Cc", [128, 8, NT])
                cwr, bcwr = pc.sb("cwr", [31, 1024])
                cwT, bcwT = pc.sb("cwT", [128, 8, 32])
                cvr, bcvr = pc.sb("cvr", [24, 128])
                cv, bcv = pc.sb("cv", [128, 24])
                pst, bpst = pc.ps("pst", [128, 512])
                P.dma("sp", lambda e: e.dma_start(out=cwr[:], in_=conv_w[:, :]), writes=[bcwr])
                P.dma("sp", lambda e: e.dma_start(out=cvr[:], in_=conv_vec[:, :]), writes=[bcvr])
                for j in range(8):
                    P.op("pe", lambda e, j=j: e.transpose(out=pst[:, j * 32:j * 32 + 31], in_=cwr[:31, j * 128:(j + 1) * 128], identity=ident_f[:31, :31]),
                         reads=[bcwr, b_ident_f], writes=[bpst])
                P.op("dve", lambda e: e.tensor_copy(out=cwT[:, :, 0:31], in_=pst[:, 0:256].rearrange("p (j k) -> p j k", j=8)[:, :, 0:31]), reads=[bpst], writes=[bcwT])
                P.op("pe", lambda e: e.transpose(out=pst[:, 256:280], in_=cvr[:24, :], identity=ident_f[:24, :24]), reads=[bcvr, b_ident_f], writes=[bpst])
                P.op("dve", lambda e: e.tensor_copy(out=cv[:], in_=pst[:, 256:280]), reads=[bpst], writes=[bcv])
                ut = [pc.sb("ut%d" % i, [128, 30 + NP]) for i in range(2)]
                uts = [pc.sb("uts%d" % i, [128, 32 + NS]) for i in range(2)]
                cbuf, bcbuf = pc.sb("cbuf", [32, 1024])
                cbufs, bcbufs = pc.sb("cbufs", [32, 1024])
                pcb, bpcb = pc.ps("pcb", [128, 512])
                for j in range(8):
                    u_t, bu = ut[j % 2]
                    s_t, bs = uts[j % 2]
                    P.dma("sp", lambda e, j=j, u_t=u_t: e.dma_start(out=u_t[:], in_=UT[j * 128:(j + 1) * 128, :]), reads=[db("UT")], writes=[bu])
                    P.dma("sp", lambda e, j=j, s_t=s_t: e.dma_start(out=s_t[:, 0:30 + NS], in_=UTS[j * 128:(j + 1) * 128, :]), reads=[db("UTS")], writes=[bs])
                    for (src, bsrc, L, c0) in ((u_t, bu, NP, 0), (s_t, bs, NS, NP)):
                        P.op("dve", lambda e, j=j, src=src, L=L, c0=c0: e.tensor_scalar(out=Cc[:, j, c0:c0 + L], in0=src[:, 0:L], scalar1=cwT[:, j, 0:1], scalar2=cv[:, j:j + 1],
                                                                                 op0=ALU.mult, op1=ALU.add), reads=[bsrc, bcwT, bcv], writes=[bC])
                        for k in range(1, 31):
                            P.op("dve", lambda e, j=j, k=k, src=src, L=L, c0=c0: e.scalar_tensor_tensor(out=Cc[:, j, c0:c0 + L], in0=src[:, k:k + L], scalar=cwT[:, j, k:k + 1],
                                                                                                   in1=Cc[:, j, c0:c0 + L], op0=ALU.mult, op1=ALU.add),
                                 reads=[bsrc, bcwT, bC], writes=[bC])
                    P.op("pe", lambda e, j=j, u_t=u_t: e.transpose(out=pcb[:30, j * 128:(j + 1) * 128] if j < 4 else pcb[:30, (j - 4) * 128:(j - 3) * 128],
                                                               in_=u_t[:, NP:NP + 30], identity=ident_f[:, :]), reads=[bu, b_ident_f], writes=[bpcb])
                    P.op("act", lambda e, j=j: e.activation(out=cbuf[:30, j * 128:(j + 1) * 128], in_=pcb[:30, (j % 4) * 128:(j % 4 + 1) * 128], func=AF.Copy), reads=[bpcb], writes=[bcbuf])
                    P.op("pe", lambda e, j=j, s_t=s_t: e.transpose(out=pcb[:30, (j % 4) * 128:(j % 4 + 1) * 128], in_=s_t[:, NS:NS + 30], identity=ident_f[:, :]),
                         reads=[bs, b_ident_f], writes=[bpcb])
                    P.op("act", lambda e, j=j: e.activation(out=cbufs[:30, j * 128:(j + 1) * 128], in_=pcb[:30, (j % 4) * 128:(j % 4 + 1) * 128], func=AF.Copy), reads=[bpcb], writes=[bcbufs])
                P.dma("sp", lambda e: e.dma_start(out=conv_p[:, :], in_=cbuf[:30, :]), reads=[bcbuf], writes=[db("conv_p")])
                P.dma("sp", lambda e: e.dma_start(out=conv_s[:, :], in_=cbufs[:30, :]), reads=[bcbufs], writes=[db("conv_s")])
                sq, bsq = pc.sb("sq", [128, 512])
                mean, bmean = pc.sb("mean", [128, 512])
                msq, bmsq = pc.sb("msq", [128, 512])
                rstd, brstd = pc.sb("rstd", [128, 512])
                tmp, btmp = pc.sb("tmp", [128, 512])
                pm, bpm = pc.ps("pm", [128, 512])
                pq, bpq = pc.ps("pq", [128, 512])
                for (t0, nt) in BLOCKS:
                    for j in range(8):
                        P.op("pe", lambda e, j=j, t0=t0, nt=nt: e.matmul(pm[:, :nt], lhsT=ones_f[:, :], rhs=Cc[:, j, t0:t0 + nt], start=(j == 0), stop=(j == 7)),
                             reads=[b_ones_f, bC], writes=[bpm])
                    for j in range(8):
                        P.op("act", lambda e, j=j, t0=t0, nt=nt: e.activation(out=sq[:, :nt], in_=Cc[:, j, t0:t0 + nt], func=AF.Square), reads=[bC], writes=[bsq])
                        P.op("pe", lambda e, j=j, nt=nt: e.matmul(pq[:, :nt], lhsT=ones_f[:, :], rhs=sq[:, :nt], start=(j == 0), stop=(j == 7)), reads=[b_ones_f, bsq], writes=[bpq])
                    P.op("act", lambda e, nt=nt: e.activation(out=mean[:, :nt], in_=pm[:, :nt], func=AF.Copy, scale=1.0 / 1024), reads=[bpm], writes=[bmean])
                    P.op("dve", lambda e, nt=nt: e.tensor_tensor(out=msq[:, :nt], in0=mean[:, :nt], in1=mean[:, :nt], op=ALU.mult), reads=[bmean], writes=[bmsq])
                    P.op("dve", lambda e, nt=nt: e.scalar_tensor_tensor(out=rstd[:, :nt], in0=pq[:, :nt], scalar=1.0 / 1024, in1=msq[:, :nt], op0=ALU.mult, op1=ALU.subtract),
                         reads=[bpq, bmsq], writes=[brstd])
                    P.op("act", lambda e, nt=nt: e.activation(out=rstd[:, :nt], in_=rstd[:, :nt], func=AF.Sqrt, bias=eps_t[:, 0:1], scale=1.0), reads=[brstd, b_eps], writes=[brstd])
                    P.op("dve", lambda e, nt=nt: e.reciprocal(out=rstd[:, :nt], in_=rstd[:, :nt]), reads=[brstd], writes=[brstd])
                    for j in range(8):
                        P.op("dve", lambda e, j=j, t0=t0, nt=nt: e.tensor_tensor(out=tmp[:, :nt], in0=Cc[:, j, t0:t0 + nt], in1=mean[:, :nt], op=ALU.subtract), reads=[bC, bmean], writes=[btmp])
                        P.op("dve", lambda e, nt=nt: e.tensor_tensor(out=tmp[:, :nt], in0=tmp[:, :nt], in1=rstd[:, :nt], op=ALU.mult), reads=[btmp, brstd], writes=[btmp])
                        P.op("dve", lambda e, j=j, nt=nt: e.tensor_scalar(out=tmp[:, :nt], in0=tmp[:, :nt], scalar1=cv[:, 8 + j:9 + j], scalar2=cv[:, 16 + j:17 + j], op0=ALU.mult, op1=ALU.add),
                             reads=[btmp, bcv], writes=[btmp])
                        P.op("act", lambda e, j=j, t0=t0, nt=nt: e.activation(out=OT[:, 8 + j, t0:t0 + nt], in_=tmp[:, :nt], func=AF.Silu), reads=[btmp], writes=[bOT])
            with Phase(P, "D0") as pd:
                oa = [pd.sb("oa%d" % i, [128, 1024]) for i in range(2)]
                ob = [pd.sb("ob%d" % i, [128, 1024], BF16) for i in range(2)]
                pstd = [pd.ps("pst%d" % i, [128, 4, 128], BF16) for i in range(2)]
                for ti, (r0, n) in enumerate(TILES):
                    a_t, ba = oa[ti % 2]
                    b_t, bb = ob[ti % 2]
                    P.dma("sp", lambda e, r0=r0, n=n, a_t=a_t: e.dma_start(out=a_t[:n], in_=OA[r0:r0 + n, :]), reads=[db("OA")], writes=[ba])
                    P.op("pool", lambda e, n=n, a_t=a_t, b_t=b_t: e.tensor_copy(out=b_t[:n], in_=a_t[:n]), reads=[ba], writes=[bb])
                    p_t, bp = pstd[ti % 2]
                    transpose_to_fm(b_t, bb, n, 8, OT, bOT, 0, r0, p_t, bp, evac=("act" if ti % 2 else "dve"))
            with Phase(P, "D1") as pd, BG(pd):
                proj_tok(pd, OT, bOT, KC, w_out_e, 0, D, resid_sink(pd, xin, db("xin"), X1, db("X1")))


    def peer_layer(L, Xsrc, bXsrc, Xdst, bXdst, tag):
        with Phase(P, "E" + tag) as ph:
            AT, bAT = ph.sb("AT", [128, KC, NT], BF16)
            norm_phase("E0" + tag, Xsrc, bXsrc, norm_ffn[L:L + 1, :], AT, bAT, hf_dst=HF, bhf=db("HF"))
            with Phase(P, "E1" + tag) as p1, BG(p1, flush=L):
                proj_tok(p1, AT, bAT, KC, peer_wq[L], 0, D, store_sink(p1, QP, db("QP"), 0))
        with Phase(P, "F" + tag) as ph:
            NBUF = 4
            kraw, bkraw = ph.sb("kraw", [128, 16, 128])
            keysT, bkeysT = ph.sb("keysT", [128, 16, 128])
            iota16, biota = ph.sb("iota16", [128, 16])
            P.op("pool", lambda e: e.iota(iota16[:], pattern=[[1, 16]], base=0, channel_multiplier=0, allow_small_or_imprecise_dtypes=True), writes=[biota])
            P.dma("sp", lambda e: e.dma_start(out=kraw[:], in_=peer_keys[L].rearrange("g n d -> n g d")), writes=[bkraw])
            pt4 = [ph.ps("pt4_%d" % i, [128, 4, 128]) for i in range(2)]
            for g4 in range(4):
                p_t, bp = pt4[g4 % 2]
                for j in range(4):
                    P.op("pe", lambda e, g4=g4, j=j, p_t=p_t: e.transpose(out=p_t[:, j, :], in_=kraw[:, g4 * 4 + j, :], identity=ident_f[:, :]), reads=[bkraw, b_ident_f], writes=[bp])
                P.op("dve", lambda e, g4=g4, p_t=p_t: e.tensor_copy(out=keysT[:, g4 * 4:g4 * 4 + 4, :], in_=p_t[:]), reads=[bp], writes=[bkeysT])
            qt_, bqt = ph.sb("qt", [128, 2048])
            qT, bqT = ph.sb("qT", [128, 16, 128])
            ssb, bssb = ph.sb("ssb", [128, 16, 128])
            s2, bs2 = ph.sb("s2", [128, 16, 128])
            sv, bsv = ph.sb("sv", [128, 16, 16])
            si, bsi = ph.sb("si", [128, 16, 16], U32)
            sif, bsif = ph.sb("sif", [128, 16, 16])
            cand, bcand = ph.sb("cand", [128, 8, 256])
            cand2, bcand2 = s2[:].rearrange("p (h c) k -> p h (c k)", c=2), bs2
            oh, boh = ph.sb("oh", [128, 8, 256])
            best, bbest = ph.sb("best", [128, 8, 16])
            pos, bpos = ph.sb("pos", [128, 8, 16], U32)
            pa_, bpa_ = ph.sb("pa", [128, 8, 16], U32)
            pb_, bpb_ = ph.sb("pb", [128, 8, 16], U32)
            paf, bpaf = ph.sb("paf", [128, 8, 16])
            pbf, bpbf = ph.sb("pbf", [128, 8, 16])
            isel, bisel = ph.sb("isel", [128, 8, 16])
            jsel, bjsel = ph.sb("jsel", [128, 8, 16])
            eidx, beidx = ph.sb("eidx", [128, 128], I32)
            gsum, bgsum = ph.sb("gsum", [128, 16])
            gate, bgate = ph.sb("gate", [128, 8, 16])
            apre, bapre = ph.sb("apre", [128, 128])
            wact, bwact = ph.sb("wact", [128, 128])
            hf_t, bhf_t = ph.sb("hf", [128, 2048])
            xt_, bxt_ = ph.sb("xt", [128, 2048])
            NUV = 10
            uv = [ph.sb("uv%d" % i, [128, 2 * D], BF16) for i in range(NUV)]
            hb2 = [ph.sb("hb%d" % i, [128, 2048], BF16) for i in range(2)]
            gate2 = [(gate, bgate), ph.sb("gateB", [128, 8, 16])]
            junkb, bjunkb = ph.sb("junkb", [128, 2048], BF16)
            dgs = [ph.sb("dg%d" % i, [128, 4, 128], BF16) for i in range(4)]
            wg, bwg = ph.sb("wg", [128, 128])
            psc = [ph.ps("psc%d" % i, [128, 4, 128]) for i in range(2)]
            py = [ph.ps("py%d" % i, [128, 512]) for i in range(4)]
            bg_flush(L)
            eidxB, beidxB = ph.sb("eidxB", [128, 128], I32)
            xtB, bxtB = ph.sb("xtB", [128, 2048])
            eidx2 = [(eidx, beidx), (eidxB, beidxB)]
            xt2 = [(xt_, bxt_), (xtB, bxtB)]
            NTL = len(TILES)

            def S1(ti):
                r0, n = TILES[ti]
                ei, bei = eidx2[ti % 2]
                x_t, bx = xt2[ti % 2]
                gt, bgt = gate2[ti % 2]
                hbt, bhbt = hb2[ti % 2]
                P.dma("sp", lambda e, r0=r0, n=n: e.dma_start(out=qt_[:n], in_=QP[r0:r0 + n, :]), reads=[db("QP")], writes=[bqt])
                P.dma("sp", lambda e, r0=r0, n=n: e.dma_start(out=hf_t[:n], in_=HF[r0:r0 + n, :]), reads=[db("HF")], writes=[bhf_t])
                P.dma("sp", lambda e, r0=r0, n=n, x_t=x_t: e.dma_start(out=x_t[:n], in_=Xsrc[r0:r0 + n, :]), reads=[bXsrc], writes=[bx])
                for g4 in range(4):
                    p_t, bp = pt4[g4 % 2]
                    for j in range(4):
                        P.op("pe", lambda e, g4=g4, j=j, p_t=p_t, n=n: e.transpose(out=p_t[:, j, :n], in_=qt_[:n, (g4 * 4 + j) * 128:(g4 * 4 + j + 1) * 128], identity=ident_f[:n, :n]),
                             reads=[bqt, b_ident_f], writes=[bp])
                    P.op("act", lambda e, g4=g4, p_t=p_t, n=n: e.activation(out=qT[:, g4 * 4:g4 * 4 + 4, :n], in_=p_t[:, :, :n], func=AF.Copy), reads=[bp], writes=[bqT])
                for g4 in range(4):
                    p_t, bp = psc[g4 % 2]
                    for j in range(4):
                        P.op("pe", lambda e, g4=g4, j=j, p_t=p_t, n=n: e.matmul(p_t[:n, j, :], lhsT=qT[:, g4 * 4 + j, :n], rhs=keysT[:, g4 * 4 + j, :], start=True, stop=True),
                             reads=[bqT, bkeysT], writes=[bp])
                    P.op("dve", lambda e, g4=g4, p_t=p_t, n=n: e.tensor_copy(out=ssb[:n, g4 * 4:g4 * 4 + 4, :], in_=p_t[:n]), reads=[bp], writes=[bssb])
                for g in range(16):
                    P.op("dve", lambda e, g=g, n=n: e.max(out=sv[:n, g, 0:8], in_=ssb[:n, g, :]), reads=[bssb], writes=[bsv])
                    P.op("dve", lambda e, g=g, n=n: e.max_index(out=si[:n, g, 0:8], in_max=sv[:n, g, 0:8], in_values=ssb[:n, g, :]), reads=[bssb, bsv], writes=[bsi])
                    P.op("dve", lambda e, g=g, n=n: e.match_replace(out=s2[:n, g, :], in_to_replace=sv[:n, g, 0:8], in_values=ssb[:n, g, :], imm_value=-1e30), reads=[bssb, bsv], writes=[bs2])
                    P.op("dve", lambda e, g=g, n=n: e.max(out=sv[:n, g, 8:16], in_=s2[:n, g, :]), reads=[bs2], writes=[bsv])
                    P.op("dve", lambda e, g=g, n=n: e.max_index(out=si[:n, g, 8:16], in_max=sv[:n, g, 8:16], in_values=s2[:n, g, :]), reads=[bs2, bsv], writes=[bsi])
                P.op("dve", lambda e, n=n: e.tensor_copy(out=sif[:n], in_=si[:n]), reads=[bsi], writes=[bsif])
                sv4 = sv[:].rearrange("p (h c) k -> p h c k", c=2)
                sif4 = sif[:].rearrange("p (h c) k -> p h c k", c=2)
                cand4 = cand[:].rearrange("p h (a b) -> p h a b", a=16)
                oh4 = oh[:].rearrange("p h (a b) -> p h a b", a=16)
                P.op("dve", lambda e, n=n: e.tensor_tensor(out=cand4[:n], in0=sv4[:n, :, 0, :].unsqueeze(3).to_broadcast([n, 8, 16, 16]),
                                                           in1=sv4[:n, :, 1, :].unsqueeze(2).to_broadcast([n, 8, 16, 16]), op=ALU.add), reads=[bsv], writes=[bcand])
                for h in range(8):
                    P.op("dve", lambda e, h=h, n=n: e.max(out=best[:n, h, 0:8], in_=cand[:n, h, :]), reads=[bcand], writes=[bbest])
                    P.op("dve", lambda e, h=h, n=n: e.max_index(out=pos[:n, h, 0:8], in_max=best[:n, h, 0:8], in_values=cand[:n, h, :]), reads=[bcand, bbest], writes=[bpos])
                    P.op("dve", lambda e, h=h, n=n: e.match_replace(out=cand2[:n, h, :], in_to_replace=best[:n, h, 0:8], in_values=cand[:n, h, :], imm_value=-1e30), reads=[bcand, bbest], writes=[bcand2])
                    P.op("dve", lambda e, h=h, n=n: e.max(out=best[:n, h, 8:16], in_=cand2[:n, h, :]), reads=[bcand2], writes=[bbest])
                    P.op("dve", lambda e, h=h, n=n: e.max_index(out=pos[:n, h, 8:16], in_max=best[:n, h, 8:16], in_values=cand2[:n, h, :]), reads=[bcand2, bbest], writes=[bpos])
                P.op("dve", lambda e, n=n: e.tensor_single_scalar(out=pa_[:n], in_=pos[:n], scalar=4, op=ALU.arith_shift_right), reads=[bpos], writes=[bpa_])
                P.op("dve", lambda e, n=n: e.tensor_single_scalar(out=pb_[:n], in_=pos[:n], scalar=15, op=ALU.bitwise_and), reads=[bpos], writes=[bpb_])
                P.op("dve", lambda e, n=n: e.tensor_copy(out=paf[:n], in_=pa_[:n]), reads=[bpa_], writes=[bpaf])
                P.op("dve", lambda e, n=n: e.tensor_copy(out=pbf[:n], in_=pb_[:n]), reads=[bpb_], writes=[bpbf])
                io4 = iota16[:n, :].unsqueeze(1).unsqueeze(1).to_broadcast([n, 8, 16, 16])
                for (pf, bpf, c, dst, bdst) in ((paf, bpaf, 0, isel, bisel), (pbf, bpbf, 1, jsel, bjsel)):
                    P.op("dve", lambda e, n=n, pf=pf, io4=io4: e.tensor_tensor(out=oh4[:n], in0=pf[:n].unsqueeze(3).to_broadcast([n, 8, 16, 16]), in1=io4, op=ALU.is_equal),
                         reads=[bpf, biota], writes=[boh])
                    P.op("dve", lambda e, n=n, c=c: e.tensor_tensor(out=oh4[:n], in0=oh4[:n], in1=sif4[:n, :, c, :].unsqueeze(2).to_broadcast([n, 8, 16, 16]), op=ALU.mult),
                         reads=[boh, bsif], writes=[boh])
                    P.op("dve", lambda e, n=n, dst=dst: e.tensor_reduce(out=dst[:n], in_=oh4[:n], axis=AX.X, op=ALU.add), reads=[boh], writes=[bdst])
                P.op("dve", lambda e, n=n: e.scalar_tensor_tensor(out=isel[:n], in0=isel[:n], scalar=128.0, in1=jsel[:n], op0=ALU.mult, op1=ALU.add), reads=[bisel, bjsel], writes=[bisel])
                if L > 0:
                    P.op("dve", lambda e, n=n: e.tensor_scalar(out=isel[:n], in0=isel[:n], scalar1=float(L * NEXP), scalar2=None, op0=ALU.add), reads=[bisel], writes=[bisel])
                P.op("dve", lambda e, n=n, ei=ei: e.tensor_copy(out=ei[:n], in_=isel[:n].rearrange("p h k -> p (h k)")), reads=[bisel], writes=[bei])
                P.op("dve", lambda e, n=n, gt=gt: e.tensor_tensor(out=gt[:n], in0=best[:n], in1=best[:n, :, 0:1].to_broadcast([n, 8, 16]), op=ALU.subtract), reads=[bbest], writes=[bgt])
                P.op("act", lambda e, n=n, gt=gt: e.activation(out=gt[:n], in_=gt[:n], func=AF.Exp), reads=[bgt], writes=[bgt])
                P.op("dve", lambda e, n=n, gt=gt: e.tensor_reduce(out=gsum[:n, 0:8], in_=gt[:n], axis=AX.X, op=ALU.add), reads=[bgt], writes=[bgsum])
                P.op("dve", lambda e, n=n: e.reciprocal(out=gsum[:n, 8:16], in_=gsum[:n, 0:8]), reads=[bgsum], writes=[bgsum])
                P.op("dve", lambda e, n=n, gt=gt: e.tensor_tensor(out=gt[:n], in0=gt[:n], in1=gsum[:n, 8:16].unsqueeze(2).to_broadcast([n, 8, 16]), op=ALU.mult), reads=[bgt, bgsum], writes=[bgt])
                P.op("pool", lambda e, n=n, hbt=hbt: e.tensor_copy(out=hbt[:n], in_=hf_t[:n]), reads=[bhf_t], writes=[bhbt])

            def tile_body(ti):
                r0, n = TILES[ti]
                ei, bei = eidx2[ti % 2]
                x_t, bx = xt2[ti % 2]
                gt, bgt = gate2[ti % 2]
                hbt, bhbt = hb2[ti % 2]
                gflat = gt[:].rearrange("p h k -> p (h k)")
                P.op("dve", lambda e, n=n: e.memset(apre[:n], 0.0), writes=[bapre])
                for g in range(32):
                    d_t, bd = dgs[g % 4]
                    tiles_ = []
                    for j in range(4):
                        sl = g * 4 + j
                        u_t, bu = uv[sl % NUV]
                        tiles_.append((u_t, bu))
                        P.dma("pool", lambda e, sl=sl, n=n, u_t=u_t, ei=ei: e.indirect_dma_start(out=u_t[:n], out_offset=None, in_=UVB, in_offset=bass.IndirectOffsetOnAxis(ap=ei[:n, sl:sl + 1], axis=0)),
                              reads=[bei, db("UVB%d" % L)], writes=[bu])
                        P.op("dve", lambda e, sl=sl, n=n, u_t=u_t, hbt=hbt: e.scalar_tensor_tensor(out=junkb[:n], in0=u_t[:n, 0:D], scalar=1.0, in1=hbt[:n], op0=ALU.mult, op1=ALU.mult,
                                                                                              accum_out=apre[:n, sl:sl + 1]), reads=[bu, bhbt, bapre], writes=[bjunkb, bapre])
                    P.op("act", lambda e, g=g, n=n: e.activation(out=wg[:n, g * 4:g * 4 + 4], in_=apre[:n, g * 4:g * 4 + 4], func=AF.Gelu_apprx_tanh), reads=[bapre], writes=[bwg])
                    P.op("dve", lambda e, g=g, n=n, gflat=gflat: e.tensor_tensor(out=wg[:n, g * 4:g * 4 + 4], in0=wg[:n, g * 4:g * 4 + 4], in1=gflat[:n, g * 4:g * 4 + 4], op=ALU.mult),
                         reads=[bwg, bgt], writes=[bwg])
                    P.op("dve", lambda e, g=g, n=n, d_t=d_t: e.tensor_tensor(out=d_t[:n], in0=ident_b[:n, :].unsqueeze(1).to_broadcast([n, 4, 128]),
                                                                             in1=wg[:n, g * 4:g * 4 + 4].unsqueeze(2).to_broadcast([n, 4, 128]), op=ALU.mult),
                         reads=[b_ident_b, bwg], writes=[bd])
                    for j in range(4):
                        sl = g * 4 + j
                        u_t, bu = tiles_[j]
                        for c in range(4):
                            P.op("pe", lambda e, sl=sl, j=j, n=n, c=c, u_t=u_t, d_t=d_t: e.matmul(py[c][0][:n, :], lhsT=d_t[:n, j, :n], rhs=u_t[:n, D + c * 512:D + (c + 1) * 512],
                                                                                             start=(sl == 0), stop=(sl == 127)), reads=[bd, bu], writes=[py[c][1]])
                for c in range(4):
                    P.op("dve", lambda e, n=n, c=c, x_t=x_t: e.tensor_tensor(out=x_t[:n, c * 512:(c + 1) * 512], in0=x_t[:n, c * 512:(c + 1) * 512], in1=py[c][0][:n, :], op=ALU.add),
                         reads=[bx, py[c][1]], writes=[bx])
                P.dma("sp", lambda e, r0=r0, n=n, x_t=x_t: e.dma_start(out=Xdst[r0:r0 + n, :], in_=x_t[:n]), reads=[bx], writes=[bXdst])

            S1(0)
            for ti in range(NTL):
                if ti + 1 < NTL:
                    S1(ti + 1)
                tile_body(ti)


    def layer1_sample(ph, OT, bOT):
        with Phase(P, "Lc") as pc:
            biasb, bbias = pc.sb("biasb", [128, 16])
            P.dma("sp", lambda e: e.dma_start(out=biasb[:], in_=sb_bias[0:1, :].to_broadcast([128, 16])), writes=[bbias])
            biasr, bbiasr = pc.sb("biasr", [128, 16, 8])
            P.op("dve", lambda e: e.tensor_copy(out=biasr[:], in_=biasb[:].unsqueeze(2).to_broadcast([128, 16, 8])), reads=[bbias], writes=[bbiasr])
            ustr, bustr = pc.sb("ustr", [128, 128])
            P.op("pool", lambda e: e.memset(ustr[:], 1.0), writes=[bustr])
            P.op("pool", lambda e: e.affine_select(out=ustr[:], in_=ustr[:], pattern=[[-1, 128]], compare_op=ALU.is_ge, fill=0.0, base=-1, channel_multiplier=1),
                 reads=[bustr], writes=[bustr])
            mnew, bmnew = pc.sb("mnew", [128, 16, 8])
            P.op("pool", lambda e: e.memset(mnew[:], 1.0), writes=[bmnew])
            P.op("pool", lambda e: e.affine_select(out=mnew[:], in_=mnew[:], pattern=[[0, 16], [1, 8]], compare_op=ALU.is_ge, fill=0.0, base=-1, channel_multiplier=-1),
                 reads=[bmnew], writes=[bmnew])
            pti, bpti = pc.sb("pti", [128, NPAGES], I32)
            ptf, bptf = pc.sb("ptf", [128, NPAGES])
            pidx, bpidx = pc.sb("pidx", [128, NPAGES], I32)
            iop, biop = pc.sb("iop", [128, 1])
            P.dma("sp", lambda e: e.dma_start(out=pti[:], in_=ptab[0:1, :].to_broadcast([128, NPAGES])), writes=[bpti])
            P.op("pool", lambda e: e.iota(iop[:], pattern=[[0, 1]], base=0, channel_multiplier=1, allow_small_or_imprecise_dtypes=True), writes=[biop])
            P.op("dve", lambda e: e.tensor_copy(out=ptf[:], in_=pti[:]), reads=[bpti], writes=[bptf])
            P.op("dve", lambda e: e.tensor_scalar(out=ptf[:], in0=ptf[:], scalar1=128.0, scalar2=iop[:, 0:1], op0=ALU.mult, op1=ALU.add), reads=[bptf, biop], writes=[bptf])
            P.op("dve", lambda e: e.tensor_copy(out=pidx[:], in_=ptf[:]), reads=[bptf], writes=[bpidx])
            qs, bqs = pc.sb("qs", [128, 16, 8])
            ks, bks = pc.sb("ks", [128, 16, 8])
            qsb, bqsb = pc.sb("qsb", [128, 16, 8], BF16)
            ksb, bksb = pc.sb("ksb", [128, 16, 8], BF16)
            P.dma("sp", lambda e: e.dma_start(out=qs[:], in_=QT[:, NP:NP + 8].rearrange("(h p) t -> p h t", p=128)), reads=[db("QT")], writes=[bqs])
            P.dma("sp", lambda e: e.dma_start(out=ks[:], in_=KT[:, NP:NP + 8].rearrange("(h p) t -> p h t", p=128)), reads=[db("KT")], writes=[bks])
            P.op("dve", lambda e: e.tensor_copy(out=qsb[:], in_=qs[:]), reads=[bqs], writes=[bqsb])
            P.op("dve", lambda e: e.tensor_copy(out=ksb[:], in_=ks[:]), reads=[bks], writes=[bksb])
            kp = [pc.sb("kp%d" % i, [128, 2048]) for i in range(2)]
            vp = [pc.sb("vp%d" % i, [128, 2048]) for i in range(2)]
            vpb = [pc.sb("vpb%d" % i, [128, 2048], BF16) for i in range(2)]
            kTb, bkTb = pc.sb("kTb", [128, 16, 128], BF16)
            zb, bzb = pc.sb("zb", [128, 128])
            spt, bspt = pc.sb("spt", [128, 128])
            dt_, bdt = pc.sb("dt", [128, 128])
            Abt, bAbt = pc.sb("Abt", [128, 128], BF16)
            spsum, bsps = pc.sb("spsum", [128, 128])
            P.op("pool", lambda e: e.memset(spsum[:], 0.0), writes=[bsps])
            P.op("pool", lambda e: e.memset(spt[:], 0.0), writes=[bspt])
            P.op("pool", lambda e: e.memset(Abt[:], 0.0), writes=[bAbt])
            ptr = [pc.ps("ptr%d" % i, [128, 4, 128]) for i in range(2)]
            pz, bpz = pc.ps("pz", [128, 512])
            pT, bpT = pc.ps("pT", [128, 512])
            po, bpo = pc.ps("po", [128, 512])
            zbf, bzbf = pc.sb("zbf", [128, 128], BF16)
            P.op("pool", lambda e: e.memset(zbf[:], 0.0), writes=[bzbf])
            P.op("pe", lambda e: e.matmul(po[:, 0:128], lhsT=zbf[:, :], rhs=zbf[:, :], start=True, stop=False), reads=[bzbf], writes=[bpo])
            v_t, bv = vp[0]
            vb_t, bvb = vpb[0]
            P.op("pool", lambda e, v_t=v_t: e.memset(v_t[:], 0.0), writes=[bv])
            P.dma("sp", lambda e, v_t=v_t: e.dma_start(out=v_t[:8, :], in_=v_all[NP:NP + 8, :]), reads=[db("v_all")], writes=[bv])
            P.op("pool", lambda e, v_t=v_t, vb_t=vb_t: e.tensor_copy(out=vb_t[:], in_=v_t[:]), reads=[bv], writes=[bvb])
            blocks = [("new", 8)] + [(p, 128) for p in range(NPAGES - 1, -1, -1)]
            for bi, (pg, nk) in enumerate(blocks):
                first = bi == 0
                last = bi == len(blocks) - 1
                if pg != "new":
                    k_t, bk = kp[bi % 2]
                    v_t, bv = vp[bi % 2]
                    vb_t, bvb = vpb[bi % 2]
                    P.dma("pool", lambda e, pg=pg, k_t=k_t: e.indirect_dma_start(out=k_t[:], out_offset=None, in_=cache_k, in_offset=bass.IndirectOffsetOnAxis(ap=pidx[:, pg:pg + 1], axis=0)),
                          reads=[bpidx], writes=[bk])
                    P.dma("pool", lambda e, pg=pg, v_t=v_t: e.indirect_dma_start(out=v_t[:], out_offset=None, in_=cache_v, in_offset=bass.IndirectOffsetOnAxis(ap=pidx[:, pg:pg + 1], axis=0)),
                          reads=[bpidx], writes=[bv])
                    P.op("act", lambda e, v_t=v_t, vb_t=vb_t: e.activation(out=vb_t[:], in_=v_t[:], func=AF.Copy), reads=[bv], writes=[bvb])
                    for g4 in range(4):
                        p_t, bp = ptr[g4 % 2]
                        for j in range(4):
                            P.op("pe", lambda e, g4=g4, j=j, p_t=p_t, k_t=k_t: e.transpose(out=p_t[:, j, :], in_=k_t[:, (g4 * 4 + j) * 128:(g4 * 4 + j + 1) * 128], identity=ident_f[:, :]),
                                 reads=[bk, b_ident_f], writes=[bp])
                        P.op("dve", lambda e, g4=g4, p_t=p_t: e.tensor_copy(out=kTb[:, g4 * 4:g4 * 4 + 4, :], in_=p_t[:]), reads=[bp], writes=[bkTb])
                    for h in range(16):
                        P.op("pe", lambda e, h=h: e.matmul(pz[:, h * 8:(h + 1) * 8], lhsT=kTb[:, h, :], rhs=qsb[:, h, :], start=True, stop=True), reads=[bkTb, bqsb], writes=[bpz])
                else:
                    for h in range(16):
                        P.op("pe", lambda e, h=h: e.matmul(pz[:8, h * 8:(h + 1) * 8], lhsT=ksb[:, h, :], rhs=qsb[:, h, :], start=True, stop=True), reads=[bksb, bqsb], writes=[bpz])
                P.op("dve", lambda e, nk=nk: e.tensor_tensor(out=zb[:nk], in0=pz[:nk, 0:128], in1=biasr[:nk].rearrange("p h q -> p (h q)"), op=ALU.add), reads=[bpz, bbiasr], writes=[bzb])
                P.op("act", lambda e, nk=nk: e.activation(out=spt[:nk], in_=zb[:nk], func=AF.Exp), reads=[bzb], writes=[bspt])
                P.op("act", lambda e, nk=nk: e.activation(out=spt[:nk], in_=spt[:nk], func=AF.Ln, bias=1.0, scale=1.0), reads=[bspt], writes=[bspt])
                if first:
                    P.op("dve", lambda e, nk=nk: e.tensor_tensor(out=spt[:nk], in0=spt[:nk], in1=mnew[:nk].rearrange("p h q -> p (h q)"), op=ALU.mult), reads=[bspt, bmnew], writes=[bspt])
                P.op("pe", lambda e, nk=nk, first=first: e.matmul(pT[:nk, 0:128], lhsT=ustr[:, :nk], rhs=spt[:, :], start=True, stop=first), reads=[bustr, bspt], writes=[bpT])
                if not first:
                    P.op("pe", lambda e, nk=nk: e.matmul(pT[:nk, 0:128], lhsT=ones_f[:, :nk], rhs=spsum[:, :], start=False, stop=True), reads=[b_ones_f, bsps], writes=[bpT])
                P.op("dve", lambda e, nk=nk: e.tensor_tensor(out=dt_[:nk], in0=zb[:nk], in1=spt[:nk], op=ALU.subtract), reads=[bzb, bspt], writes=[bdt])
                P.op("dve", lambda e, nk=nk: e.tensor_tensor(out=dt_[:nk], in0=dt_[:nk], in1=pT[:nk, 0:128], op=ALU.subtract), reads=[bdt, bpT], writes=[bdt])
                if first:
                    P.op("act", lambda e, nk=nk: e.activation(out=dt_[:nk], in_=dt_[:nk], func=AF.Exp), reads=[bdt], writes=[bdt])
                    P.op("dve", lambda e, nk=nk: e.tensor_tensor(out=Abt[:nk], in0=dt_[:nk], in1=mnew[:nk].rearrange("p h q -> p (h q)"), op=ALU.mult), reads=[bdt, bmnew], writes=[bAbt])
                else:
                    P.op("act", lambda e, nk=nk: e.activation(out=Abt[:nk], in_=dt_[:nk], func=AF.Exp), reads=[bdt], writes=[bAbt])
                if not last:
                    P.op("pool", lambda e: e.tensor_tensor(out=spsum[:], in0=spsum[:], in1=spt[:], op=ALU.add), reads=[bsps, bspt], writes=[bsps])
                for h in range(16):
                    P.op("pe", lambda e, h=h, vb_t=vb_t, first=first, last=last: e.matmul(po[:, h * 8:(h + 1) * 8], lhsT=vb_t[:, h * 128:(h + 1) * 128], rhs=Abt[:, h * 8:(h + 1) * 8],
                                                                                       start=False, stop=(last and h == 15)), reads=[bvb, bAbt], writes=[bpo])
            P.op("act", lambda e: e.activation(out=OT[:, :, NP:NP + 8], in_=po[:, 0:128].rearrange("p (h q) -> p h q", h=16), func=AF.Copy), reads=[bpo], writes=[bOT])

    def layer1(Xsrc, bXsrc, Xdst, bXdst):
        NPT = (NP + 127) // 128
        PBLOCKS = [b for b in BLOCKS if b[0] < NP]
        with Phase(P, "L") as ph:
            with Phase(P, "La") as pa:
                AT, bAT = pa.sb("AT", [128, KC, NT], BF16)
                norm_phase("La0", Xsrc, bXsrc, norm_mix[1:2, :], AT, bAT)
                with Phase(P, "La1") as p1, BG(p1):
                    proj_tok(p1, AT, bAT, KC, w_qkv, 2048, 2048, store_sink(p1, k_all, db("k_all"), 0))
                with Phase(P, "La2") as p1, BG(p1):
                    proj_tok(p1, AT, bAT, KC, w_qkv, 4096, 2048, store_sink(p1, v_all, db("v_all"), 0))
                with Phase(P, "La3") as p1, BG(p1):
                    stg = [p1.sb("cs%d" % i, [128, 512]) for i in range(4)]
                    st_ = {"i": 0}

                    def mk_sink(dst, bdst, scale):
                        def sink(gi, bi, t0, nt, pts):
                            for j, (p_t, bp) in enumerate(pts):
                                s_t, bs = stg[st_["i"] % 4]
                                st_["i"] += 1
                                P.op("act", lambda e, s_t=s_t, p_t=p_t, nt=nt: e.activation(out=s_t[:, :nt], in_=p_t[:, :nt], func=AF.Copy, scale=scale), reads=[bp], writes=[bs])
                                P.dma("sp", lambda e, s_t=s_t, gi=gi, j=j, t0=t0, nt=nt: e.dma_start(out=dst[(gi * 2 + j) * 128:(gi * 2 + j + 1) * 128, t0:t0 + nt], in_=s_t[:, :nt]),
                                      reads=[bs], writes=[bdst])
                        return sink
                    proj_ch(p1, AT, bAT, KC, w_qkv, [[c, c + 128] for c in range(0, 2048, 256)], mk_sink(QT, db("QT"), 128.0 ** -0.5), tag="q")
                with Phase(P, "La4") as p1, BG(p1):
                    stg = [p1.sb("cs%d" % i, [128, 512]) for i in range(4)]
                    st_ = {"i": 0}
                    proj_ch(p1, AT, bAT, KC, w_qkv, [[c, c + 128] for c in range(2048, 4096, 256)], mk_sink(KT, db("KT"), 1.0), tag="k")
            OT, bOT = ph.sb("OT", [128, KC, NT], BF16)
            with Phase(P, "Lb") as pb, BG(pb, 3):
                biasb, bbias = pb.sb("biasb", [128, 16])
                P.dma("sp", lambda e: e.dma_start(out=biasb[:], in_=sb_bias[0:1, :].to_broadcast([128, 16])), writes=[bbias])
                ustr, bustr = pb.sb("ustr", [128, 128])
                P.op("pool", lambda e: e.memset(ustr[:], 1.0), writes=[bustr])
                P.op("pool", lambda e: e.affine_select(out=ustr[:], in_=ustr[:], pattern=[[-1, 128]], compare_op=ALU.is_ge, fill=0.0, base=-1, channel_multiplier=1),
                     reads=[bustr], writes=[bustr])
                masks = []
                for r in range(4):
                    m_t, bm = pb.sb("mask%d" % r, [128, 512])
                    P.op("pool", lambda e, m_t=m_t: e.memset(m_t[:], 1.0), writes=[bm])
                    P.op("pool", lambda e, m_t=m_t, r=r: e.affine_select(out=m_t[:], in_=m_t[:], pattern=[[1, 512]], compare_op=ALU.is_ge, fill=0.0, base=-128 * r - 1, channel_multiplier=-1),
                         reads=[bm], writes=[bm])
                    masks.append((m_t, bm))
                qf = [pb.sb("qf%d" % i, [128, NP]) for i in range(2)]
                kf = [pb.sb("kf%d" % i, [128, NP]) for i in range(2)]
                vf = [pb.sb("vf%d" % i, [128, NPT, 128]) for i in range(2)]
                qb, bqb = pb.sb("qb", [128, NP], BF16)
                kb_, bkb = pb.sb("kb", [128, NP], BF16)
                vbb, bvbb = pb.sb("vbb", [128, NPT, 128], BF16)
                for v_t, bv in vf:
                    P.op("pool", lambda e, v_t=v_t: e.memset(v_t[:], 0.0), writes=[bv])
                esb = [pb.sb("esb%d" % i, [128, 512]) for i in range(2)]
                t1 = [pb.sb("t1%d" % i, [128, 512]) for i in range(2)]
                Ab = [pb.sb("Ab%d" % i, [128, 512], BF16) for i in range(2)]
                for a_t, ba in Ab:
                    P.op("pool", lambda e, a_t=a_t: e.memset(a_t[:], 0.0), writes=[ba])
                spsum, bsps = pb.sb("spsum", [128, 512])
                pz = [pb.ps("pz%d" % i, [128, 512]) for i in range(2)]
                pT = [pb.ps("pT%d" % i, [128, 512]) for i in range(2)]
                po = [pb.ps("po%d" % i, [128, 512]) for i in range(2)]
                nfull = NP // 128
                rem = NP - nfull * 128
                cnt = 0
                och = 0
                for h in range(16):
                    q_t, bq = qf[h % 2]
                    k_t, bk = kf[h % 2]
                    v_t, bv = vf[h % 2]
                    P.dma("sp", lambda e, h=h, q_t=q_t: e.dma_start(out=q_t[:], in_=QT[h * 128:(h + 1) * 128, 0:NP]), reads=[db("QT")], writes=[bq])
                    P.dma("sp", lambda e, h=h, k_t=k_t: e.dma_start(out=k_t[:], in_=KT[h * 128:(h + 1) * 128, 0:NP]), reads=[db("KT")], writes=[bk])
                    P.dma("sp", lambda e, h=h, v_t=v_t: e.dma_start(out=v_t[:, 0:nfull, :], in_=v_all[0:nfull * 128, h * 128:(h + 1) * 128].rearrange("(t p) d -> p t d", p=128)),
                          reads=[db("v_all")], writes=[bv])
                    if rem:
                        P.dma("sp", lambda e, h=h, v_t=v_t: e.dma_start(out=v_t[:rem, nfull, :], in_=v_all[nfull * 128:NP, h * 128:(h + 1) * 128]), reads=[db("v_all")], writes=[bv])
                    P.op("pool", lambda e, q_t=q_t: e.tensor_copy(out=qb[:], in_=q_t[:]), reads=[bq], writes=[bqb])
                    P.op("pool", lambda e, k_t=k_t: e.tensor_copy(out=kb_[:], in_=k_t[:]), reads=[bk], writes=[bkb])
                    P.op("pool", lambda e, v_t=v_t: e.tensor_copy(out=vbb[:], in_=v_t[:]), reads=[bv], writes=[bvbb])
                    for (q0, nq) in PBLOCKS:
                        kb_last = (q0 + nq - 1) // 128
                        o_t, bo = po[och % 2]
                        och += 1
                        P.op("pool", lambda e: e.memset(spsum[:], 0.0), writes=[bsps])
                        for kb in range(kb_last, -1, -1):
                            nk = min(128, NP - kb * 128)
                            z_t, bz = pz[cnt % 2]
                            T_t, bT = pT[cnt % 2]
                            e_t, be = esb[cnt % 2]
                            d_t, bd = t1[cnt % 2]
                            a_t, ba = Ab[cnt % 2]
                            cnt += 1
                            first = kb == kb_last
                            diag = kb * 128 + nk > q0
                            r = kb - q0 // 128
                            P.op("pe", lambda e, kb=kb, nk=nk, q0=q0, nq=nq, z_t=z_t: e.matmul(z_t[:nk, :nq], lhsT=kb_[:, kb * 128:kb * 128 + nk], rhs=qb[:, q0:q0 + nq], start=True, stop=True),
                                 reads=[bkb, bqb], writes=[bz])
                            if nk < 128:
                                P.op("dve", lambda e, e_t=e_t: e.memset(e_t[:], 0.0), writes=[be])
                            P.op("act", lambda e, h=h, nk=nk, nq=nq, z_t=z_t, e_t=e_t: e.activation(out=e_t[:nk, :nq], in_=z_t[:nk, :nq], func=AF.Exp, bias=biasb[:nk, h:h + 1], scale=1.0),
                                 reads=[bz, bbias], writes=[be])
                            P.op("act", lambda e, nk=nk, nq=nq, e_t=e_t: e.activation(out=e_t[:nk, :nq], in_=e_t[:nk, :nq], func=AF.Ln, bias=1.0, scale=1.0), reads=[be], writes=[be])
                            if diag:
                                m_t, bm = masks[r]
                                P.op("dve", lambda e, nk=nk, nq=nq, e_t=e_t, m_t=m_t: e.tensor_tensor(out=e_t[:nk, :nq], in0=e_t[:nk, :nq], in1=m_t[:nk, :nq], op=ALU.mult),
                                     reads=[be, bm], writes=[be])
                            P.op("pe", lambda e, nk=nk, nq=nq, T_t=T_t, e_t=e_t, first=first: e.matmul(T_t[:nk, :nq], lhsT=ustr[:, :nk], rhs=e_t[:, :nq], start=True, stop=first),
                                 reads=[bustr, be], writes=[bT])
                            if not first:
                                P.op("pe", lambda e, nk=nk, nq=nq, T_t=T_t: e.matmul(T_t[:nk, :nq], lhsT=ones_f[:, :nk], rhs=spsum[:, :nq], start=False, stop=True),
                                     reads=[b_ones_f, bsps], writes=[bT])
                            P.op("dve", lambda e, h=h, nk=nk, nq=nq, z_t=z_t, e_t=e_t, d_t=d_t: e.scalar_tensor_tensor(out=d_t[:nk, :nq], in0=z_t[:nk, :nq], scalar=biasb[:nk, h:h + 1], in1=e_t[:nk, :nq],
                                                                                                              op0=ALU.add, op1=ALU.subtract), reads=[bz, bbias, be], writes=[bd])
                            P.op("dve", lambda e, nk=nk, nq=nq, T_t=T_t, d_t=d_t: e.tensor_tensor(out=d_t[:nk, :nq], in0=d_t[:nk, :nq], in1=T_t[:nk, :nq], op=ALU.subtract), reads=[bd, bT], writes=[bd])
                            if diag:
                                P.op("act", lambda e, nk=nk, nq=nq, d_t=d_t: e.activation(out=d_t[:nk, :nq], in_=d_t[:nk, :nq], func=AF.Exp), reads=[bd], writes=[bd])
                                P.op("dve", lambda e, nk=nk, nq=nq, d_t=d_t, a_t=a_t, m_t=m_t: e.tensor_tensor(out=a_t[:nk, :nq], in0=d_t[:nk, :nq], in1=m_t[:nk, :nq], op=ALU.mult),
                                     reads=[bd, bm], writes=[ba])
                            else:
                                P.op("act", lambda e, nk=nk, nq=nq, d_t=d_t, a_t=a_t: e.activation(out=a_t[:nk, :nq], in_=d_t[:nk, :nq], func=AF.Exp), reads=[bd], writes=[ba])
                            if kb > 0:
                                P.op("pool", lambda e, nq=nq, e_t=e_t: e.tensor_tensor(out=spsum[:, :nq], in0=spsum[:, :nq], in1=e_t[:, :nq], op=ALU.add), reads=[bsps, be], writes=[bsps])
                            P.op("pe", lambda e, kb=kb, nk=nk, nq=nq, o_t=o_t, a_t=a_t, first=first: e.matmul(o_t[:, :nq], lhsT=vbb[:nk, kb, :], rhs=a_t[:nk, :nq], start=first, stop=(kb == 0)),
                                 reads=[bvbb, ba], writes=[bo])
                            bg_step()
                        P.op("act", lambda e, h=h, q0=q0, nq=nq, o_t=o_t: e.activation(out=OT[:, h, q0:q0 + nq], in_=o_t[:, :nq], func=AF.Copy), reads=[bo], writes=[bOT])
            if stop_after != "Lb":
                layer1_sample(ph, OT, bOT)
            with Phase(P, "Ld") as pd, BG(pd):
                proj_tok(pd, OT, bOT, KC, w_out_o, 0, D, resid_sink(pd, Xsrc, bXsrc, Xdst, bXdst))

    layer0_inproj()
    if stop_after == "A":
        G.__exit__(None, None, None)
        P.emit()
        return nc
    layer0_gla()
    if stop_after == "B":
        G.__exit__(None, None, None)
        P.emit()
        return nc
    layer0_conv_out()
    if stop_after == "D":
        G.__exit__(None, None, None)
        P.emit()
        return nc
    peer_layer(0, X1, db("X1"), X2, db("X2"), "0")
    if stop_after == "E":
        G.__exit__(None, None, None)
        P.emit()
        return nc
    layer1(X2, db("X2"), X3, db("X3"))
    if stop_after in ("Lb", "L"):
        G.__exit__(None, None, None)
        P.emit()
        return nc
    peer_layer(1, X3, db("X3"), X4, db("X4"), "1")
    with Phase(P, "Z") as ph:
        gain, bgain = gain_tile(ph, "gain", norm_final[0:1, :])
        xz = [ph.sb("x%d" % i, [128, D]) for i in range(2)]
        yz = [ph.sb("y%d" % i, [128, D]) for i in range(2)]
        junk, bjunk = ph.sb("junk", [128, D])
        ss, bss = ph.sb("ss", [128, 4])
        for ti, (r0, n) in enumerate(TILES):
            x_t, bx = xz[ti % 2]
            y_t, by = yz[ti % 2]
            P.dma("sp", lambda e, r0=r0, n=n, x_t=x_t: e.dma_start(out=x_t[:n], in_=X4[r0:r0 + n, :]), reads=[db("X4")], writes=[bx])
            rmsnorm_tile(ph, x_t, bx, n, gain, bgain, y_t, by, (junk, bjunk, ss, bss))
            P.dma("sp", lambda e, r0=r0, n=n, y_t=y_t: e.dma_start(out=y_out[r0:r0 + n, :], in_=y_t[:n]), reads=[by], writes=[db("y_out")])

    G.__exit__(None, None, None)
    P.emit()
    return nc


def make_in_maps(inp, ncores=8, SEQ=2048):
    f32 = np.float32
    a = lambda v: np.ascontiguousarray(np.asarray(v))
    npool = inp["cache_k"].shape[1]
    shared = {
        "cache_k": a(inp["cache_k"][0]).reshape(npool * 128, D),
        "cache_v": a(inp["cache_v"][0]).reshape(npool * 128, D),
        "norm_mix": a(inp["norm_mix"]),
        "norm_ffn": a(inp["norm_ffn"]),
        "norm_final": a(inp["norm_final"]).reshape(1, D),
        "w_in": a(inp["w_in_even"][0]),
        "w_lr": a(inp["w_gate_lr"][0]),
        "b_lr": a(inp["b_gate_lr"][0]).reshape(1, 512),
        "gla_norm": a(inp["gla_norm"][0]).reshape(1, 256),
        "conv_w": a(inp["conv_w"][0]),
        "conv_vec": a(np.concatenate([np.asarray(inp["conv_b"][0]).reshape(8, 128), np.asarray(inp["conv_norm_g"][0]).reshape(8, 128),
                                      np.asarray(inp["conv_norm_b"][0]).reshape(8, 128)], 0)),
        "w_out_e": a(inp["w_out_even"][0]),
        "w_qkv": a(inp["w_qkv_odd"][0]),
        "w_out_o": a(inp["w_out_odd"][0]),
        "sb_bias": a(inp["sb_bias"][0]).reshape(1, 16),
        "peer_wq": a(inp["peer_wq"]),
        "peer_keys": a(inp["peer_keys"]).reshape(2, 16, 128, 128),
        "peer_u": a(inp["peer_u"]),
        "peer_v": a(inp["peer_v"]),
    }
    maps = []
    for c in range(ncores):
        b = c % 4
        m = dict(shared)
        m["xin"] = a(np.concatenate([np.asarray(inp["meta_tokens"]), np.asarray(inp["x_prompt"][b]), np.asarray(inp["x_sample"][c])], 0).astype(f32))
        m["sgla"] = a(inp["state_gla"][0, c])
        m["sconv"] = a(inp["state_conv"][0, c])
        m["ptab"] = a(inp["page_table"][c:c + 1]).astype(np.int32)
        maps.append(m)
    return maps


_CACHE = {}


def kernel(**inputs):
    inp = {k: np.asarray(v) for k, v in inputs.items()}
    SEQ = inp["x_prompt"].shape[1]
    NPAGES = inp["page_table"].shape[1]
    NPOOL = inp["cache_k"].shape[1]
    NB = inp["x_prompt"].shape[0]
    NSB = inp["x_sample"].shape[0]
    NP = N_META + SEQ
    nc = build(SEQ=SEQ, NPAGES=NPAGES, NPOOL=NPOOL)
    in_maps = make_in_maps(inp, ncores=8, SEQ=SEQ)
    res = run_bass_kernel_spmd(nc, in_maps, core_ids=list(range(8)))
    r = res.results
    f32 = np.float32
    y_prompt = np.stack([r[b]["y_out"][N_META:NP] for b in range(NB)]).astype(f32)
    y_sample = np.stack([r[c]["y_out"][NP:NP + 8] for c in range(NSB)]).astype(f32)
    gla_prompt = np.stack([r[b]["gla_p"] for b in range(NB)])[None].astype(f32)
    gla_sample = np.stack([r[c]["gla_s"] for c in range(NSB)])[None].astype(f32)
    conv_prompt = np.stack([r[b]["conv_p"] for b in range(NB)])[None].astype(f32)
    conv_sample = np.stack([r[c]["conv_s"] for c in range(NSB)])[None].astype(f32)
    k_prompt = np.stack([r[b]["k_all"][:NP].reshape(NP, 16, 128) for b in range(NB)])[None].astype(f32)
    v_prompt = np.stack([r[b]["v_all"][:NP].reshape(NP, 16, 128) for b in range(NB)])[None].astype(f32)
    k_sample = np.stack([r[c]["k_all"][NP:NP + 8].reshape(8, 16, 128) for c in range(NSB)])[None].astype(f32)
    v_sample = np.stack([r[c]["v_all"][NP:NP + 8].reshape(8, 16, 128) for c in range(NSB)])[None].astype(f32)
    return (y_prompt, y_sample, gla_prompt, gla_sample, conv_prompt, conv_sample, k_prompt, v_prompt, k_sample, v_sample)
```
